# Optimizing a Trainium2 kernel written in Bass

```python
import jax, jax.numpy as jnp
from jax import lax
import numpy as np

D_MODEL = 1024
BATCH = 8
SEQ = 4096
DEPTH = 2

BRANCH_W = 512
N_BRANCH = 3
D_FF = 2816
MACARON_W = 0.5
EPS = 1e-6
NEG_INF = -1e30

GM_GROUPS = 4
GM_CH = BRANCH_W // GM_GROUPS
GM_CHUNK = 128

NSA_HEADS = 8
NSA_KV_GROUPS = 2
NSA_HPG = NSA_HEADS // NSA_KV_GROUPS
HEAD_DIM = 64
ROT_DIM = HEAD_DIM // 4
ROPE_THETA = 500000.0
CMP_BLOCK = 32
CMP_STRIDE = 16
CMP_HIDDEN = 2 * HEAD_DIM
SLC_BLOCK = 64
SLC_TOPK = 16
WINDOW = 512
NSA_QCHUNK = 64
FORCE_SCORE = 1e4

CONV_W = 3

A_COLS = 2 * BRANCH_W
Q_COLS = NSA_HEADS * HEAD_DIM
KV_COLS = 6 * NSA_KV_GROUPS * HEAD_DIM
NG_COLS = 3 * NSA_HEADS
C_COLS = 3 * BRANCH_W
IN_COLS = A_COLS + Q_COLS + KV_COLS + NG_COLS + C_COLS
IN_SPLITS = (A_COLS, A_COLS + Q_COLS, A_COLS + Q_COLS + KV_COLS,
             A_COLS + Q_COLS + KV_COLS + NG_COLS)

kernel_name = "hybrid_gmlp_nsa_shortconv_macaron"


def rmsnorm(x, g):
    xf = x.astype(jnp.float32)
    y = xf * lax.rsqrt(jnp.mean(xf * xf, axis=-1, keepdims=True) + EPS) * g
    return y.astype(x.dtype)


def layernorm(x, g, b):
    xf = x.astype(jnp.float32)
    mu = jnp.mean(xf, axis=-1, keepdims=True)
    var = jnp.mean(jnp.square(xf - mu), axis=-1, keepdims=True)
    return ((xf - mu) * lax.rsqrt(var + EPS) * g + b).astype(x.dtype)


def rope_tables(seq):
    pos = jnp.arange(seq, dtype=jnp.float32)
    inv = 1.0 / (ROPE_THETA ** (jnp.arange(0, ROT_DIM, 2, dtype=jnp.float32) / ROT_DIM))
    ang = pos[:, None] * inv[None, :]
    return jnp.cos(ang), jnp.sin(ang)


def apply_partial_rope(x, cos, sin):
    half = ROT_DIM // 2
    x1, x2, xp = x[..., :half], x[..., half:ROT_DIM], x[..., ROT_DIM:]
    r1 = (x1 * cos - x2 * sin).astype(x.dtype)
    r2 = (x2 * cos + x1 * sin).astype(x.dtype)
    return jnp.concatenate([r1, r2, xp], axis=-1)


def swiglu(h, w13, w2):
    g, u = jnp.split(h @ w13, 2, axis=-1)
    return (jax.nn.silu(g) * u) @ w2


def masked_softmax(s, mask, scale):
    s = jnp.where(mask, s.astype(jnp.float32) * scale, NEG_INF)
    p = jax.nn.softmax(s, axis=-1)
    return jnp.where(mask, p, 0.0)


def gmlp_spatial_gating(uv, ln_g, ln_b, w_s, b_s):
    Bsz, S, _ = uv.shape
    u, v = jnp.split(jax.nn.gelu(uv), 2, axis=-1)
    v = layernorm(v, ln_g, ln_b)
    v = v.reshape(Bsz, S // GM_CHUNK, GM_CHUNK, GM_GROUPS, GM_CH)
    tril = np.tril(np.ones((GM_CHUNK, GM_CHUNK), dtype=bool))
    ws = jnp.where(tril[None], w_s, 0.0)
    sv = jnp.einsum('gts,bnsgc->bntgc', ws, v) + b_s.T[:, :, None]
    return u * sv.reshape(Bsz, S, BRANCH_W)


def compress_blocks(k, pe, w1, w2):
    Bsz, G, S, hd = k.shape
    k16 = k.reshape(Bsz, G, S // CMP_STRIDE, CMP_STRIDE, hd)
    blocks = jnp.concatenate([k16[:, :, :-1], k16[:, :, 1:]], axis=3) + pe
    nc = blocks.shape[2]
    hid = jax.nn.gelu(jnp.einsum('bgnl,lh->bgnh', blocks.reshape(Bsz, G, nc, CMP_BLOCK * hd), w1))
    return jnp.einsum('bgnh,hd->bgnd', hid, w2)


def cmp_to_slc_matrix(seq):
    nc = seq // CMP_STRIDE - 1
    nb = seq // SLC_BLOCK
    cs = np.arange(nc)[:, None] * CMP_STRIDE
    ss = np.arange(nb)[None, :] * SLC_BLOCK
    ov = np.clip(np.minimum(cs + CMP_BLOCK, ss + SLC_BLOCK) - np.maximum(cs, ss), 0, None)
    return (ov / CMP_BLOCK).astype(np.float32)


def native_sparse_attention(q, kv, gate_logits, cmp_pe, cmp_w1, cmp_w2, cos, sin):
    Bsz, S, _ = q.shape
    G, HPG, hd = NSA_KV_GROUPS, NSA_HPG, HEAD_DIM
    scale = HEAD_DIM ** -0.5
    qh = q.reshape(Bsz, S, G, HPG, hd).transpose(0, 2, 3, 1, 4)
    kvh = kv.reshape(Bsz, S, 6, G, hd).transpose(2, 0, 3, 1, 4)
    k_cmp, v_cmp, k_slc, v_slc, k_win, v_win = kvh
    gates = jax.nn.sigmoid(gate_logits.reshape(Bsz, S, G, HPG, 3).transpose(0, 2, 3, 1, 4))

    q_rot = apply_partial_rope(qh, cos, sin)
    k_slc = apply_partial_rope(k_slc, cos, sin)
    k_win = apply_partial_rope(k_win, cos, sin)

    kc = compress_blocks(k_cmp, cmp_pe[0], cmp_w1[0], cmp_w2[0])
    vc = compress_blocks(v_cmp, cmp_pe[1], cmp_w1[1], cmp_w2[1])
    nc = kc.shape[2]
    cmp_end = jnp.asarray((np.arange(nc) * CMP_STRIDE + CMP_BLOCK - 1).astype(np.int32))
    slc_map = jnp.asarray(cmp_to_slc_matrix(S))
    nb = S // SLC_BLOCK
    n_sel = min(SLC_TOPK, nb)
    k_slc_blk = k_slc.reshape(Bsz, G, nb, SLC_BLOCK, hd)
    v_slc_blk = v_slc.reshape(Bsz, G, nb, SLC_BLOCK, hd)
    pad = ((0, 0), (0, 0), (WINDOW, 0), (0, 0))
    k_win_pad = jnp.pad(k_win, pad)
    v_win_pad = jnp.pad(v_win, pad)
    bi = jnp.arange(Bsz)[:, None, None, None]
    gi = jnp.arange(G)[None, :, None, None]
    blk = jnp.arange(nb)

    def query_block(i):
        q0 = i * NSA_QCHUNK
        t = q0 + jnp.arange(NSA_QCHUNK)
        qn = lax.dynamic_slice_in_dim(qh, q0, NSA_QCHUNK, axis=3)
        qr = lax.dynamic_slice_in_dim(q_rot, q0, NSA_QCHUNK, axis=3)
        gc = lax.dynamic_slice_in_dim(gates, q0, NSA_QCHUNK, axis=3)

        s = jnp.einsum('bghqd,bgnd->bghqn', qn, kc)
        p_cmp = masked_softmax(s, cmp_end[None, :] <= t[:, None], scale)
        o_cmp = jnp.einsum('bghqn,bgnd->bghqd', p_cmp.astype(vc.dtype), vc)

        imp = jnp.einsum('bghqn,nj->bgqj', p_cmp, slc_map)
        cur = t // SLC_BLOCK
        forced = (blk[None, :] == 0) | (blk[None, :] == cur[:, None]) | (blk[None, :] == cur[:, None] - 1)
        causal_blk = blk[None, :] <= cur[:, None]
        score = jnp.where(forced, FORCE_SCORE, jnp.where(causal_blk, imp, -1.0))
        _, idx = lax.top_k(score, n_sel)
        kg = k_slc_blk[bi, gi, idx]
        vg = v_slc_blk[bi, gi, idx]
        kpos = idx[..., None] * SLC_BLOCK + jnp.arange(SLC_BLOCK)
        m_slc = (kpos <= t[:, None, None]).reshape(Bsz, G, 1, NSA_QCHUNK, n_sel * SLC_BLOCK)
        s = jnp.einsum('bghqd,bgqnkd->bghqnk', qr, kg).reshape(Bsz, G, HPG, NSA_QCHUNK, n_sel * SLC_BLOCK)
        p = masked_softmax(s, m_slc, scale)
        o_slc = jnp.einsum('bghqm,bgqmd->bghqd', p.astype(vg.dtype),
                           vg.reshape(Bsz, G, NSA_QCHUNK, n_sel * SLC_BLOCK, hd))

        kw = lax.dynamic_slice_in_dim(k_win_pad, q0, WINDOW + NSA_QCHUNK, axis=2)
        vw = lax.dynamic_slice_in_dim(v_win_pad, q0, WINDOW + NSA_QCHUNK, axis=2)
        wpos = q0 - WINDOW + jnp.arange(WINDOW + NSA_QCHUNK)
        d = t[:, None] - wpos[None, :]
        m_win = (d >= 0) & (d < WINDOW) & (wpos[None, :] >= 0)
        s = jnp.einsum('bghqd,bgkd->bghqk', qr, kw)
        p = masked_softmax(s, m_win, scale)
        o_win = jnp.einsum('bghqk,bgkd->bghqd', p.astype(vw.dtype), vw)

        return gc[..., 0:1] * o_cmp + gc[..., 1:2] * o_slc + gc[..., 2:3] * o_win

    out = lax.map(query_block, jnp.arange(S // NSA_QCHUNK))
    return out.transpose(1, 0, 4, 2, 3, 5).reshape(Bsz, S, NSA_HEADS * hd)


def short_gated_conv(bcx, conv_w):
    b_g, c_g, xt = jnp.split(bcx, 3, axis=-1)
    h = c_g * xt
    y = lax.conv_general_dilated(h, conv_w[:, None, :], window_strides=(1,),
                                 padding=[(CONV_W - 1, 0)],
                                 dimension_numbers=('NWC', 'WIO', 'NWC'),
                                 feature_group_count=BRANCH_W)
    return b_g * y


def token_mixing(h, w_in, gm_ln_g, gm_ln_b, gm_ws, gm_bs, cmp_pe, cmp_w1, cmp_w2,
                 conv_w, w_branch, w_gate, w_out, cos, sin):
    Bsz, S, _ = h.shape
    proj = h @ w_in
    a_in, q_in, kv_in, ng_in, c_in = jnp.split(proj, IN_SPLITS, axis=-1)
    y_a = gmlp_spatial_gating(a_in, gm_ln_g, gm_ln_b, gm_ws, gm_bs)
    y_b = native_sparse_attention(q_in, kv_in, ng_in, cmp_pe, cmp_w1, cmp_w2, cos, sin)
    y_c = short_gated_conv(c_in, conv_w)
    ys = jnp.stack([y_a, y_b, y_c], axis=2)
    branches = jnp.einsum('bsnw,nwd->bsnd', ys, w_branch)
    gates = jax.nn.sigmoid(h @ w_gate).reshape(Bsz, S, N_BRANCH, D_MODEL)
    merged = jnp.einsum('bsnd,bsnd->bsd', gates, branches)
    return merged @ w_out


def pre_norm_modulate(x, g, shift, scale):
    return rmsnorm(x, g) * (1.0 + scale[:, None, :]) + shift[:, None, :]


def post_norm_residual(x, y, g, gate, res_w):
    return x + res_w * gate[:, None, :] * rmsnorm(y, g)


def setup_inputs(seed: int = 0) -> dict:
    key = jax.random.key(seed)
    ks = jax.random.split(key, 20)
    L = DEPTH

    def nrm(k, shape, fan_in, gain=1.0):
        return gain * fan_in ** -0.5 * jax.random.normal(k, shape, jnp.float32)

    def noise(k, shape, s):
        return s * jax.random.normal(k, shape, jnp.float32)

    return {
        "x": jax.random.normal(ks[0], (BATCH, SEQ, D_MODEL), jnp.float32),
        "c": jax.random.normal(ks[1], (BATCH, D_MODEL), jnp.float32),
        "mod_w": nrm(ks[2], (L, D_MODEL, 9 * D_MODEL), D_MODEL, 0.5),
        "mod_b": noise(ks[3], (L, 9 * D_MODEL), 0.02),
        "norm_g": 1.0 + noise(ks[4], (L, 6, D_MODEL), 0.05),
        "ffn_w13": nrm(ks[5], (L, 2, D_MODEL, 2 * D_FF), D_MODEL),
        "ffn_w2": nrm(ks[6], (L, 2, D_FF, D_MODEL), D_FF),
        "w_in": nrm(ks[7], (L, D_MODEL, IN_COLS), D_MODEL),
        "gm_ln_g": 1.0 + noise(ks[8], (L, BRANCH_W), 0.05),
        "gm_ln_b": noise(ks[9], (L, BRANCH_W), 0.02),
        "gm_ws": nrm(ks[10], (L, GM_GROUPS, GM_CHUNK, GM_CHUNK), GM_CHUNK),
        "gm_bs": 1.0 + noise(ks[11], (L, GM_GROUPS, GM_CHUNK), 0.05),
        "cmp_pe": noise(ks[12], (L, 2, CMP_BLOCK, HEAD_DIM), 0.1),
        "cmp_w1": nrm(ks[13], (L, 2, CMP_BLOCK * HEAD_DIM, CMP_HIDDEN), CMP_BLOCK * HEAD_DIM),
        "cmp_w2": nrm(ks[14], (L, 2, CMP_HIDDEN, HEAD_DIM), CMP_HIDDEN),
        "conv_w": nrm(ks[15], (L, CONV_W, BRANCH_W), CONV_W),
        "w_branch": nrm(ks[16], (L, N_BRANCH, BRANCH_W, D_MODEL), BRANCH_W),
        "w_gate": nrm(ks[17], (L, D_MODEL, N_BRANCH * D_MODEL), D_MODEL),
        "w_out": nrm(ks[18], (L, D_MODEL, D_MODEL), D_MODEL),
    }


def reference(x, c, mod_w, mod_b, norm_g, ffn_w13, ffn_w2, w_in, gm_ln_g, gm_ln_b,
              gm_ws, gm_bs, cmp_pe, cmp_w1, cmp_w2, conv_w, w_branch, w_gate, w_out):
    Bsz, S, _ = x.shape
    cos, sin = rope_tables(S)
    c_act = jax.nn.silu(c)
    for l in range(DEPTH):
        mod = (c_act @ mod_w[l] + mod_b[l]).reshape(Bsz, 9, D_MODEL)
        h = pre_norm_modulate(x, norm_g[l, 0], mod[:, 0], mod[:, 1])
        x = post_norm_residual(x, swiglu(h, ffn_w13[l, 0], ffn_w2[l, 0]), norm_g[l, 1], mod[:, 2], MACARON_W)
        h = pre_norm_modulate(x, norm_g[l, 2], mod[:, 3], mod[:, 4])
        y = token_mixing(h, w_in[l], gm_ln_g[l], gm_ln_b[l], gm_ws[l], gm_bs[l], cmp_pe[l],
                         cmp_w1[l], cmp_w2[l], conv_w[l], w_branch[l], w_gate[l], w_out[l], cos, sin)
        x = post_norm_residual(x, y, norm_g[l, 3], mod[:, 5], 1.0)
        h = pre_norm_modulate(x, norm_g[l, 4], mod[:, 6], mod[:, 7])
        x = post_norm_residual(x, swiglu(h, ffn_w13[l, 1], ffn_w2[l, 1]), norm_g[l, 5], mod[:, 8], MACARON_W)
    return x
```

```python
import os
import numpy as np
from contextlib import ExitStack
import concourse.bass as bass
import concourse.mybir as mybir
from concourse.bass_utils import run_bass_kernel_spmd

F32 = mybir.dt.float32
BF16 = mybir.dt.bfloat16
AF = mybir.ActivationFunctionType
ALU = mybir.AluOpType
AX = mybir.AxisListType

S = 4096
D = 1024
DFF = 2816
L = 2
NCH = D // 128
NJ = DFF // 128
EPS = 1e-6
BW = 512
A_COLS = 1024
Q_COLS = 512
KV_COLS = 768
NG_COLS = 24
C_COLS = 1536
IN_COLS = A_COLS + Q_COLS + KV_COLS + NG_COLS + C_COLS
OFF_Q = A_COLS
OFF_KV = OFF_Q + Q_COLS
OFF_NG = OFF_KV + KV_COLS
OFF_C = OFF_NG + NG_COLS


class Buf:
    __slots__ = ("name", "w", "r", "sem", "last_dma", "excl")

    def __init__(self, name, excl=False):
        self.name = name
        self.excl = excl
        self.w = None
        self.r = []
        self.sem = None
        self.last_dma = None


class Sched:
    ENGS = ("pe", "act", "dve", "pool")

    def __init__(self, nc, es):
        self.nc = nc
        self.es = es
        self.eng = {"pe": nc.tensor, "act": nc.scalar, "dve": nc.vector, "pool": nc.gpsimd, "sp": nc.sync}
        self.cnt = {e: 0 for e in self.ENGS}
        self.esem = {e: es.enter_context(nc.semaphore("sem_" + e)) for e in self.ENGS}
        self.known = {e: {} for e in self.eng}
        self.free_sems = []
        self.all_dsems = []
        self.live_bufs = []
        self.nsem = 0

    def _waits(self, eng, toks):
        need = {}
        for t in toks:
            cur = need.get(t[0])
            if cur is None or cur[1] < t[2]:
                need[t[0]] = (t[1], t[2])
        kn = self.known[eng]
        e = self.eng[eng]
        for key, (sem, val) in need.items():
            if kn.get(key, 0) >= val:
                continue
            kn[key] = val
            e.wait_ge(sem, val)

    def _deps(self, eng, reads, writes):
        toks = []
        for b in reads:
            if b.w is not None and not (eng == "pe" and b.w[3] == "pe"):
                toks.append(b.w)
        for b in writes:
            if b.w is not None and b.w[3] != eng:
                toks.append(b.w)
            for r in b.r:
                if r[3] != eng:
                    toks.append(r)
        return toks

    def _update(self, tok, reads, writes):
        for b in reads:
            if b not in writes:
                b.r.append(tok)
        for b in writes:
            b.w = tok
            b.r = []

    def op(self, eng, fn, reads=(), writes=()):
        if any(b.excl for b in reads):
            writes = list(writes) + [b for b in reads if b.excl]
            reads = [b for b in reads if not b.excl]
        self._waits(eng, self._deps(eng, reads, writes))
        ins = fn(self.eng[eng])
        self.cnt[eng] += 1
        ins.then_inc(self.esem[eng], 1)
        tok = (eng, self.esem[eng], self.cnt[eng], eng)
        self._update(tok, reads, writes)
        return tok

    def _get_sem(self, b):
        if b.sem is None:
            if self.free_sems:
                b.sem = self.free_sems.pop()
            else:
                self.nsem += 1
                s = self.es.enter_context(self.nc.semaphore("dsem%d" % self.nsem))
                b.sem = [s, 0]
                self.all_dsems.append(b.sem)
            self.live_bufs.append(b)
        return b.sem

    def dma(self, queue, out, in_, sbuf, reads=(), writes=(), **kw):
        toks = self._deps("dma", reads, writes)
        if sbuf.last_dma is not None:
            toks.append(sbuf.last_dma)
        self._waits(queue, toks)
        sm = self._get_sem(sbuf)
        ins = self.eng[queue].dma_start(out=out, in_=in_, **kw)
        sm[1] += 16
        ins.then_inc(sm[0], 16)
        tok = (id(sm), sm[0], sm[1], "dma")
        sbuf.last_dma = tok
        self._update(tok, reads, writes)
        return tok

    def barrier(self):
        toks = [(e, self.esem[e], self.cnt[e], e) for e in self.ENGS if self.cnt[e] > 0]
        for sm in self.all_dsems:
            if sm[1] > 0:
                toks.append((id(sm), sm[0], sm[1], "dma"))
        for e in self.eng:
            self._waits(e, [t for t in toks if t[0] != e])
        for b in self.live_bufs:
            if b.sem is not None:
                self.free_sems.append(b.sem)
                b.sem = None
        self.live_bufs = []

    def finish(self):
        toks = []
        for sm in self.all_dsems:
            if sm[1] > 0:
                toks.append((id(sm), sm[0], sm[1], "dma"))
        toks += [(e, self.esem[e], self.cnt[e], e) for e in self.ENGS if self.cnt[e] > 0]
        self._waits("sp", toks)


class Ctx:
    pass


_UID = [0]


def _alloc(nc, es, name, shape, dt):
    _UID[0] += 1
    return es.enter_context(nc.sbuf_tensor("%s_%d" % (name, _UID[0]), list(shape), dt))


def _palloc(nc, es, name, shape, dt):
    _UID[0] += 1
    return es.enter_context(nc.psum_tensor("%s_%d" % (name, _UID[0]), list(shape), dt))


def phase_transpose_in(cx, x_ap, xT_ap):
    nc, sc = cx.nc, cx.sc
    with ExitStack() as es:
        NB = 2
        xin = [_alloc(nc, es, "ti_x%d" % i, [128, 4, D], F32) for i in range(NB)]
        xo = [_alloc(nc, es, "ti_o%d" % i, [128, NCH, 512], F32) for i in range(NB)]
        ps = [_palloc(nc, es, "ti_p%d" % i, [128, 512], F32) for i in range(4)]
        b_in = [Buf("ti_x") for _ in range(NB)]
        b_o = [Buf("ti_o") for _ in range(NB)]
        b_ps = [Buf("ti_p", True) for _ in range(4)]
        pi = 0
        for t in range(S // 512):
            k = t % NB
            src = x_ap[t * 512:(t + 1) * 512, :].rearrange("(a p) f -> p a f", p=128)
            sc.dma("sp", xin[k][:], src, b_in[k], reads=[cx.b_x], writes=[b_in[k]])
            for ch in range(NCH):
                pb = pi % 4
                pi += 1
                for a in range(4):
                    sc.op("pe", lambda e, a=a, ch=ch, pb=pb, k=k: e.transpose(
                        ps[pb][:, a * 128:(a + 1) * 128], xin[k][:, a, ch * 128:(ch + 1) * 128], cx.ident[:]),
                        reads=[b_in[k], cx.b_const], writes=[b_ps[pb]])
                if ch % 2 == 0:
                    sc.op("dve", lambda e, ch=ch, pb=pb, k=k: e.tensor_copy(out=xo[k][:, ch, :], in_=ps[pb][:]),
                          reads=[b_ps[pb]], writes=[b_o[k]])
                else:
                    sc.op("act", lambda e, ch=ch, pb=pb, k=k: e.copy(out=xo[k][:, ch, :], in_=ps[pb][:]),
                          reads=[b_ps[pb]], writes=[b_o[k]])
            dst = xT_ap[:, t * 512:(t + 1) * 512].rearrange("(c p) n -> p c n", p=128)
            sc.dma("sp", dst, xo[k][:], b_o[k], reads=[b_o[k]], writes=[cx.b_xT[t]])
        sc.barrier()


def phase_mod(cx, c_ap, mod_w_ap, mod_b_ap, norm_g_ap):
    nc, sc = cx.nc, cx.sc
    with ExitStack() as es:
        crow = _alloc(nc, es, "md_crow", [8, 128], F32)
        cT = _alloc(nc, es, "md_cT", [128, 8], F32)
        sg = _alloc(nc, es, "md_sg", [128, 8], F32)
        brow = _alloc(nc, es, "md_brow", [72, 128], F32)
        grow = _alloc(nc, es, "md_grow", [48, 128], F32)
        wbuf = [_alloc(nc, es, "md_w%d" % i, [128, NCH, 1024], F32) for i in range(2)]
        pt = _palloc(nc, es, "md_pt", [128, 512], F32)
        pm = _palloc(nc, es, "md_pm", [128, 512], F32)
        b_crow, b_cT, b_brow, b_grow = (Buf(n) for n in ("crow", "cT", "brow", "grow"))
        b_pt, b_pm = Buf("pt", True), Buf("pm", True)
        b_w = [Buf("md_w") for _ in range(2)]
        b_mod = cx.b_mod
        sc.dma("sp", crow[:], c_ap.rearrange("o (a p) -> (o a) p", p=128), b_crow, writes=[b_crow])
        sc.op("pe", lambda e: e.transpose(pt[:, 0:8], crow[:], cx.ident[0:8, 0:8]),
              reads=[b_crow, cx.b_const], writes=[b_pt])
        sc.op("act", lambda e: e.activation(out=sg[:], in_=pt[:, 0:8], func=AF.Sigmoid), reads=[b_pt], writes=[b_cT])
        sc.op("dve", lambda e: e.tensor_tensor(out=cT[:], in0=pt[:, 0:8], in1=sg[:], op=ALU.mult),
              reads=[b_pt, b_cT], writes=[b_cT])
        for l in range(L):
            sc.dma("sp", brow[:], mod_b_ap[l].rearrange("(a p) -> a p", p=128), b_brow, writes=[b_brow])
            sc.dma("sp", grow[:], norm_g_ap[l].rearrange("k (a p) -> (k a) p", p=128), b_grow, writes=[b_grow])
            sc.op("pe", lambda e: e.transpose(pt[:, 0:72], brow[:], cx.ident[0:72, 0:72]),
                  reads=[b_brow, cx.b_const], writes=[b_pt])
            sc.op("dve", lambda e, l=l: e.tensor_copy(out=cx.modbT[l][:], in_=pt[:, 0:72]), reads=[b_pt], writes=[b_mod])
            sc.op("pe", lambda e: e.transpose(pt[:, 0:48], grow[:], cx.ident[0:48, 0:48]),
                  reads=[b_grow, cx.b_const], writes=[b_pt])
            sc.op("dve", lambda e, l=l: e.tensor_copy(out=cx.normgT[l][:], in_=pt[:, 0:48]), reads=[b_pt], writes=[b_mod])
            for v in range(9):
                k = v % 2
                src = mod_w_ap[l][:, v * 1024:(v + 1) * 1024].rearrange("(a p) n -> p a n", p=128)
                sc.dma("sp", wbuf[k][:], src, b_w[k], writes=[b_w[k]])
                for ch in range(NCH):
                    col = v * 8 + ch
                    for kc in range(NCH):
                        sc.op("pe", lambda e, k=k, ch=ch, kc=kc, col=col: e.matmul(
                            pm[:, col:col + 1], wbuf[k][:, kc, ch * 128:(ch + 1) * 128], cT[:, kc:kc + 1],
                            start=(kc == 0), stop=(kc == NCH - 1)),
                            reads=[b_w[k], b_cT], writes=[b_pm])
            sc.op("dve", lambda e, l=l: e.tensor_tensor(out=cx.modT[l][:], in0=pm[:, 0:72], in1=cx.modbT[l][:], op=ALU.add),
                  reads=[b_pm, b_mod], writes=[b_mod])
            for s_ in range(3):
                g0 = cx.normgT[l][:, (2 * s_) * 8:(2 * s_ + 1) * 8]
                g1 = cx.normgT[l][:, (2 * s_ + 1) * 8:(2 * s_ + 2) * 8]
                shift = cx.modT[l][:, (3 * s_) * 8:(3 * s_ + 1) * 8]
                scale = cx.modT[l][:, (3 * s_ + 1) * 8:(3 * s_ + 2) * 8]
                gate = cx.modT[l][:, (3 * s_ + 2) * 8:(3 * s_ + 3) * 8]
                resw = 1.0 if s_ == 1 else 0.5
                sc.op("dve", lambda e, l=l, s_=s_, scale=scale, g0=g0: e.scalar_tensor_tensor(
                    out=cx.modA[l][s_][:], in0=scale, scalar=1.0, in1=g0, op0=ALU.add, op1=ALU.mult),
                    reads=[b_mod], writes=[b_mod])
                sc.op("dve", lambda e, l=l, s_=s_, shift=shift: e.tensor_copy(out=cx.modB[l][s_][:], in_=shift),
                      reads=[b_mod], writes=[b_mod])
                sc.op("dve", lambda e, l=l, s_=s_, gate=gate, g1=g1, resw=resw: e.scalar_tensor_tensor(
                    out=cx.modC[l][s_][:], in0=gate, scalar=resw, in1=g1, op0=ALU.mult, op1=ALU.mult),
                    reads=[b_mod], writes=[b_mod])
        sc.barrier()


def emit_rstd(cx, sq, b_sq, ps, b_ps, rstd, b_rstd, lnt, TT):
    sc = cx.sc
    for ch in range(NCH):
        sc.op("pe", lambda e, ch=ch: e.matmul(ps[:, 0:TT], cx.ones_bf[:], sq[:, ch, :],
                                              start=(ch == 0), stop=(ch == NCH - 1)),
              reads=[b_sq, cx.b_const], writes=[b_ps])
    sc.op("act", lambda e: e.activation(out=lnt[:, 0:TT], in_=ps[:, 0:TT], func=AF.Ln, scale=1.0 / D, bias=cx.eps_col[:]),
          reads=[b_ps, cx.b_const], writes=[b_rstd])
    sc.op("act", lambda e: e.activation(out=rstd[:, 0:TT], in_=lnt[:, 0:TT], func=AF.Exp, scale=-0.5),
          reads=[b_rstd], writes=[b_rstd])


def phase_ffn(cx, l, i, w13_ap, w2_ap, xT_in, b_xin, xT_out, b_xout, out_tok=None):
    nc, sc = cx.nc, cx.sc
    TT = 256
    import os
    NT = int(os.environ.get('FFN_NT', S // TT))
    s_ = 0 if i == 0 else 2
    A, B, C = cx.modA[l][s_], cx.modB[l][s_], cx.modC[l][s_]
    with ExitStack() as es:
        w13s = _alloc(nc, es, "f_w13", [128, NCH, 2 * DFF], BF16)
        w2s = _alloc(nc, es, "f_w2", [128, NJ, D], BF16)
        xt = [_alloc(nc, es, "f_x%d" % k, [128, NCH, TT], F32) for k in range(2)]
        hT = _alloc(nc, es, "f_h", [128, NCH, TT], BF16)
        hid = _alloc(nc, es, "f_hid", [128, NJ, TT], BF16)
        yT = _alloc(nc, es, "f_y", [128, NCH, TT], F32)
        sq = _alloc(nc, es, "f_sq", [128, NCH, TT], BF16)
        rstd = _alloc(nc, es, "f_rstd", [128, TT], F32)
        lnt = _alloc(nc, es, "f_lnt", [128, TT], F32)
        tmp = [_alloc(nc, es, "f_tmp%d" % k, [128, TT], F32) for k in range(2)]
        sgt = [_alloc(nc, es, "f_sg%d" % k, [128, TT], F32) for k in range(2)]
        if out_tok is not None:
            otok = [_alloc(nc, es, "f_ot%d" % k, [128, D], F32) for k in range(2)]
            b_otok = [Buf("otok") for _ in range(2)]
        pG = [_palloc(nc, es, "f_pg%d" % k, [128, 512], F32) for k in range(2)]
        pU = [_palloc(nc, es, "f_pu%d" % k, [128, 512], F32) for k in range(2)]
        pY = [_palloc(nc, es, "f_py%d" % k, [128, 512], F32) for k in range(2)]
        pS = _palloc(nc, es, "f_ps", [128, 512], F32)
        pT = _palloc(nc, es, "f_pt", [128, 512], F32)
        b_w13 = [Buf("w13") for _ in range(NCH)]
        b_w2 = [Buf("w2") for _ in range(NJ)]
        b_x = [Buf("x") for _ in range(2)]
        b_h, b_hid, b_y, b_sq, b_rstd = (Buf(n) for n in ("h", "hid", "y", "sq", "rstd"))
        b_pS, b_pT = Buf("pS", True), Buf("pT", True)
        b_hidj = [Buf("hidj") for _ in range(NJ)]
        b_yc = [Buf("yc") for _ in range(NCH)]
        b_tmp = [Buf("tmp") for _ in range(2)]
        b_sg = [Buf("sg") for _ in range(2)]
        b_pG = [Buf("pG", True) for _ in range(2)]
        b_pU = [Buf("pU", True) for _ in range(2)]
        b_pY = [Buf("pY", True) for _ in range(2)]

        CW = 1408
        for kc in range(NCH):
            for cc in range(2 * DFF // CW):
                sc.dma("pool", w13s[:, kc, cc * CW:(cc + 1) * CW],
                       w13_ap[kc * 128:(kc + 1) * 128, cc * CW:(cc + 1) * CW], b_w13[kc], writes=[b_w13[kc]])
        for j in range(NJ):
            sc.dma("pool", w2s[:, j, :], w2_ap[j * 128:(j + 1) * 128, :], b_w2[j], writes=[b_w2[j]])

        def load_x(t):
            k = t % 2
            src = xT_in[:, t * TT:(t + 1) * TT].rearrange("(c p) n -> p c n", p=128)
            sc.dma("sp", xt[k][:], src, b_x[k], writes=[b_x[k]])

        load_x(0)
        for t in range(NT):
            k = t % 2
            if t + 1 < NT:
                load_x(t + 1)
            X = xt[k]
            STG = float(os.environ.get('FFN_STG', 9))
            for ch in range(NCH if STG >= 0.25 else 0):
                sc.op("act", lambda e, ch=ch, X=X: e.activation(out=sq[:, ch, :], in_=X[:, ch, :], func=AF.Square),
                      reads=[b_x[k]], writes=[b_sq])
            if STG >= 0.5:
                emit_rstd(cx, sq, b_sq, pS, b_pS, rstd, b_rstd, lnt, TT)
            for ch in range(NCH if STG >= 0.75 else 0):
                kk = ch % 2
                sc.op("dve", lambda e, ch=ch, X=X, kk=kk: e.scalar_tensor_tensor(
                    out=tmp[kk][:], in0=X[:, ch, :], scalar=A[:, ch:ch + 1], in1=rstd[:], op0=ALU.mult, op1=ALU.mult),
                    reads=[b_x[k], b_rstd, cx.b_mod], writes=[b_tmp[kk]])
                if STG >= 1:
                  sc.op("act", lambda e, ch=ch, kk=kk: e.activation(
                    out=hT[:, ch, :], in_=tmp[kk][:], func=AF.Identity, bias=B[:, ch:ch + 1], scale=1.0),
                    reads=[b_tmp[kk], cx.b_mod], writes=[b_h])
            for j in range(NJ if STG >= 2 else 0):
                kk = j % 2
                for kc in range(NCH):
                    sc.op("pe", lambda e, j=j, kc=kc, kk=kk: e.matmul(
                        pG[kk][:, 0:TT], w13s[:, kc, j * 128:(j + 1) * 128], hT[:, kc, :],
                        start=(kc == 0), stop=(kc == NCH - 1)),
                        reads=[b_w13[kc], b_h], writes=[b_pG[kk]])
                for kc in range(NCH):
                    sc.op("pe", lambda e, j=j, kc=kc, kk=kk: e.matmul(
                        pU[kk][:, 0:TT], w13s[:, kc, DFF + j * 128:DFF + (j + 1) * 128], hT[:, kc, :],
                        start=(kc == 0), stop=(kc == NCH - 1)),
                        reads=[b_w13[kc], b_h], writes=[b_pU[kk]])
                sc.op("act", lambda e, kk=kk: e.activation(out=sgt[kk][:], in_=pG[kk][:, 0:TT], func=AF.Silu),
                      reads=[b_pG[kk]], writes=[b_sg[kk]])
                sc.op("dve", lambda e, j=j, kk=kk: e.tensor_tensor(out=hid[:, j, :], in0=pU[kk][:, 0:TT], in1=sgt[kk][:],
                                                                   op=ALU.mult),
                      reads=[b_pU[kk], b_sg[kk]], writes=[b_hidj[j]])
            for oc in range(NCH if STG >= 3 else 0):
                kk = oc % 2
                for j in range(NJ):
                    sc.op("pe", lambda e, j=j, oc=oc, kk=kk: e.matmul(
                        pY[kk][:, 0:TT], w2s[:, j, oc * 128:(oc + 1) * 128], hid[:, j, :],
                        start=(j == 0), stop=(j == NJ - 1)),
                        reads=[b_w2[j], b_hidj[j]], writes=[b_pY[kk]])
                sc.op("dve", lambda e, oc=oc, kk=kk: e.tensor_copy(out=yT[:, oc, :], in_=pY[kk][:, 0:TT]),
                      reads=[b_pY[kk]], writes=[b_yc[oc]])
                sc.op("act", lambda e, oc=oc: e.activation(out=sq[:, oc, :], in_=yT[:, oc, :], func=AF.Square),
                      reads=[b_yc[oc]], writes=[b_sq])
            if STG >= 4:
                emit_rstd(cx, sq, b_sq, pS, b_pS, rstd, b_rstd, lnt, TT)
            for ch in range(NCH if STG >= 4 else 0):
                kk = ch % 2
                sc.op("dve", lambda e, ch=ch, kk=kk: e.scalar_tensor_tensor(
                    out=tmp[kk][:], in0=yT[:, ch, :], scalar=C[:, ch:ch + 1], in1=rstd[:], op0=ALU.mult, op1=ALU.mult),
                    reads=[b_yc[ch], b_rstd, cx.b_mod], writes=[b_tmp[kk]])
                sc.op("pool", lambda e, ch=ch, X=X, kk=kk: e.tensor_tensor(out=X[:, ch, :], in0=X[:, ch, :], in1=tmp[kk][:],
                                                                           op=ALU.add),
                      reads=[b_x[k], b_tmp[kk]], writes=[b_x[k]])
            if out_tok is None:
                dst = xT_out[:, t * TT:(t + 1) * TT].rearrange("(c p) n -> p c n", p=128)
                sc.dma("sp", dst, X[:], b_x[k], reads=[b_x[k]])
            else:
                for a in range(TT // 128):
                    ko = (t * (TT // 128) + a) % 2
                    for half in range(2):
                        for c4 in range(4):
                            ch = half * 4 + c4
                            sc.op("pe", lambda e, a=a, ch=ch, c4=c4, X=X: e.transpose(
                                pT[:, c4 * 128:(c4 + 1) * 128], X[:, ch, a * 128:(a + 1) * 128], cx.ident[:]),
                                reads=[b_x[k], cx.b_const], writes=[b_pT])
                        if half == 0:
                            sc.op("dve", lambda e, ko=ko: e.tensor_copy(out=otok[ko][:, 0:512], in_=pT[:]),
                                  reads=[b_pT], writes=[b_otok[ko]])
                        else:
                            sc.op("act", lambda e, ko=ko: e.copy(out=otok[ko][:, 512:1024], in_=pT[:]),
                                  reads=[b_pT], writes=[b_otok[ko]])
                    r0 = t * TT + a * 128
                    sc.dma("sp", out_tok[r0:r0 + 128, :], otok[ko][:], b_otok[ko], reads=[b_otok[ko]], writes=[cx.b_out])
        sc.barrier()


class Ring:
    def __init__(self, tiles, name, excl=False):
        self.tiles = tiles
        self.bufs = [Buf(name, excl) for _ in tiles]
        self.i = 0

    def next(self):
        k = self.i % len(self.tiles)
        self.i += 1
        return self.tiles[k], self.bufs[k]


FM_U, FM_Q, FM_QS, FM_KCMP, FM_VCMP, FM_KSLC, FM_KSLCS, FM_KWIN, FM_KWINS, FM_C, FM_X, FM_B = 0, 4, 8, 12, 13, 14, 15, 16, 17, 18, 22, 26
N_FM = 30
TM_OFF = N_FM * 128
TM_W = 280
WP_COLS = TM_OFF + 512 + TM_W


def phase_proj(cx, l, xT_in, dr):
    nc, sc = cx.nc, cx.sc
    TT = 256
    NT = int(os.environ.get("PROJ_NT", S // TT))
    A, B = cx.modA[l][1], cx.modB[l][1]
    with ExitStack() as es:
        wp = _alloc(nc, es, "p_wp", [128, NCH, WP_COLS], BF16)
        xt = Ring([_alloc(nc, es, "p_x%d" % k, [128, NCH, TT], F32) for k in range(2)], "x")
        sq = _alloc(nc, es, "p_sq", [128, NCH, TT], BF16)
        rstd = _alloc(nc, es, "p_rstd", [128, TT], F32)
        lnt = _alloc(nc, es, "p_lnt", [128, TT], F32)
        tmp = Ring([_alloc(nc, es, "p_tmp%d" % k, [128, TT], F32) for k in range(3)], "tmp")
        tmq = Ring([_alloc(nc, es, "p_tmq%d" % k, [128, TT], F32) for k in range(2)], "tmq")
        hT = Ring([_alloc(nc, es, "p_h%d" % k, [128, NCH, TT], BF16) for k in range(2)], "h")
        uT = _alloc(nc, es, "p_u", [128, 4, TT], BF16)
        qr = Ring([_alloc(nc, es, "p_qr%d" % k, [128, 4, TT], BF16) for k in range(2)], "qr")
        qn = Ring([_alloc(nc, es, "p_qn%d" % k, [128, 4, TT], BF16) for k in range(2)], "qn")
        kst = Ring([_alloc(nc, es, "p_kst%d" % k, [128, 2, TT], BF16) for k in range(2)], "kst")
        kcm = Ring([_alloc(nc, es, "p_kcm%d" % k, [128, 2, TT], BF16) for k in range(2)], "kcm")
        yaT = Ring([_alloc(nc, es, "p_ya%d" % k, [128, 4, TT], BF16) for k in range(2)], "ya")
        ycT = Ring([_alloc(nc, es, "p_yc%d" % k, [128, 4, TT], BF16) for k in range(2)], "yc")
        vst = Ring([_alloc(nc, es, "p_vst%d" % k, [128, TT // 128, 256], BF16) for k in range(2)], "vst")
        ngst = Ring([_alloc(nc, es, "p_ng%d" % k, [128, TT // 128, 24], F32) for k in range(2)], "ng")
        rc = Ring([_alloc(nc, es, "p_rc%d" % k, [128, TT], F32) for k in range(2)], "rc")
        rs = Ring([_alloc(nc, es, "p_rs%d" % k, [128, TT], F32) for k in range(2)], "rs")
        hc = _alloc(nc, es, "p_hc", [128, 4, TT + 2], F32)
        xs = Ring([_alloc(nc, es, "p_xs%d" % k, [128, TT], F32) for k in range(2)], "xs")
        acc = Ring([_alloc(nc, es, "p_acc%d" % k, [128, TT], F32) for k in range(2)], "acc")
        vf = [_alloc(nc, es, "p_vf%d" % k, [128, 512], F32) for k in range(TT // 128)]
        vt = Ring([_alloc(nc, es, "p_vt%d" % k, [128, 512], F32) for k in range(2)], "vt")
        vn = Ring([_alloc(nc, es, "p_vn%d" % k, [128, 512], BF16) for k in range(2)], "vn")
        st6 = _alloc(nc, es, "p_st6", [128, TT // 128, 6], F32)
        mv = _alloc(nc, es, "p_mv", [128, TT // 128, 2], F32)
        lv = _alloc(nc, es, "p_lv", [128, TT // 128], F32)
        rv = _alloc(nc, es, "p_rv", [128, TT // 128], F32)
        wsT = _alloc(nc, es, "p_wsT", [128, 4, 128], BF16)
        wsr = _alloc(nc, es, "p_wsr", [128, 4, 128], F32)
        bsr = _alloc(nc, es, "p_bsr", [1, 512], F32)
        bsb = _alloc(nc, es, "p_bsb", [1, 512], BF16)
        onesr = _alloc(nc, es, "p_onesr", [1, 128], BF16)
        onesf = _alloc(nc, es, "p_onesf", [1, 128], F32)
        lrow = _alloc(nc, es, "p_lrow", [1, 1024], F32)
        lng = _alloc(nc, es, "p_lng", [128, 512], F32)
        lnb = _alloc(nc, es, "p_lnb", [128, 512], F32)
        cwr = _alloc(nc, es, "p_cwr", [3, 512], F32)
        cw = _alloc(nc, es, "p_cw", [128, 4, 3], F32)
        b_hc = [Buf("hc") for _ in range(4)]
        b_u = [Buf("u") for _ in range(4)]
        b_vf = [Buf("vf") for _ in range(TT // 128)]
        b_sq, b_rstd, b_st, b_set = Buf("sq"), Buf("rstd"), Buf("st"), Buf("set")
        pS = _palloc(nc, es, "p_pS", [128, 512], F32)
        b_pS = Buf("pS", True)
        pF = Ring([_palloc(nc, es, "p_pF%d" % k, [128, 512], F32) for k in range(4)], "pF", True)
        pV = _palloc(nc, es, "p_pV", [128, 512], F32)
        b_pV = Buf("pV", True)
        pW = _palloc(nc, es, "p_pW", [128, 512], F32)
        b_pW = Buf("pW", True)
        pG = _palloc(nc, es, "p_pG", [128, 512], F32)
        b_pG = Buf("pG", True)
        b_wp = [Buf("wp") for _ in range(NCH)]

        CW = 1544
        for kc in range(NCH):
            for cc in range(WP_COLS // CW):
                sc.dma("pool", wp[:, kc, cc * CW:(cc + 1) * CW],
                       dr.wp[l][kc * 128:(kc + 1) * 128, cc * CW:(cc + 1) * CW], b_wp[kc], writes=[b_wp[kc]])
        sc.dma("sp", wsr[:], dr.gm_ws[l].rearrange("g t s -> t g s"), b_set, writes=[b_set])
        sc.dma("sp", bsr[:], dr.gm_bs[l].rearrange("(o g) t -> o (g t)", o=1), b_set, writes=[b_set])
        sc.dma("sp", lrow[:, 0:512], dr.gm_ln_g[l].rearrange("(o n) -> o n", o=1), b_set, writes=[b_set])
        sc.dma("sp", lrow[:, 512:1024], dr.gm_ln_b[l].rearrange("(o n) -> o n", o=1), b_set, writes=[b_set])
        sc.dma("sp", cwr[:], dr.conv_w[l], b_set, writes=[b_set])
        sc.op("dve", lambda e: e.memset(onesr[:], 1.0), writes=[b_set])
        sc.op("dve", lambda e: e.memset(onesf[:], 1.0), writes=[b_set])
        sc.op("dve", lambda e: e.tensor_copy(out=bsb[:], in_=bsr[:]), reads=[b_set], writes=[b_set])
        sc.op("dve", lambda e: e.memset(hc[:, :, 0:2], 0.0), writes=b_hc)
        for g in range(4):
            sc.op("pe", lambda e, g=g: e.transpose(pG[:, 0:128], wsr[:, g, :], cx.ident[:]),
                  reads=[b_set, cx.b_const], writes=[b_pG])
            sc.op("dve", lambda e, g=g: e.tensor_tensor(out=wsT[:, g, :], in0=pG[:, 0:128], in1=cx.triL[:], op=ALU.mult),
                  reads=[b_pG, cx.b_const], writes=[b_set])
        for hh, dst in ((0, lng), (1, lnb)):
            sc.op("pe", lambda e, hh=hh: e.matmul(pG[:, 0:512], onesf[0:1, :], lrow[0:1, hh * 512:(hh + 1) * 512],
                                                  start=True, stop=True), reads=[b_set], writes=[b_pG])
            sc.op("dve", lambda e, dst=dst: e.tensor_copy(out=dst[:], in_=pG[:, 0:512]), reads=[b_pG], writes=[b_set])
        for cc in range(4):
            sc.op("pe", lambda e, cc=cc: e.transpose(pG[:, 0:3], cwr[:, cc * 128:(cc + 1) * 128], cx.ident[0:3, 0:3]),
                  reads=[b_set, cx.b_const], writes=[b_pG])
            sc.op("dve", lambda e, cc=cc: e.tensor_copy(out=cw[:, cc, :], in_=pG[:, 0:3]), reads=[b_pG], writes=[b_set])

        def fm_chunk(idx, H, b_H):
            P, b_P = pF.next()
            for kc in range(NCH):
                sc.op("pe", lambda e, kc=kc, P=P: e.matmul(P[:, 0:TT], wp[:, kc, idx * 128:(idx + 1) * 128], H[:, kc, :],
                                                          start=(kc == 0), stop=(kc == NCH - 1)),
                      reads=[b_wp[kc], b_H], writes=[b_P])
            return P, b_P

        def load_x(t):
            X, b_X = xt.next()
            src = xT_in[:, t * TT:(t + 1) * TT].rearrange("(c p) n -> p c n", p=128)
            sc.dma("sp", X[:], src, b_X, writes=[b_X])
            return X, b_X

        nxt = load_x(0)
        for t in range(NT):
            X, b_X = nxt
            if t + 1 < NT:
                nxt = load_x(t + 1)
            t0 = t * TT
            RC, b_RC = rc.next()
            RS, b_RS = rs.next()
            sc.dma("sp", RC[:], dr.ropeC[:, t0:t0 + TT], b_RC, writes=[b_RC])
            sc.dma("sp", RS[:], dr.ropeS[:, t0:t0 + TT], b_RS, writes=[b_RS])
            for ch in range(NCH):
                sc.op("act", lambda e, ch=ch, X=X: e.activation(out=sq[:, ch, :], in_=X[:, ch, :], func=AF.Square),
                      reads=[b_X], writes=[b_sq])
            emit_rstd(cx, sq, b_sq, pS, b_pS, rstd, b_rstd, lnt, TT)
            H, b_H = hT.next()
            for ch in range(NCH):
                T1, b_T1 = tmp.next()
                sc.op("dve", lambda e, ch=ch, X=X, T1=T1: e.scalar_tensor_tensor(
                    out=T1[:], in0=X[:, ch, :], scalar=A[:, ch:ch + 1], in1=rstd[:], op0=ALU.mult, op1=ALU.mult),
                    reads=[b_X, b_rstd, cx.b_mod], writes=[b_T1])
                sc.op("act", lambda e, ch=ch, T1=T1, H=H: e.activation(
                    out=H[:, ch, :], in_=T1[:], func=AF.Identity, bias=B[:, ch:ch + 1], scale=1.0),
                    reads=[b_T1, cx.b_mod], writes=[b_H])
            sc.dma("sp", dr.hT[:, t0:t0 + TT].rearrange("(c p) n -> p c n", p=128), H[:], b_H, reads=[b_H])
            for a in range(TT // 128):
                for kc in range(NCH):
                    sc.op("pe", lambda e, kc=kc, a=a, H=H: e.matmul(
                        pV[:, :], H[:, kc, a * 128:(a + 1) * 128], wp[:, kc, TM_OFF:TM_OFF + 512],
                        start=(kc == 0), stop=(kc == NCH - 1)), reads=[b_wp[kc], b_H], writes=[b_pV])
                sc.op("act", lambda e, a=a: e.activation(out=vf[a][:], in_=pV[:], func=AF.Gelu_apprx_tanh),
                      reads=[b_pV], writes=[b_vf[a]])
                sc.op("dve", lambda e, a=a: e.bn_stats(out=st6[:, a, :], in_=vf[a][:]), reads=[b_vf[a]], writes=[b_st])
                sc.op("dve", lambda e, a=a: e.bn_aggr(out=mv[:, a, :], in_=st6[:, a, :]), reads=[b_st], writes=[b_st])
            sc.op("act", lambda e: e.activation(out=lv[:], in_=mv[:, :, 1], func=AF.Ln, scale=1.0, bias=cx.eps_col[:]),
                  reads=[b_st, cx.b_const], writes=[b_st])
            sc.op("act", lambda e: e.activation(out=rv[:], in_=lv[:], func=AF.Exp, scale=-0.5), reads=[b_st], writes=[b_st])
            for cc in range(4):
                P, b_P = fm_chunk(FM_U + cc, H, b_H)
                sc.op("act", lambda e, cc=cc, P=P: e.activation(out=uT[:, cc, :], in_=P[:, 0:TT], func=AF.Gelu_apprx_tanh),
                      reads=[b_P], writes=[b_u[cc]])
            YA, b_YA = yaT.next()
            VS, b_VS = vst.next()
            NG, b_NG = ngst.next()
            for a in range(TT // 128):
                V1, b_V1 = vt.next()
                sc.op("dve", lambda e, a=a, V1=V1: e.tensor_scalar(
                    out=V1[:], in0=vf[a][:], scalar1=mv[:, a, 0:1], scalar2=rv[:, a:a + 1], op0=ALU.subtract, op1=ALU.mult),
                    reads=[b_vf[a], b_st], writes=[b_V1])
                sc.op("pool", lambda e, V1=V1: e.tensor_tensor(out=V1[:], in0=V1[:], in1=lng[:], op=ALU.mult),
                      reads=[b_V1, b_set], writes=[b_V1])
                VN, b_VN = vn.next()
                sc.op("pool", lambda e, V1=V1, VN=VN: e.tensor_tensor(out=VN[:], in0=V1[:], in1=lnb[:], op=ALU.add),
                      reads=[b_V1, b_set], writes=[b_VN])
                for g in range(4):
                    sc.op("pe", lambda e, g=g, VN=VN: e.matmul(pG[:, g * 128:(g + 1) * 128], VN[:, g * 128:(g + 1) * 128],
                                                               wsT[:, g, :], start=True, stop=False),
                          reads=[b_VN, b_set], writes=[b_pG])
                    sc.op("pe", lambda e, g=g: e.matmul(pG[:, g * 128:(g + 1) * 128], onesr[0:1, :],
                                                        bsb[0:1, g * 128:(g + 1) * 128], start=False, stop=True),
                          reads=[b_set], writes=[b_pG])
                sc.op("dve", lambda e, a=a, YA=YA: e.tensor_tensor(
                    out=YA[:, :, a * 128:(a + 1) * 128], in0=pG[:, 0:512].rearrange("p (g t) -> p g t", g=4),
                    in1=uT[:, :, a * 128:(a + 1) * 128], op=ALU.mult),
                    reads=[b_pG] + b_u, writes=[b_YA])
                for kc in range(NCH):
                    sc.op("pe", lambda e, kc=kc, a=a, H=H: e.matmul(
                        pW[:, 0:TM_W], H[:, kc, a * 128:(a + 1) * 128], wp[:, kc, TM_OFF + 512:TM_OFF + 512 + TM_W],
                        start=(kc == 0), stop=(kc == NCH - 1)), reads=[b_wp[kc], b_H], writes=[b_pW])
                sc.op("act", lambda e, a=a, VS=VS: e.copy(out=VS[:, a, :], in_=pW[:, 0:256]), reads=[b_pW], writes=[b_VS])
                sc.op("act", lambda e, a=a, NG=NG: e.activation(out=NG[:, a, :], in_=pW[:, 256:280], func=AF.Sigmoid),
                      reads=[b_pW], writes=[b_NG])
            sc.dma("sp", dr.yaT[:, t0:t0 + TT].rearrange("(c p) n -> p c n", p=128), YA[:], b_YA, reads=[b_YA])
            sc.dma("sp", dr.v[t0:t0 + TT, :].rearrange("(a p) c -> p a c", p=128), VS[:], b_VS, reads=[b_VS])
            sc.dma("sp", dr.ng[t0:t0 + TT, :].rearrange("(a p) c -> p a c", p=128), NG[:], b_NG, reads=[b_NG])
            QR, b_QR = qr.next()
            QN, b_QN = qn.next()

            def rope(Pq, b_Pq, Ps, b_Ps, dst, b_dst):
                T1, b_T1 = tmq.next()
                T2, b_T2 = tmp.next()
                sc.op("dve", lambda e: e.tensor_tensor(out=T1[:], in0=Ps[:, 0:TT], in1=RS[:], op=ALU.mult),
                      reads=[b_Ps, b_RS], writes=[b_T1])
                sc.op("dve", lambda e: e.tensor_tensor(out=T2[:], in0=Pq[:, 0:TT], in1=RC[:], op=ALU.mult),
                      reads=[b_Pq, b_RC], writes=[b_T2])
                sc.op("pool", lambda e: e.tensor_tensor(out=dst, in0=T1[:], in1=T2[:], op=ALU.add),
                      reads=[b_T1, b_T2], writes=[b_dst])

            for qc in range(4):
                Pq, b_Pq = fm_chunk(FM_Q + qc, H, b_H)
                Ps, b_Ps = fm_chunk(FM_QS + qc, H, b_H)
                sc.op("act", lambda e, qc=qc, Pq=Pq, QN=QN: e.copy(out=QN[:, qc, :], in_=Pq[:, 0:TT]),
                      reads=[b_Pq], writes=[b_QN])
                rope(Pq, b_Pq, Ps, b_Ps, QR[:, qc, :], b_QR)
            sc.dma("sp", dr.qrT[:, t0:t0 + TT].rearrange("(c p) n -> p c n", p=128), QR[:], b_QR, reads=[b_QR])
            sc.dma("sp", dr.qnT[:, t0:t0 + TT].rearrange("(c p) n -> p c n", p=128), QN[:], b_QN, reads=[b_QN])
            KC, b_KC = kcm.next()
            KS, b_KS = kst.next()
            for i2, idx in enumerate((FM_KCMP, FM_VCMP)):
                P, b_P = fm_chunk(idx, H, b_H)
                sc.op("act", lambda e, i2=i2, P=P, KC=KC: e.copy(out=KC[:, i2, :], in_=P[:, 0:TT]), reads=[b_P], writes=[b_KC])
            for i2, (idx, idxs) in enumerate(((FM_KSLC, FM_KSLCS), (FM_KWIN, FM_KWINS))):
                Pq, b_Pq = fm_chunk(idx, H, b_H)
                Ps, b_Ps = fm_chunk(idxs, H, b_H)
                rope(Pq, b_Pq, Ps, b_Ps, KS[:, i2, :], b_KS)
            sc.dma("sp", dr.kcmpT[:, :, t0:t0 + TT].rearrange("i p n -> p i n"), KC[:], b_KC, reads=[b_KC])
            sc.dma("sp", dr.kT[:, :, t0:t0 + TT].rearrange("i p n -> p i n"), KS[:], b_KS, reads=[b_KS])
            YC, b_YC = ycT.next()
            for cc in range(4):
                Pc, b_Pc = fm_chunk(FM_C + cc, H, b_H)
                Px, b_Px = fm_chunk(FM_X + cc, H, b_H)
                Pb, b_Pb = fm_chunk(FM_B + cc, H, b_H)
                XS, b_XS = xs.next()
                AC, b_AC = acc.next()
                sc.op("act", lambda e, Px=Px, XS=XS: e.copy(out=XS[:], in_=Px[:, 0:TT]), reads=[b_Px], writes=[b_XS])
                sc.op("dve", lambda e, cc=cc, Pc=Pc, XS=XS: e.tensor_tensor(out=hc[:, cc, 2:2 + TT], in0=Pc[:, 0:TT], in1=XS[:],
                                                                            op=ALU.mult),
                      reads=[b_Pc, b_XS], writes=[b_hc[cc]])
                sc.op("dve", lambda e, cc=cc, AC=AC: e.tensor_scalar(out=AC[:], in0=hc[:, cc, 2:2 + TT], scalar1=cw[:, cc, 2:3],
                                                                     scalar2=None, op0=ALU.mult),
                      reads=[b_hc[cc], b_set], writes=[b_AC])
                sc.op("dve", lambda e, cc=cc, AC=AC: e.scalar_tensor_tensor(
                    out=AC[:], in0=hc[:, cc, 1:1 + TT], scalar=cw[:, cc, 1:2], in1=AC[:], op0=ALU.mult, op1=ALU.add),
                    reads=[b_hc[cc], b_set, b_AC], writes=[b_AC])
                sc.op("dve", lambda e, cc=cc, AC=AC: e.scalar_tensor_tensor(
                    out=AC[:], in0=hc[:, cc, 0:TT], scalar=cw[:, cc, 0:1], in1=AC[:], op0=ALU.mult, op1=ALU.add),
                    reads=[b_hc[cc], b_set, b_AC], writes=[b_AC])
                sc.op("dve", lambda e, cc=cc, AC=AC, Pb=Pb, YC=YC: e.tensor_tensor(out=YC[:, cc, :], in0=Pb[:, 0:TT], in1=AC[:],
                                                                                   op=ALU.mult),
                      reads=[b_Pb, b_AC], writes=[b_YC])
                sc.op("pool", lambda e, cc=cc: e.tensor_copy(out=hc[:, cc, 0:2], in_=hc[:, cc, TT:TT + 2]),
                      reads=[b_hc[cc]], writes=[b_hc[cc]])
            sc.dma("sp", dr.ycT[:, t0:t0 + TT].rearrange("(c p) n -> p c n", p=128), YC[:], b_YC, reads=[b_YC])
        sc.barrier()


def phase_cmp(cx, l, dr):
    nc, sc = cx.nc, cx.sc
    with ExitStack() as es:
        w1s = _alloc(nc, es, "c_w1", [128, 2, 32, 128], BF16)
        w2s = _alloc(nc, es, "c_w2", [128, 2, 64], BF16)
        peT = _alloc(nc, es, "c_pe", [128, 2, 32], BF16)
        kin = _alloc(nc, es, "c_kin", [128, 2, S], BF16)
        hid = _alloc(nc, es, "c_hid", [128, 256], BF16)
        bcol = _alloc(nc, es, "c_bcol", [128, 1], F32)
        kst = _alloc(nc, es, "c_kst", [64, 256], BF16)
        vst = _alloc(nc, es, "c_vst", [128, 2, 64], BF16)
        pc = _palloc(nc, es, "c_pc", [128, 512], F32)
        ph = _palloc(nc, es, "c_ph", [128, 512], F32)
        pk = _palloc(nc, es, "c_pk", [128, 512], F32)
        b_w1, b_w2, b_pe, b_kin, b_hid, b_bcol, b_kst, b_vst = (Buf(n) for n in "w1 w2 pe kin hid bcol kst vst".split())
        b_pc, b_ph, b_pk = Buf("pc", True), Buf("ph", True), Buf("pk", True)
        for kv in range(2):
            for half in range(2):
                sc.dma("pool", w1s[half * 64:(half + 1) * 64, kv, :, :],
                       dr.cmp_w1[l, kv].rearrange("(lp d) h -> d lp h", d=64), b_w1, writes=[b_w1])
                sc.dma("pool", peT[half * 64:(half + 1) * 64, kv, :], dr.cmp_peT[l, kv], b_pe, writes=[b_pe])
            sc.dma("pool", w2s[:, kv, :], dr.cmp_w2[l, kv], b_w2, writes=[b_w2])
        sc.dma("sp", kin[:], dr.kcmpT.rearrange("i p n -> p i n"), b_kin, writes=[b_kin])
        sc.op("dve", lambda e: e.memset(hid[:], 0.0), writes=[b_hid])
        for kv in range(2):
            for g in range(2):
                ps_ = slice(g * 64, (g + 1) * 64)
                for lp in range(32):
                    sc.op("pe", lambda e, lp=lp: e.matmul(pc[:, 0:1], w1s[ps_, kv, lp, :], peT[ps_, kv, lp:lp + 1],
                                                          start=(lp == 0), stop=(lp == 31)),
                          reads=[b_w1, b_pe], writes=[b_pc])
                sc.op("dve", lambda e: e.tensor_copy(out=bcol[:], in_=pc[:, 0:1]), reads=[b_pc], writes=[b_bcol])
                kv3 = kin[:, kv, :].rearrange("p (n s) -> p n s", s=16)
                for lp in range(32):
                    rhs = kv3[ps_, 0:255, lp] if lp < 16 else kv3[ps_, 1:256, lp - 16]
                    sc.op("pe", lambda e, lp=lp, rhs=rhs: e.matmul(ph[:, 0:255], w1s[ps_, kv, lp, :], rhs,
                                                                   start=(lp == 0), stop=(lp == 31)),
                          reads=[b_w1, b_kin], writes=[b_ph])
                sc.op("act", lambda e: e.activation(out=hid[:, 0:255], in_=ph[:, 0:255], func=AF.Gelu_apprx_tanh,
                                                    bias=bcol[:], scale=1.0),
                      reads=[b_ph, b_bcol], writes=[b_hid])
                if kv == 0:
                    sc.op("pe", lambda e: e.matmul(pk[0:64, 0:256], w2s[:, 0, :], hid[:, 0:256], start=True, stop=True),
                          reads=[b_w2, b_hid], writes=[b_pk])
                    sc.op("dve", lambda e: e.tensor_copy(out=kst[:], in_=pk[0:64, 0:256]), reads=[b_pk], writes=[b_kst])
                    sc.dma("sp", dr.kcT[g], kst[:], b_kst, reads=[b_kst])
                else:
                    for nt in range(2):
                        sc.op("pe", lambda e, nt=nt: e.matmul(pk[:, nt * 64:(nt + 1) * 64], hid[:, nt * 128:(nt + 1) * 128],
                                                              w2s[:, 1, :], start=True, stop=True),
                              reads=[b_w2, b_hid], writes=[b_pk])
                    sc.op("dve", lambda e: e.tensor_copy(out=vst[:], in_=pk[:, 0:128].rearrange("p (a d) -> p a d", a=2)),
                          reads=[b_pk], writes=[b_vst])
                    sc.dma("sp", dr.vc[g].rearrange("(a p) d -> p a d", p=128), vst[:], b_vst, reads=[b_vst])
        sc.barrier()


def phase_att(cx, l, dr):
    nc, sc = cx.nc, cx.sc
    NQ = int(os.environ.get("ATT_NQ", S // 128))
    SCALE = 0.125
    NEG = -30000.0
    with ExitStack() as es:
        KE = _alloc(nc, es, "a_KE", [128, S], BF16)
        KW = _alloc(nc, es, "a_KW", [64, S], BF16)
        VS = _alloc(nc, es, "a_VS", [128, 32, 65], BF16)
        VW = _alloc(nc, es, "a_VW", [128, 32, 65], BF16)
        KC = _alloc(nc, es, "a_KC", [64, 256], BF16)
        VC = _alloc(nc, es, "a_VC", [128, 2, 65], BF16)
        SM = _alloc(nc, es, "a_SM", [128, 2, 64], BF16)
        QM = Ring([_alloc(nc, es, "a_QM%d" % k, [128, 4, 128], BF16) for k in range(2)], "QM")
        QN = Ring([_alloc(nc, es, "a_QN%d" % k, [64, 4, 128], BF16) for k in range(2)], "QN")
        NG = Ring([_alloc(nc, es, "a_NG%d" % k, [128, 12], F32) for k in range(2)], "NG")
        CM = Ring([_alloc(nc, es, "a_CM%d" % k, [128, 2, 128], F32) for k in range(2)], "CM")
        CK = Ring([_alloc(nc, es, "a_CK%d" % k, [128, 2, 64], F32) for k in range(2)], "CK")
        PT = Ring([_alloc(nc, es, "a_PT%d" % k, [128, 4, 128], BF16) for k in range(4)], "PT")
        negm = _alloc(nc, es, "a_negm", [128, 128], BF16)
        ybb = _alloc(nc, es, "a_ybb", [128, 256], BF16)
        ybf = _alloc(nc, es, "a_ybf", [128, 256], F32)
        YT = Ring([_alloc(nc, es, "a_YT%d" % k, [128, 2, 128], BF16) for k in range(2)], "YT")
        sm_ = _alloc(nc, es, "a_small", [128, 256], F32)
        identb = _alloc(nc, es, "a_identb", [128, 128], BF16)
        rsum = sm_[:, 0:12].rearrange("p (h r) -> p h r", r=3)
        coef = sm_[:, 12:24].rearrange("p (h r) -> p h r", r=3)
        imp = sm_[:, 32:96]
        score = sm_[:, 96:160]
        sc2 = sm_[:, 160:224]
        m1 = sm_[:, 224:232]
        m2 = sm_[:, 232:240]
        pS = Ring([_palloc(nc, es, "a_pS%d" % k, [128, 512], F32) for k in range(2)], "pS", True)
        pOC = _palloc(nc, es, "a_pOC", [128, 4, 128], F32)
        pU = _palloc(nc, es, "a_pU", [128, 4, 128], F32)
        pOS = _palloc(nc, es, "a_pOS", [128, 4, 128], F32)
        pOW = _palloc(nc, es, "a_pOW", [128, 4, 128], F32)
        pT = _palloc(nc, es, "a_pT", [128, 1024], BF16)
        b_pOC, b_pU, b_pOS, b_pOW, b_pT = (Buf(n, True) for n in "pOC pU pOS pOW pT".split())
        b_KE, b_KW, b_VS, b_VW, b_KC, b_VC, b_SM, b_E = (Buf(n) for n in "KE KW VS VW KC VC SM E".split())
        b_negm, b_ybb, b_ybf, b_sm, b_id = (Buf(n) for n in "negm ybb ybf sm idb".split())

        for q4 in range(4):
            sc.dma("pool", KE[64:128, q4 * 1024:(q4 + 1) * 1024], dr.Esel[:, q4 * 1024:(q4 + 1) * 1024], b_E, writes=[b_E])
        sc.dma("pool", SM[:], dr.slcmap.rearrange("(a p) j -> p a j", p=128), b_SM, writes=[b_SM])
        sc.op("dve", lambda e: e.tensor_copy(out=identb[:], in_=cx.ident[:]), reads=[cx.b_const], writes=[b_id])
        sc.op("dve", lambda e: e.memset(negm[:], 0.0), writes=[b_negm])

        for g in range(2):
            gs = slice(g * 64, (g + 1) * 64)
            sc.dma("sp", KE[0:64, :], dr.kT[0, gs, :], b_KE, writes=[b_KE])
            sc.dma("sp", KW[:], dr.kT[1, gs, :], b_KW, writes=[b_KW])
            sc.dma("sp", VS[:, :, 0:64], dr.v[:, g * 64:(g + 1) * 64].rearrange("(a p) d -> p a d", p=128), b_VS, writes=[b_VS])
            sc.dma("sp", VW[:, :, 0:64], dr.v[:, 128 + g * 64:128 + (g + 1) * 64].rearrange("(a p) d -> p a d", p=128),
                   b_VW, writes=[b_VW])
            sc.dma("sp", KC[:], dr.kcT[g], b_KC, writes=[b_KC])
            sc.dma("sp", VC[:, :, 0:64], dr.vc[g].rearrange("(a p) d -> p a d", p=128), b_VC, writes=[b_VC])
            sc.op("dve", lambda e: e.memset(VS[:, :, 64:65], 1.0), writes=[b_VS])
            sc.op("dve", lambda e: e.memset(VW[:, :, 64:65], 1.0), writes=[b_VW])
            sc.op("dve", lambda e: e.memset(VC[:, :, 64:65], 1.0), writes=[b_VC])

            def load_q(qt):
                t0 = qt * 128
                Q, b_Q = QM.next()
                Qn, b_Qn = QN.next()
                G_, b_G = NG.next()
                C_, b_C = CM.next()
                K_, b_K = CK.next()
                sc.dma("sp", Q[0:64, :, :], dr.qrT[g * 256:(g + 1) * 256, t0:t0 + 128].rearrange("(h d) n -> d h n", d=64),
                       b_Q, writes=[b_Q])
                sc.dma("sp", Qn[:], dr.qnT[g * 256:(g + 1) * 256, t0:t0 + 128].rearrange("(h d) n -> d h n", d=64),
                       b_Qn, writes=[b_Qn])
                sc.dma("sp", G_[:], dr.ng[t0:t0 + 128, g * 12:(g + 1) * 12], b_G, writes=[b_G])
                sc.dma("sp", C_[:], dr.cmpmask[:, t0:t0 + 128].rearrange("(a p) q -> p a q", p=128), b_C, writes=[b_C])
                sc.dma("sp", K_[:, 0, :], dr.cmask[t0:t0 + 128, :], b_K, writes=[b_K])
                sc.dma("sp", K_[:, 1, :], dr.cbias[t0:t0 + 128, :], b_K, writes=[b_K])
                return (Q, b_Q, Qn, b_Qn, G_, b_G, C_, b_C, K_, b_K)

            def pv(Pt, b_Pt, dst, b_dst, V, b_V, kt, first, ncol=65):
                for h in range(4):
                    sc.op("pe", lambda e, h=h: e.matmul(dst[:, h, 0:ncol], Pt[:, h, :], V[:, kt, 0:ncol],
                                                        start=(first and h == 0), stop=False, skip_group_check=True),
                          reads=[b_Pt, b_V], writes=[b_dst])

            nxt = load_q(0)
            for qt in range(NQ):
                (Q, b_Q, Qn, b_Qn, G_, b_G, C_, b_C, K_, b_K) = nxt
                if qt + 1 < NQ:
                    nxt = load_q(qt + 1)
                t0 = qt * 128
                nnt = 1 if (t0 + 127) < (16 * 128 + 31) else 2
                for nt in range(nnt):
                    P_, b_P = pS.next()
                    sc.op("pe", lambda e: e.matmul(P_[:, :], KC[:, nt * 128:(nt + 1) * 128], Qn[:].rearrange("p h n -> p (h n)"),
                                                   start=True, stop=True), reads=[b_KC, b_Qn], writes=[b_P])
                    Pt, b_Pt = PT.next()
                    sc.op("act", lambda e: e.activation(out=Pt[:].rearrange("p h n -> p (h n)"), in_=P_[:, :], func=AF.Exp,
                                                        scale=SCALE), reads=[b_P], writes=[b_Pt])
                    sc.op("dve", lambda e: e.tensor_tensor(out=Pt[:], in0=Pt[:],
                                                           in1=C_[:, nt, :].unsqueeze(1).to_broadcast([128, 4, 128]),
                                                           op=ALU.mult), reads=[b_Pt, b_C], writes=[b_Pt])
                    pv(Pt, b_Pt, pOC, b_pOC, VC, b_VC, nt, nt == 0)
                    pv(Pt, b_Pt, pU, b_pU, SM, b_SM, nt, nt == 0, ncol=64)
                sc.op("dve", lambda e: e.tensor_scalar(out=rsum[:, :, 0], in0=pOC[:, :, 64], scalar1=1e-30, scalar2=None,
                                                       op0=ALU.max), reads=[b_pOC], writes=[b_sm])
                sc.op("dve", lambda e: e.reciprocal(out=rsum[:, :, 0], in_=rsum[:, :, 0]), reads=[b_sm], writes=[b_sm])
                sc.op("dve", lambda e: e.tensor_scalar(out=imp, in0=pU[:, 0, 0:64], scalar1=rsum[:, 0, 0:1], scalar2=None,
                                                       op0=ALU.mult), reads=[b_pU, b_sm], writes=[b_sm])
                for h in range(1, 4):
                    sc.op("dve", lambda e, h=h: e.scalar_tensor_tensor(out=imp, in0=pU[:, h, 0:64], scalar=rsum[:, h, 0:1],
                                                                       in1=imp, op0=ALU.mult, op1=ALU.add),
                          reads=[b_pU, b_sm], writes=[b_sm])
                sc.op("dve", lambda e: e.tensor_tensor(out=score, in0=imp, in1=K_[:, 0, :], op=ALU.mult),
                      reads=[b_sm, b_K], writes=[b_sm])
                sc.op("dve", lambda e: e.tensor_tensor(out=score, in0=score, in1=K_[:, 1, :], op=ALU.add),
                      reads=[b_sm, b_K], writes=[b_sm])
                sc.op("dve", lambda e: e.max(out=m1, in_=score), reads=[b_sm], writes=[b_sm])
                sc.op("dve", lambda e: e.match_replace(out=sc2, in_to_replace=m1, in_values=score, imm_value=-1e9),
                      reads=[b_sm], writes=[b_sm])
                sc.op("dve", lambda e: e.max(out=m2, in_=sc2), reads=[b_sm], writes=[b_sm])
                sc.op("dve", lambda e: e.tensor_scalar(out=negm[:, 64:128], in0=score, scalar1=m2[:, 7:8], scalar2=NEG,
                                                       op0=ALU.is_lt, op1=ALU.mult), reads=[b_sm], writes=[b_negm])
                sc.op("pe", lambda e: e.transpose(pT[:, 0:128], negm[:], identb[:]), reads=[b_negm, b_id], writes=[b_pT])
                for h in range(4):
                    if h % 2 == 0:
                        sc.op("act", lambda e, h=h: e.copy(out=Q[64:128, h, :], in_=pT[64:128, 0:128]),
                              reads=[b_pT], writes=[b_Q])
                    else:
                        sc.op("dve", lambda e, h=h: e.tensor_copy(out=Q[64:128, h, :], in_=pT[64:128, 0:128]),
                              reads=[b_pT], writes=[b_Q])
                Qf = Q[:].rearrange("p h n -> p (h n)")
                for kt in range(qt + 1):
                    P_, b_P = pS.next()
                    sc.op("pe", lambda e: e.matmul(P_[:, :], KE[:, kt * 128:(kt + 1) * 128], Qf, start=True, stop=True),
                          reads=[b_KE, b_E, b_Q], writes=[b_P])
                    Pt, b_Pt = PT.next()
                    sc.op("act", lambda e: e.activation(out=Pt[:].rearrange("p h n -> p (h n)"), in_=P_[:, :], func=AF.Exp,
                                                        scale=SCALE), reads=[b_P], writes=[b_Pt])
                    if kt == qt:
                        sc.op("dve", lambda e: e.tensor_tensor(out=Pt[:], in0=Pt[:],
                                                               in1=cx.triL[:].unsqueeze(1).to_broadcast([128, 4, 128]),
                                                               op=ALU.mult), reads=[b_Pt, cx.b_const], writes=[b_Pt])
                    pv(Pt, b_Pt, pOS, b_pOS, VS, b_VS, kt, kt == 0)
                k0 = max(0, qt - 4)
                for kt in range(k0, qt + 1):
                    P_, b_P = pS.next()
                    sc.op("pe", lambda e: e.matmul(P_[:, :], KW[:, kt * 128:(kt + 1) * 128], Q[0:64, :, :].rearrange("p h n -> p (h n)"),
                                                   start=True, stop=True), reads=[b_KW, b_Q], writes=[b_P])
                    Pt, b_Pt = PT.next()
                    sc.op("act", lambda e: e.activation(out=Pt[:].rearrange("p h n -> p (h n)"), in_=P_[:, :], func=AF.Exp,
                                                        scale=SCALE), reads=[b_P], writes=[b_Pt])
                    if kt == qt or kt == qt - 4:
                        msk = cx.triL if kt == qt else cx.triU
                        sc.op("dve", lambda e: e.tensor_tensor(out=Pt[:], in0=Pt[:],
                                                               in1=msk[:].unsqueeze(1).to_broadcast([128, 4, 128]),
                                                               op=ALU.mult), reads=[b_Pt, cx.b_const], writes=[b_Pt])
                    pv(Pt, b_Pt, pOW, b_pOW, VW, b_VW, kt, kt == k0)
                sc.op("dve", lambda e: e.reciprocal(out=rsum[:, :, 1], in_=pOS[:, :, 64]), reads=[b_pOS], writes=[b_sm])
                sc.op("dve", lambda e: e.reciprocal(out=rsum[:, :, 2], in_=pOW[:, :, 64]), reads=[b_pOW], writes=[b_sm])
                sc.op("dve", lambda e: e.tensor_tensor(out=coef, in0=rsum, in1=G_[:].rearrange("p (h r) -> p h r", r=3),
                                                       op=ALU.mult), reads=[b_sm, b_G], writes=[b_sm])
                for h in range(4):
                    hs = slice(h * 64, (h + 1) * 64)
                    sc.op("dve", lambda e, h=h, hs=hs: e.tensor_scalar(out=ybf[:, hs], in0=pOC[:, h, 0:64], scalar1=coef[:, h, 0:1],
                                                                       scalar2=None, op0=ALU.mult),
                          reads=[b_pOC, b_sm], writes=[b_ybf])
                    sc.op("dve", lambda e, h=h, hs=hs: e.scalar_tensor_tensor(out=ybf[:, hs], in0=pOS[:, h, 0:64], scalar=coef[:, h, 1:2],
                                                                              in1=ybf[:, hs], op0=ALU.mult, op1=ALU.add),
                          reads=[b_pOS, b_sm, b_ybf], writes=[b_ybf])
                    sc.op("dve", lambda e, h=h, hs=hs: e.scalar_tensor_tensor(out=ybb[:, hs], in0=pOW[:, h, 0:64], scalar=coef[:, h, 2:3],
                                                                              in1=ybf[:, hs], op0=ALU.mult, op1=ALU.add),
                          reads=[b_pOW, b_sm, b_ybf], writes=[b_ybb])
                Y_, b_Y = YT.next()
                for c2 in range(2):
                    sc.op("pe", lambda e, c2=c2: e.transpose(pT[:, 256 + c2 * 128:256 + (c2 + 1) * 128], ybb[:, c2 * 128:(c2 + 1) * 128],
                                                             identb[:]), reads=[b_ybb, b_id], writes=[b_pT])
                sc.op("act", lambda e: e.copy(out=Y_[:], in_=pT[:, 256:512].rearrange("p (c n) -> p c n", c=2)),
                      reads=[b_pT], writes=[b_Y])
                sc.dma("sp", dr.ybT[g * 256:(g + 1) * 256, t0:t0 + 128].rearrange("(c p) n -> p c n", p=128), Y_[:], b_Y, reads=[b_Y])
        sc.barrier()


def phase_merge(cx, l, xT_in, xT_out, dr):
    nc, sc = cx.nc, cx.sc
    TT = 256
    NT = int(os.environ.get("MERGE_NT", S // TT))
    C = cx.modC[l][1]
    with ExitStack() as es:
        wg = _alloc(nc, es, "m_wg", [128, NCH, 3 * D], BF16)
        wb = _alloc(nc, es, "m_wb", [128, 3, 4, D], BF16)
        wo = _alloc(nc, es, "m_wo", [128, NCH, D], BF16)
        xt = Ring([_alloc(nc, es, "m_x%d" % k, [128, NCH, TT], F32) for k in range(2)], "x")
        hT = Ring([_alloc(nc, es, "m_h%d" % k, [128, NCH, TT], BF16) for k in range(2)], "h")
        ys = Ring([_alloc(nc, es, "m_ys%d" % k, [128, 3, 4, TT], BF16) for k in range(2)], "ys")
        mg = _alloc(nc, es, "m_mg", [128, NCH, TT], BF16)
        yT = _alloc(nc, es, "m_y", [128, NCH, TT], F32)
        sq = _alloc(nc, es, "m_sq", [128, NCH, TT], BF16)
        rstd = _alloc(nc, es, "m_rstd", [128, TT], F32)
        lnt = _alloc(nc, es, "m_lnt", [128, TT], F32)
        sg = Ring([_alloc(nc, es, "m_sg%d" % k, [128, TT], F32) for k in range(3)], "sg")
        ac = Ring([_alloc(nc, es, "m_ac%d" % k, [128, TT], F32) for k in range(2)], "ac")
        t2 = Ring([_alloc(nc, es, "m_t2%d" % k, [128, TT], F32) for k in range(2)], "t2")
        pG = Ring([_palloc(nc, es, "m_pG%d" % k, [128, 512], F32) for k in range(2)], "pG", True)
        pB = Ring([_palloc(nc, es, "m_pB%d" % k, [128, 512], F32) for k in range(2)], "pB", True)
        pY = Ring([_palloc(nc, es, "m_pY%d" % k, [128, 512], F32) for k in range(2)], "pY", True)
        pS = _palloc(nc, es, "m_pS", [128, 512], F32)
        b_pS = Buf("pS", True)
        b_wg = [Buf("wg") for _ in range(NCH)]
        b_wb = [Buf("wb") for _ in range(3)]
        b_wo = [Buf("wo") for _ in range(NCH)]
        b_mg = [Buf("mg") for _ in range(NCH)]
        b_yc = [Buf("yc") for _ in range(NCH)]
        b_sq, b_rstd = Buf("sq"), Buf("rstd")
        for kc in range(NCH):
            for cc in range(2):
                sc.dma("pool", wg[:, kc, cc * 1536:(cc + 1) * 1536], dr.w_gate[l, kc * 128:(kc + 1) * 128, cc * 1536:(cc + 1) * 1536],
                       b_wg[kc], writes=[b_wg[kc]])
        for n in range(3):
            for k4 in range(4):
                sc.dma("pool", wb[:, n, k4, :], dr.w_branch[l, n, k4 * 128:(k4 + 1) * 128, :], b_wb[n], writes=[b_wb[n]])
        for kc in range(NCH):
            sc.dma("pool", wo[:, kc, :], dr.w_out[l, kc * 128:(kc + 1) * 128, :], b_wo[kc], writes=[b_wo[kc]])
        ysrc = (dr.yaT, dr.ybT, dr.ycT)

        def load(t):
            t0 = t * TT
            X, b_X = xt.next()
            H, b_H = hT.next()
            Y, b_Y = ys.next()
            sc.dma("sp", X[:], xT_in[:, t0:t0 + TT].rearrange("(c p) n -> p c n", p=128), b_X, writes=[b_X])
            sc.dma("sp", H[:], dr.hT[:, t0:t0 + TT].rearrange("(c p) n -> p c n", p=128), b_H, writes=[b_H])
            for n in range(3):
                sc.dma("sp", Y[:, n, :, :], ysrc[n][:, t0:t0 + TT].rearrange("(c p) n -> p c n", p=128), b_Y, writes=[b_Y])
            return X, b_X, H, b_H, Y, b_Y

        nxt = load(0)
        for t in range(NT):
            X, b_X, H, b_H, Y, b_Y = nxt
            if t + 1 < NT:
                nxt = load(t + 1)
            t0 = t * TT
            for oc in range(NCH):
                AC, b_AC = ac.next()
                for n in range(3):
                    G_, b_G = pG.next()
                    for kc in range(NCH):
                        sc.op("pe", lambda e, kc=kc: e.matmul(G_[:, 0:TT], wg[:, kc, n * D + oc * 128:n * D + (oc + 1) * 128], H[:, kc, :],
                                                              start=(kc == 0), stop=(kc == NCH - 1)),
                              reads=[b_wg[kc], b_H], writes=[b_G])
                    B_, b_B = pB.next()
                    for k4 in range(4):
                        sc.op("pe", lambda e, k4=k4: e.matmul(B_[:, 0:TT], wb[:, n, k4, oc * 128:(oc + 1) * 128], Y[:, n, k4, :],
                                                              start=(k4 == 0), stop=(k4 == 3)),
                              reads=[b_wb[n], b_Y], writes=[b_B])
                    SG, b_SG = sg.next()
                    sc.op("act", lambda e: e.activation(out=SG[:], in_=G_[:, 0:TT], func=AF.Sigmoid), reads=[b_G], writes=[b_SG])
                    if n == 0:
                        sc.op("dve", lambda e: e.tensor_tensor(out=AC[:], in0=B_[:, 0:TT], in1=SG[:], op=ALU.mult),
                              reads=[b_B, b_SG], writes=[b_AC])
                    else:
                        T2, b_T2 = t2.next()
                        sc.op("dve", lambda e: e.tensor_tensor(out=T2[:], in0=B_[:, 0:TT], in1=SG[:], op=ALU.mult),
                              reads=[b_B, b_SG], writes=[b_T2])
                        if n == 1:
                            sc.op("pool", lambda e: e.tensor_tensor(out=AC[:], in0=AC[:], in1=T2[:], op=ALU.add),
                                  reads=[b_AC, b_T2], writes=[b_AC])
                        else:
                            sc.op("pool", lambda e: e.tensor_tensor(out=mg[:, oc, :], in0=AC[:], in1=T2[:], op=ALU.add),
                                  reads=[b_AC, b_T2], writes=[b_mg[oc]])
            for oc2 in range(NCH):
                Y_, b_Yp = pY.next()
                for oc in range(NCH):
                    sc.op("pe", lambda e, oc=oc: e.matmul(Y_[:, 0:TT], wo[:, oc, oc2 * 128:(oc2 + 1) * 128], mg[:, oc, :],
                                                          start=(oc == 0), stop=(oc == NCH - 1)),
                          reads=[b_wo[oc], b_mg[oc]], writes=[b_Yp])
                sc.op("dve", lambda e: e.tensor_copy(out=yT[:, oc2, :], in_=Y_[:, 0:TT]), reads=[b_Yp], writes=[b_yc[oc2]])
                sc.op("act", lambda e: e.activation(out=sq[:, oc2, :], in_=yT[:, oc2, :], func=AF.Square),
                      reads=[b_yc[oc2]], writes=[b_sq])
            emit_rstd(cx, sq, b_sq, pS, b_pS, rstd, b_rstd, lnt, TT)
            for ch in range(NCH):
                T2, b_T2 = t2.next()
                sc.op("dve", lambda e: e.scalar_tensor_tensor(out=T2[:], in0=yT[:, ch, :], scalar=C[:, ch:ch + 1], in1=rstd[:],
                                                              op0=ALU.mult, op1=ALU.mult),
                      reads=[b_yc[ch], b_rstd, cx.b_mod], writes=[b_T2])
                sc.op("pool", lambda e: e.tensor_tensor(out=X[:, ch, :], in0=X[:, ch, :], in1=T2[:], op=ALU.add),
                      reads=[b_X, b_T2], writes=[b_X])
            sc.dma("sp", xT_out[:, t0:t0 + TT].rearrange("(c p) n -> p c n", p=128), X[:], b_X, reads=[b_X])
        sc.barrier()


def host_constants():
    ct = {}
    ct["ident_in"] = np.eye(128, dtype=np.float32)
    k = np.arange(128)
    ct["triL_in"] = (k[:, None] <= k[None, :]).astype(np.float32)
    ct["triU_in"] = (k[:, None] > k[None, :]).astype(np.float32)
    pos = np.arange(S, dtype=np.float32)
    inv = (1.0 / (np.float32(500000.0) ** (np.arange(0, 16, 2, dtype=np.float32) / np.float32(16)))).astype(np.float32)
    ang = pos[:, None] * inv[None, :]
    cos, sin = np.cos(ang).astype(np.float32), np.sin(ang).astype(np.float32)
    C = np.ones((64, S), np.float32)
    Sn = np.zeros((64, S), np.float32)
    C[0:8] = cos.T
    C[8:16] = cos.T
    Sn[0:8] = -sin.T
    Sn[8:16] = sin.T
    ct["ropeC"] = np.ascontiguousarray(np.concatenate([C, C], 0))
    ct["ropeS"] = np.ascontiguousarray(np.concatenate([Sn, Sn], 0))
    ct["Esel"] = (np.arange(64)[:, None] == (np.arange(S)[None, :] // 64)).astype(np.float32)
    n = np.arange(256)
    cm = ((n[:, None] * 16 + 31) <= np.arange(S)[None, :]) & (n[:, None] < 255)
    ct["cmpmask"] = cm.astype(np.float32)
    ncb = S // 16 - 1
    cs = np.arange(ncb)[:, None] * 16
    ss = np.arange(64)[None, :] * 64
    ov = np.clip(np.minimum(cs + 32, ss + 64) - np.maximum(cs, ss), 0, None)
    sm = np.zeros((256, 64), np.float32)
    sm[:ncb] = ov / 32.0
    ct["slcmap"] = sm
    t = np.arange(S)
    cur = t // 64
    blk = np.arange(64)
    forced = (blk[None, :] == 0) | (blk[None, :] == cur[:, None]) | (blk[None, :] == cur[:, None] - 1)
    causal = blk[None, :] <= cur[:, None]
    ct["cmask"] = (causal & ~forced).astype(np.float32)
    ct["cbias"] = np.where(forced, 1e4, np.where(causal, 0.0, -1.0)).astype(np.float32)
    return ct


def relayout_w_in(w_in):
    perm = np.arange(64)
    perm[0:8] = np.arange(8, 16)
    perm[8:16] = np.arange(0, 8)
    cols = []
    cols += list(range(0, 512))
    cols += list(range(OFF_Q, OFF_Q + 512))
    cols += [OFF_Q + h * 64 + perm[d] for h in range(8) for d in range(64)]
    kv = lambda i: OFF_KV + i * 128
    cols += list(range(kv(0), kv(0) + 128))
    cols += list(range(kv(1), kv(1) + 128))
    cols += list(range(kv(2), kv(2) + 128))
    cols += [kv(2) + g * 64 + perm[d] for g in range(2) for d in range(64)]
    cols += list(range(kv(4), kv(4) + 128))
    cols += [kv(4) + g * 64 + perm[d] for g in range(2) for d in range(64)]
    cols += list(range(OFF_C + 512, OFF_C + 1024))
    cols += list(range(OFF_C + 1024, OFF_C + 1536))
    cols += list(range(OFF_C, OFF_C + 512))
    cols += list(range(512, 1024))
    cols += list(range(kv(3), kv(3) + 128))
    cols += list(range(kv(5), kv(5) + 128))
    cols += list(range(OFF_NG, OFF_NG + 24))
    cols = np.asarray(cols)
    assert cols.size == WP_COLS
    return np.ascontiguousarray(w_in[:, :, cols])


class DR:
    pass


def build_program(upto="all", debug=False):
    nc = bass.Bass("TRN2", target_bir_lowering=False)
    cx = Ctx()
    cx.nc = nc
    es = ExitStack()
    cx.es = es
    sc = Sched(nc, es)
    cx.sc = sc
    dr = DR()

    def din(name, shape):
        return nc.dram_tensor(name, list(shape), F32, kind="ExternalInput").ap()

    kind_i = "ExternalOutput" if debug else "Internal"

    def dscr(name, shape, dt=BF16):
        return nc.dram_tensor(name, list(shape), dt, kind=kind_i).ap()

    x = din("x", [S, D])
    c = din("c", [1, D])
    mod_w = din("mod_w", [L, D, 9 * D])
    mod_b = din("mod_b", [L, 9 * D])
    norm_g = din("norm_g", [L, 6, D])
    ffn_w13 = din("ffn_w13", [L, 2, D, 2 * DFF])
    ffn_w2 = din("ffn_w2", [L, 2, DFF, D])
    dr.wp = din("wp", [L, D, WP_COLS])
    dr.gm_ln_g = din("gm_ln_g", [L, 512])
    dr.gm_ln_b = din("gm_ln_b", [L, 512])
    dr.gm_ws = din("gm_ws", [L, 4, 128, 128])
    dr.gm_bs = din("gm_bs", [L, 4, 128])
    dr.cmp_peT = din("cmp_peT", [L, 2, 64, 32])
    dr.cmp_w1 = din("cmp_w1", [L, 2, 2048, 128])
    dr.cmp_w2 = din("cmp_w2", [L, 2, 128, 64])
    dr.conv_w = din("conv_w", [L, 3, 512])
    dr.w_branch = din("w_branch", [L, 3, 512, D])
    dr.w_gate = din("w_gate", [L, D, 3 * D])
    dr.w_out = din("w_out", [L, D, D])
    ident_d = din("ident_in", [128, 128])
    triL_d = din("triL_in", [128, 128])
    triU_d = din("triU_in", [128, 128])
    dr.ropeC = din("ropeC", [128, S])
    dr.ropeS = din("ropeS", [128, S])
    dr.Esel = din("Esel", [64, S])
    dr.cmpmask = din("cmpmask", [256, S])
    dr.slcmap = din("slcmap", [256, 64])
    dr.cmask = din("cmask", [S, 64])
    dr.cbias = din("cbias", [S, 64])
    out = nc.dram_tensor("out", [S, D], F32, kind="ExternalOutput").ap()
    xT = [nc.dram_tensor("xT%d" % i, [D, S], F32, kind=kind_i).ap() for i in range(2)]
    dr.hT = dscr("hT_d", [D, S])
    dr.yaT = dscr("yaT_d", [512, S])
    dr.ybT = dscr("ybT_d", [512, S])
    dr.ycT = dscr("ycT_d", [512, S])
    dr.qrT = dscr("qrT_d", [512, S])
    dr.qnT = dscr("qnT_d", [512, S])
    dr.kT = dscr("kT_d", [2, 128, S])
    dr.kcmpT = dscr("kcmpT_d", [2, 128, S])
    dr.v = dscr("v_d", [S, 256])
    dr.ng = dscr("ng_d", [S, 24], F32)
    dr.kcT = dscr("kcT_d", [2, 64, 256])
    dr.vc = dscr("vc_d", [2, 256, 64])
    cx.b_x = Buf("x")
    cx.b_out = Buf("out")
    cx.b_xT = [Buf("xT") for _ in range(S // 512)]
    bx = [Buf("xTd") for _ in range(S // 512)]

    cx.ident = _alloc(nc, es, "ident", [128, 128], F32)
    cx.triL = _alloc(nc, es, "triL", [128, 128], F32)
    cx.triU = _alloc(nc, es, "triU", [128, 128], F32)
    cx.ones_bf = _alloc(nc, es, "ones_bf", [128, 128], BF16)
    cx.eps_col = _alloc(nc, es, "eps_col", [128, 1], F32)
    cx.b_const = Buf("const")
    cx.b_mod = Buf("mod")
    cx.modT = [_alloc(nc, es, "modT%d" % l, [128, 72], F32) for l in range(L)]
    cx.modbT = [_alloc(nc, es, "modbT%d" % l, [128, 72], F32) for l in range(L)]
    cx.normgT = [_alloc(nc, es, "normgT%d" % l, [128, 48], F32) for l in range(L)]
    cx.modA = [[_alloc(nc, es, "modA%d_%d" % (l, s_), [128, 8], F32) for s_ in range(3)] for l in range(L)]
    cx.modB = [[_alloc(nc, es, "modB%d_%d" % (l, s_), [128, 8], F32) for s_ in range(3)] for l in range(L)]
    cx.modC = [[_alloc(nc, es, "modC%d_%d" % (l, s_), [128, 8], F32) for s_ in range(3)] for l in range(L)]
    sc.dma("sp", cx.ident[:], ident_d, cx.b_const, writes=[cx.b_const])
    sc.dma("sp", cx.triL[:], triL_d, cx.b_const, writes=[cx.b_const])
    sc.dma("sp", cx.triU[:], triU_d, cx.b_const, writes=[cx.b_const])
    sc.op("dve", lambda e: e.memset(cx.ones_bf[:], 1.0), writes=[cx.b_const])
    sc.op("dve", lambda e: e.memset(cx.eps_col[:], EPS), writes=[cx.b_const])

    stages = upto.split(",")

    def want(nm):
        return upto == "all" or nm in stages

    phase_transpose_in(cx, x, xT[0])
    phase_mod(cx, c, mod_w, mod_b, norm_g)
    if debug:
        dbgm = nc.dram_tensor("dbg_mod", [128, 72 + 24], F32, kind="ExternalOutput").ap()
        bd = Buf("dbg")
        sc.dma("sp", dbgm[:, 0:72], cx.modT[0][:], bd, reads=[cx.b_mod])
        for s_ in range(3):
            sc.dma("sp", dbgm[:, 72 + s_ * 8:80 + s_ * 8], cx.modA[0][s_][:], bd, reads=[cx.b_mod])
    cur = 0
    for l in range(L):
        if want("ffn%d0" % l):
            phase_ffn(cx, l, 0, ffn_w13[l, 0], ffn_w2[l, 0], xT[cur], bx, xT[1 - cur], bx)
            cur = 1 - cur
        if want("proj%d" % l):
            phase_proj(cx, l, xT[cur], dr)
        if want("cmp%d" % l):
            phase_cmp(cx, l, dr)
        if want("att%d" % l):
            phase_att(cx, l, dr)
        if want("merge%d" % l):
            phase_merge(cx, l, xT[cur], xT[1 - cur], dr)
            cur = 1 - cur
        if want("ffn%d1" % l):
            last = (l == L - 1)
            phase_ffn(cx, l, 1, ffn_w13[l, 1], ffn_w2[l, 1], xT[cur], bx, xT[1 - cur], bx, out_tok=(out if last else None))
            cur = 1 - cur
    sc.finish()
    es.close()
    return nc


def make_in_maps(inputs, cores):
    ct = host_constants()
    wp = relayout_w_in(np.asarray(inputs["w_in"], np.float32))
    peT = np.ascontiguousarray(np.transpose(np.asarray(inputs["cmp_pe"], np.float32), (0, 1, 3, 2)))
    shared = {k: np.ascontiguousarray(np.asarray(inputs[k], np.float32)) for k in
              ("mod_w", "mod_b", "norm_g", "ffn_w13", "ffn_w2", "gm_ln_g", "gm_ln_b", "gm_ws", "gm_bs",
               "cmp_w1", "cmp_w2", "conv_w", "w_branch", "w_gate", "w_out")}
    shared["wp"] = wp
    shared["cmp_peT"] = peT
    shared.update(ct)
    maps = []
    for b in cores:
        m = dict(shared)
        m["x"] = np.ascontiguousarray(np.asarray(inputs["x"][b], np.float32))
        m["c"] = np.ascontiguousarray(np.asarray(inputs["c"][b:b + 1], np.float32))
        maps.append(m)
    return maps


def kernel(**inputs):
    nc = build_program("all", debug=False)
    maps = make_in_maps(inputs, list(range(8)))
    res = run_bass_kernel_spmd(nc, maps, core_ids=list(range(8)))
    return np.stack([np.asarray(r["out"], np.float32) for r in res.results], 0)
```

```python
import os
import numpy as np
from contextlib import ExitStack
import concourse.bass as bass
import concourse.mybir as mybir
from concourse.bass_utils import run_bass_kernel_spmd

F32 = mybir.dt.float32
BF16 = mybir.dt.bfloat16
AF = mybir.ActivationFunctionType
ALU = mybir.AluOpType
AX = mybir.AxisListType

S = 4096
D = 1024
DFF = 2816
L = 2
NCH = D // 128
NJ = DFF // 128
EPS = 1e-6
BW = 512
A_COLS = 1024
Q_COLS = 512
KV_COLS = 768
NG_COLS = 24
C_COLS = 1536
IN_COLS = A_COLS + Q_COLS + KV_COLS + NG_COLS + C_COLS
OFF_Q = A_COLS
OFF_KV = OFF_Q + Q_COLS
OFF_NG = OFF_KV + KV_COLS
OFF_C = OFF_NG + NG_COLS


class Buf:
    __slots__ = ("name", "w", "r", "sem", "last_dma", "excl")

    def __init__(self, name, excl=False):
        self.name = name
        self.excl = excl
        self.w = None
        self.r = []
        self.sem = None
        self.last_dma = None


class Sched:
    ENGS = ("pe", "act", "dve", "pool")

    def __init__(self, nc, es):
        self.nc = nc
        self.es = es
        self.eng = {"pe": nc.tensor, "act": nc.scalar, "dve": nc.vector, "pool": nc.gpsimd, "sp": nc.sync}
        self.cnt = {e: 0 for e in self.ENGS}
        self.esem = {e: es.enter_context(nc.semaphore("sem_" + e)) for e in self.ENGS}
        self.known = {e: {} for e in self.eng}
        self.free_sems = []
        self.all_dsems = []
        self.live_bufs = []
        self.nsem = 0

    def _waits(self, eng, toks):
        need = {}
        for t in toks:
            cur = need.get(t[0])
            if cur is None or cur[1] < t[2]:
                need[t[0]] = (t[1], t[2])
        kn = self.known[eng]
        e = self.eng[eng]
        for key, (sem, val) in need.items():
            if kn.get(key, 0) >= val:
                continue
            kn[key] = val
            e.wait_ge(sem, val)

    def _deps(self, eng, reads, writes):
        toks = []
        for b in reads:
            if b.w is not None and not (eng == "pe" and b.w[3] == "pe"):
                toks.append(b.w)
        for b in writes:
            if b.w is not None and b.w[3] != eng:
                toks.append(b.w)
            for r in b.r:
                if r[3] != eng:
                    toks.append(r)
        return toks

    def _update(self, tok, reads, writes):
        for b in reads:
            if b not in writes:
                b.r.append(tok)
        for b in writes:
            b.w = tok
            b.r = []

    def op(self, eng, fn, reads=(), writes=()):
        if any(b.excl for b in reads):
            writes = list(writes) + [b for b in reads if b.excl]
            reads = [b for b in reads if not b.excl]
        self._waits(eng, self._deps(eng, reads, writes))
        ins = fn(self.eng[eng])
        self.cnt[eng] += 1
        ins.then_inc(self.esem[eng], 1)
        tok = (eng, self.esem[eng], self.cnt[eng], eng)
        self._update(tok, reads, writes)
        return tok

    def _get_sem(self, b):
        if b.sem is None:
            if self.free_sems:
                b.sem = self.free_sems.pop()
            else:
                self.nsem += 1
                s = self.es.enter_context(self.nc.semaphore("dsem%d" % self.nsem))
                b.sem = [s, 0]
                self.all_dsems.append(b.sem)
            self.live_bufs.append(b)
        return b.sem

    def dma(self, queue, out, in_, sbuf, reads=(), writes=(), **kw):
        toks = self._deps("dma", reads, writes)
        if sbuf.last_dma is not None:
            toks.append(sbuf.last_dma)
        self._waits(queue, toks)
        sm = self._get_sem(sbuf)
        ins = self.eng[queue].dma_start(out=out, in_=in_, **kw)
        sm[1] += 16
        ins.then_inc(sm[0], 16)
        tok = (id(sm), sm[0], sm[1], "dma")
        sbuf.last_dma = tok
        self._update(tok, reads, writes)
        return tok

    def barrier(self):
        toks = [(e, self.esem[e], self.cnt[e], e) for e in self.ENGS if self.cnt[e] > 0]
        for sm in self.all_dsems:
            if sm[1] > 0:
                toks.append((id(sm), sm[0], sm[1], "dma"))
        for e in self.eng:
            self._waits(e, [t for t in toks if t[0] != e])
        for b in self.live_bufs:
            if b.sem is not None:
                self.free_sems.append(b.sem)
                b.sem = None
        self.live_bufs = []

    def finish(self):
        toks = []
        for sm in self.all_dsems:
            if sm[1] > 0:
                toks.append((id(sm), sm[0], sm[1], "dma"))
        toks += [(e, self.esem[e], self.cnt[e], e) for e in self.ENGS if self.cnt[e] > 0]
        self._waits("sp", toks)


class Ctx:
    pass


_UID = [0]


def _alloc(nc, es, name, shape, dt):
    _UID[0] += 1
    return es.enter_context(nc.sbuf_tensor("%s_%d" % (name, _UID[0]), list(shape), dt))


def _palloc(nc, es, name, shape, dt):
    _UID[0] += 1
    return es.enter_context(nc.psum_tensor("%s_%d" % (name, _UID[0]), list(shape), dt))


def phase_transpose_in(cx, x_ap, xT_ap):
    nc, sc = cx.nc, cx.sc
    with ExitStack() as es:
        NB = 2
        xin = [_alloc(nc, es, "ti_x%d" % i, [128, 4, D], F32) for i in range(NB)]
        xo = [_alloc(nc, es, "ti_o%d" % i, [128, NCH, 512], F32) for i in range(NB)]
        ps = [_palloc(nc, es, "ti_p%d" % i, [128, 512], F32) for i in range(4)]
        b_in = [Buf("ti_x") for _ in range(NB)]
        b_o = [Buf("ti_o") for _ in range(NB)]
        b_ps = [Buf("ti_p", True) for _ in range(4)]
        pi = 0
        for t in range(S // 512):
            k = t % NB
            src = x_ap[t * 512:(t + 1) * 512, :].rearrange("(a p) f -> p a f", p=128)
            sc.dma("sp", xin[k][:], src, b_in[k], reads=[cx.b_x], writes=[b_in[k]])
            for ch in range(NCH):
                pb = pi % 4
                pi += 1
                for a in range(4):
                    sc.op("pe", lambda e, a=a, ch=ch, pb=pb, k=k: e.transpose(
                        ps[pb][:, a * 128:(a + 1) * 128], xin[k][:, a, ch * 128:(ch + 1) * 128], cx.ident[:]),
                        reads=[b_in[k], cx.b_const], writes=[b_ps[pb]])
                if ch % 2 == 0:
                    sc.op("dve", lambda e, ch=ch, pb=pb, k=k: e.tensor_copy(out=xo[k][:, ch, :], in_=ps[pb][:]),
                          reads=[b_ps[pb]], writes=[b_o[k]])
                else:
                    sc.op("act", lambda e, ch=ch, pb=pb, k=k: e.copy(out=xo[k][:, ch, :], in_=ps[pb][:]),
                          reads=[b_ps[pb]], writes=[b_o[k]])
            dst = xT_ap[:, t * 512:(t + 1) * 512].rearrange("(c p) n -> p c n", p=128)
            sc.dma("sp", dst, xo[k][:], b_o[k], reads=[b_o[k]], writes=[cx.b_xT[t]])
        sc.barrier()


def phase_mod(cx, c_ap, mod_w_ap, mod_b_ap, norm_g_ap):
    nc, sc = cx.nc, cx.sc
    with ExitStack() as es:
        crow = _alloc(nc, es, "md_crow", [8, 128], F32)
        cT = _alloc(nc, es, "md_cT", [128, 8], F32)
        sg = _alloc(nc, es, "md_sg", [128, 8], F32)
        brow = _alloc(nc, es, "md_brow", [72, 128], F32)
        grow = _alloc(nc, es, "md_grow", [48, 128], F32)
        wbuf = [_alloc(nc, es, "md_w%d" % i, [128, NCH, 1024], F32) for i in range(2)]
        pt = _palloc(nc, es, "md_pt", [128, 512], F32)
        pm = _palloc(nc, es, "md_pm", [128, 512], F32)
        b_crow, b_cT, b_brow, b_grow = (Buf(n) for n in ("crow", "cT", "brow", "grow"))
        b_pt, b_pm = Buf("pt", True), Buf("pm", True)
        b_w = [Buf("md_w") for _ in range(2)]
        b_mod = cx.b_mod
        sc.dma("sp", crow[:], c_ap.rearrange("o (a p) -> (o a) p", p=128), b_crow, writes=[b_crow])
        sc.op("pe", lambda e: e.transpose(pt[:, 0:8], crow[:], cx.ident[0:8, 0:8]),
              reads=[b_crow, cx.b_const], writes=[b_pt])
        sc.op("act", lambda e: e.activation(out=sg[:], in_=pt[:, 0:8], func=AF.Sigmoid), reads=[b_pt], writes=[b_cT])
        sc.op("dve", lambda e: e.tensor_tensor(out=cT[:], in0=pt[:, 0:8], in1=sg[:], op=ALU.mult),
              reads=[b_pt, b_cT], writes=[b_cT])
        for l in range(L):
            sc.dma("sp", brow[:], mod_b_ap[l].rearrange("(a p) -> a p", p=128), b_brow, writes=[b_brow])
            sc.dma("sp", grow[:], norm_g_ap[l].rearrange("k (a p) -> (k a) p", p=128), b_grow, writes=[b_grow])
            sc.op("pe", lambda e: e.transpose(pt[:, 0:72], brow[:], cx.ident[0:72, 0:72]),
                  reads=[b_brow, cx.b_const], writes=[b_pt])
            sc.op("dve", lambda e, l=l: e.tensor_copy(out=cx.modbT[l][:], in_=pt[:, 0:72]), reads=[b_pt], writes=[b_mod])
            sc.op("pe", lambda e: e.transpose(pt[:, 0:48], grow[:], cx.ident[0:48, 0:48]),
                  reads=[b_grow, cx.b_const], writes=[b_pt])
            sc.op("dve", lambda e, l=l: e.tensor_copy(out=cx.normgT[l][:], in_=pt[:, 0:48]), reads=[b_pt], writes=[b_mod])
            for v in range(9):
                k = v % 2
                src = mod_w_ap[l][:, v * 1024:(v + 1) * 1024].rearrange("(a p) n -> p a n", p=128)
                sc.dma("sp", wbuf[k][:], src, b_w[k], writes=[b_w[k]])
                for ch in range(NCH):
                    col = v * 8 + ch
                    for kc in range(NCH):
                        sc.op("pe", lambda e, k=k, ch=ch, kc=kc, col=col: e.matmul(
                            pm[:, col:col + 1], wbuf[k][:, kc, ch * 128:(ch + 1) * 128], cT[:, kc:kc + 1],
                            start=(kc == 0), stop=(kc == NCH - 1)),
                            reads=[b_w[k], b_cT], writes=[b_pm])
            sc.op("dve", lambda e, l=l: e.tensor_tensor(out=cx.modT[l][:], in0=pm[:, 0:72], in1=cx.modbT[l][:], op=ALU.add),
                  reads=[b_pm, b_mod], writes=[b_mod])
            for s_ in range(3):
                g0 = cx.normgT[l][:, (2 * s_) * 8:(2 * s_ + 1) * 8]
                g1 = cx.normgT[l][:, (2 * s_ + 1) * 8:(2 * s_ + 2) * 8]
                shift = cx.modT[l][:, (3 * s_) * 8:(3 * s_ + 1) * 8]
                scale = cx.modT[l][:, (3 * s_ + 1) * 8:(3 * s_ + 2) * 8]
                gate = cx.modT[l][:, (3 * s_ + 2) * 8:(3 * s_ + 3) * 8]
                resw = 1.0 if s_ == 1 else 0.5
                sc.op("dve", lambda e, l=l, s_=s_, scale=scale, g0=g0: e.scalar_tensor_tensor(
                    out=cx.modA[l][s_][:], in0=scale, scalar=1.0, in1=g0, op0=ALU.add, op1=ALU.mult),
                    reads=[b_mod], writes=[b_mod])
                sc.op("dve", lambda e, l=l, s_=s_, shift=shift: e.tensor_copy(out=cx.modB[l][s_][:], in_=shift),
                      reads=[b_mod], writes=[b_mod])
                sc.op("dve", lambda e, l=l, s_=s_, gate=gate, g1=g1, resw=resw: e.scalar_tensor_tensor(
                    out=cx.modC[l][s_][:], in0=gate, scalar=resw, in1=g1, op0=ALU.mult, op1=ALU.mult),
                    reads=[b_mod], writes=[b_mod])
        sc.barrier()


def emit_rstd(cx, sq, b_sq, ps, b_ps, rstd, b_rstd, lnt, TT):
    sc = cx.sc
    for ch in range(NCH):
        sc.op("pe", lambda e, ch=ch: e.matmul(ps[:, 0:TT], cx.ones_bf[:], sq[:, ch, :],
                                              start=(ch == 0), stop=(ch == NCH - 1)),
              reads=[b_sq, cx.b_const], writes=[b_ps])
    sc.op("act", lambda e: e.activation(out=lnt[:, 0:TT], in_=ps[:, 0:TT], func=AF.Ln, scale=1.0 / D, bias=cx.eps_col[:]),
          reads=[b_ps, cx.b_const], writes=[b_rstd])
    sc.op("act", lambda e: e.activation(out=rstd[:, 0:TT], in_=lnt[:, 0:TT], func=AF.Exp, scale=-0.5),
          reads=[b_rstd], writes=[b_rstd])


def phase_ffn(cx, l, i, w13_ap, w2_ap, xT_in, b_xin, xT_out, b_xout, out_tok=None):
    nc, sc = cx.nc, cx.sc
    TT = 256
    import os
    NT = int(os.environ.get('FFN_NT', S // TT))
    s_ = 0 if i == 0 else 2
    A, B, C = cx.modA[l][s_], cx.modB[l][s_], cx.modC[l][s_]
    with ExitStack() as es:
        w13s = _alloc(nc, es, "f_w13", [128, NCH, 2 * DFF], BF16)
        w2s = _alloc(nc, es, "f_w2", [128, NJ, D], BF16)
        xt = [_alloc(nc, es, "f_x%d" % k, [128, NCH, TT], F32) for k in range(2)]
        hT = _alloc(nc, es, "f_h", [128, NCH, TT], BF16)
        hT2 = _alloc(nc, es, "f_h2", [128, NCH, TT], BF16)
        hid = _alloc(nc, es, "f_hid", [128, NJ, TT], BF16)
        yT = _alloc(nc, es, "f_y", [128, NCH, TT], F32)
        sq = _alloc(nc, es, "f_sq", [128, NCH, TT], BF16)
        rstd = _alloc(nc, es, "f_rstd", [128, TT], F32)
        lnt = _alloc(nc, es, "f_lnt", [128, TT], F32)
        tmp = [_alloc(nc, es, "f_tmp%d" % k, [128, TT], F32) for k in range(2)]
        sgt = [_alloc(nc, es, "f_sg%d" % k, [128, TT], F32) for k in range(2)]
        if out_tok is not None:
            otok = [_alloc(nc, es, "f_ot%d" % k, [128, D], F32) for k in range(2)]
            b_otok = [Buf("otok") for _ in range(2)]
        pG = [_palloc(nc, es, "f_pg%d" % k, [128, 512], F32) for k in range(2)]
        pU = [_palloc(nc, es, "f_pu%d" % k, [128, 512], F32) for k in range(2)]
        pY = [_palloc(nc, es, "f_py%d" % k, [128, 512], F32) for k in range(2)]
        pS = _palloc(nc, es, "f_ps", [128, 512], F32)
        pT = _palloc(nc, es, "f_pt", [128, 512], F32)
        b_w13 = [Buf("w13") for _ in range(NCH)]
        b_w2 = [Buf("w2") for _ in range(NJ)]
        b_x = [Buf("x") for _ in range(2)]
        b_h, b_hid, b_y, b_sq, b_rstd = (Buf(n) for n in ("h", "hid", "y", "sq", "rstd"))
        b_pS, b_pT = Buf("pS", True), Buf("pT", True)
        b_hidj = [Buf("hidj") for _ in range(NJ)]
        b_yc = [Buf("yc") for _ in range(NCH)]
        b_tmp = [Buf("tmp") for _ in range(2)]
        b_sg = [Buf("sg") for _ in range(2)]
        b_pG = [Buf("pG", True) for _ in range(2)]
        b_pU = [Buf("pU", True) for _ in range(2)]
        b_pY = [Buf("pY", True) for _ in range(2)]

        CW = 1408
        for kc in range(NCH):
            for cc in range(2 * DFF // CW):
                sc.dma("pool", w13s[:, kc, cc * CW:(cc + 1) * CW],
                       w13_ap[kc * 128:(kc + 1) * 128, cc * CW:(cc + 1) * CW], b_w13[kc], writes=[b_w13[kc]])
        for j in range(NJ):
            sc.dma("pool", w2s[:, j, :], w2_ap[j * 128:(j + 1) * 128, :], b_w2[j], writes=[b_w2[j]])

        def load_x(t):
            k = t % 2
            src = xT_in[:, t * TT:(t + 1) * TT].rearrange("(c p) n -> p c n", p=128)
            sc.dma("sp", xt[k][:], src, b_x[k], writes=[b_x[k]])

        hTs = [hT, hT2]
        b_hs = [b_h, Buf("h2")]

        def pre(t):
            k = t % 2
            X = xt[k]
            H, bH = hTs[k], b_hs[k]
            for ch in range(NCH):
                sc.op("act", lambda e, ch=ch: e.activation(out=sq[:, ch, :], in_=X[:, ch, :], func=AF.Square),
                      reads=[b_x[k]], writes=[b_sq])
            emit_rstd(cx, sq, b_sq, pS, b_pS, rstd, b_rstd, lnt, TT)
            for ch in range(NCH):
                kk = ch % 2
                sc.op("dve", lambda e, ch=ch, kk=kk: e.scalar_tensor_tensor(
                    out=tmp[kk][:], in0=X[:, ch, :], scalar=A[:, ch:ch + 1], in1=rstd[:], op0=ALU.mult, op1=ALU.mult),
                    reads=[b_x[k], b_rstd, cx.b_mod], writes=[b_tmp[kk]])
                sc.op("act", lambda e, ch=ch, kk=kk: e.activation(
                    out=H[:, ch, :], in_=tmp[kk][:], func=AF.Identity, bias=B[:, ch:ch + 1], scale=1.0),
                    reads=[b_tmp[kk], cx.b_mod], writes=[bH])

        load_x(0)
        pre(0)
        for t in range(NT):
            k = t % 2
            if t + 1 < NT:
                load_x(t + 1)
            X = xt[k]
            H, bH = hTs[k], b_hs[k]
            for j in range(NJ):
                if j == NJ // 2 and t + 1 < NT:
                    pre(t + 1)
                kk = j % 2
                for kc in range(NCH):
                    sc.op("pe", lambda e, j=j, kc=kc, kk=kk: e.matmul(
                        pG[kk][:, 0:TT], w13s[:, kc, j * 128:(j + 1) * 128], H[:, kc, :],
                        start=(kc == 0), stop=(kc == NCH - 1)),
                        reads=[b_w13[kc], bH], writes=[b_pG[kk]])
                for kc in range(NCH):
                    sc.op("pe", lambda e, j=j, kc=kc, kk=kk: e.matmul(
                        pU[kk][:, 0:TT], w13s[:, kc, DFF + j * 128:DFF + (j + 1) * 128], H[:, kc, :],
                        start=(kc == 0), stop=(kc == NCH - 1)),
                        reads=[b_w13[kc], bH], writes=[b_pU[kk]])
                sc.op("act", lambda e, kk=kk: e.activation(out=sgt[kk][:], in_=pG[kk][:, 0:TT], func=AF.Silu),
                      reads=[b_pG[kk]], writes=[b_sg[kk]])
                sc.op("dve", lambda e, j=j, kk=kk: e.tensor_tensor(out=hid[:, j, :], in0=pU[kk][:, 0:TT], in1=sgt[kk][:],
                                                                   op=ALU.mult),
                      reads=[b_pU[kk], b_sg[kk]], writes=[b_hidj[j]])
            STG = 9
            for oc in range(NCH if STG >= 3 else 0):
                kk = oc % 2
                for j in range(NJ):
                    sc.op("pe", lambda e, j=j, oc=oc, kk=kk: e.matmul(
                        pY[kk][:, 0:TT], w2s[:, j, oc * 128:(oc + 1) * 128], hid[:, j, :],
                        start=(j == 0), stop=(j == NJ - 1)),
                        reads=[b_w2[j], b_hidj[j]], writes=[b_pY[kk]])
                sc.op("dve", lambda e, oc=oc, kk=kk: e.tensor_copy(out=yT[:, oc, :], in_=pY[kk][:, 0:TT]),
                      reads=[b_pY[kk]], writes=[b_yc[oc]])
                sc.op("act", lambda e, oc=oc: e.activation(out=sq[:, oc, :], in_=yT[:, oc, :], func=AF.Square),
                      reads=[b_yc[oc]], writes=[b_sq])
            if STG >= 4:
                emit_rstd(cx, sq, b_sq, pS, b_pS, rstd, b_rstd, lnt, TT)
            for ch in range(NCH if STG >= 4 else 0):
                kk = ch % 2
                sc.op("dve", lambda e, ch=ch, kk=kk: e.scalar_tensor_tensor(
                    out=tmp[kk][:], in0=yT[:, ch, :], scalar=C[:, ch:ch + 1], in1=rstd[:], op0=ALU.mult, op1=ALU.mult),
                    reads=[b_yc[ch], b_rstd, cx.b_mod], writes=[b_tmp[kk]])
                sc.op("pool", lambda e, ch=ch, X=X, kk=kk: e.tensor_tensor(out=X[:, ch, :], in0=X[:, ch, :], in1=tmp[kk][:],
                                                                           op=ALU.add),
                      reads=[b_x[k], b_tmp[kk]], writes=[b_x[k]])
            if out_tok is None:
                dst = xT_out[:, t * TT:(t + 1) * TT].rearrange("(c p) n -> p c n", p=128)
                sc.dma("sp", dst, X[:], b_x[k], reads=[b_x[k]])
            else:
                for a in range(TT // 128):
                    ko = (t * (TT // 128) + a) % 2
                    for half in range(2):
                        for c4 in range(4):
                            ch = half * 4 + c4
                            sc.op("pe", lambda e, a=a, ch=ch, c4=c4, X=X: e.transpose(
                                pT[:, c4 * 128:(c4 + 1) * 128], X[:, ch, a * 128:(a + 1) * 128], cx.ident[:]),
                                reads=[b_x[k], cx.b_const], writes=[b_pT])
                        if half == 0:
                            sc.op("dve", lambda e, ko=ko: e.tensor_copy(out=otok[ko][:, 0:512], in_=pT[:]),
                                  reads=[b_pT], writes=[b_otok[ko]])
                        else:
                            sc.op("act", lambda e, ko=ko: e.copy(out=otok[ko][:, 512:1024], in_=pT[:]),
                                  reads=[b_pT], writes=[b_otok[ko]])
                    r0 = t * TT + a * 128
                    sc.dma("sp", out_tok[r0:r0 + 128, :], otok[ko][:], b_otok[ko], reads=[b_otok[ko]], writes=[cx.b_out])
        sc.barrier()


class Ring:
    def __init__(self, tiles, name, excl=False):
        self.tiles = tiles
        self.bufs = [Buf(name, excl) for _ in tiles]
        self.i = 0

    def next(self):
        k = self.i % len(self.tiles)
        self.i += 1
        return self.tiles[k], self.bufs[k]


FM_U, FM_Q, FM_QS, FM_KCMP, FM_VCMP, FM_KSLC, FM_KSLCS, FM_KWIN, FM_KWINS, FM_C, FM_X, FM_B = 0, 4, 8, 12, 13, 14, 15, 16, 17, 18, 22, 26
N_FM = 30
TM_OFF = N_FM * 128
TM_W = 280
WP_COLS = TM_OFF + 512 + TM_W


def phase_proj(cx, l, xT_in, dr):
    nc, sc = cx.nc, cx.sc
    TT = 256
    NT = int(os.environ.get("PROJ_NT", S // TT))
    A, B = cx.modA[l][1], cx.modB[l][1]
    with ExitStack() as es:
        wp = _alloc(nc, es, "p_wp", [128, NCH, WP_COLS], BF16)
        xt = Ring([_alloc(nc, es, "p_x%d" % k, [128, NCH, TT], F32) for k in range(2)], "x")
        sq = _alloc(nc, es, "p_sq", [128, NCH, TT], BF16)
        rstd = _alloc(nc, es, "p_rstd", [128, TT], F32)
        lnt = _alloc(nc, es, "p_lnt", [128, TT], F32)
        tmp = Ring([_alloc(nc, es, "p_tmp%d" % k, [128, TT], F32) for k in range(3)], "tmp")
        tmq = Ring([_alloc(nc, es, "p_tmq%d" % k, [128, TT], F32) for k in range(2)], "tmq")
        hT = Ring([_alloc(nc, es, "p_h%d" % k, [128, NCH, TT], BF16) for k in range(2)], "h")
        uT = _alloc(nc, es, "p_u", [128, 4, TT], BF16)
        qr = Ring([_alloc(nc, es, "p_qr%d" % k, [128, 4, TT], BF16) for k in range(2)], "qr")
        qn = Ring([_alloc(nc, es, "p_qn%d" % k, [128, 4, TT], BF16) for k in range(2)], "qn")
        kst = Ring([_alloc(nc, es, "p_kst%d" % k, [128, 2, TT], BF16) for k in range(2)], "kst")
        kcm = Ring([_alloc(nc, es, "p_kcm%d" % k, [128, 2, TT], BF16) for k in range(2)], "kcm")
        yaT = Ring([_alloc(nc, es, "p_ya%d" % k, [128, 4, TT], BF16) for k in range(2)], "ya")
        ycT = Ring([_alloc(nc, es, "p_yc%d" % k, [128, 4, TT], BF16) for k in range(2)], "yc")
        vst = Ring([_alloc(nc, es, "p_vst%d" % k, [128, TT // 128, 256], BF16) for k in range(2)], "vst")
        ngst = Ring([_alloc(nc, es, "p_ng%d" % k, [128, TT // 128, 24], F32) for k in range(2)], "ng")
        rc = Ring([_alloc(nc, es, "p_rc%d" % k, [128, TT], F32) for k in range(2)], "rc")
        rs = Ring([_alloc(nc, es, "p_rs%d" % k, [128, TT], F32) for k in range(2)], "rs")
        hc = _alloc(nc, es, "p_hc", [128, 4, TT + 2], F32)
        xs = Ring([_alloc(nc, es, "p_xs%d" % k, [128, TT], F32) for k in range(2)], "xs")
        acc = Ring([_alloc(nc, es, "p_acc%d" % k, [128, TT], F32) for k in range(2)], "acc")
        vf = [_alloc(nc, es, "p_vf%d" % k, [128, 512], F32) for k in range(TT // 128)]
        vt = Ring([_alloc(nc, es, "p_vt%d" % k, [128, 512], F32) for k in range(2)], "vt")
        vn = Ring([_alloc(nc, es, "p_vn%d" % k, [128, 512], BF16) for k in range(2)], "vn")
        st6 = _alloc(nc, es, "p_st6", [128, TT // 128, 6], F32)
        mv = _alloc(nc, es, "p_mv", [128, TT // 128, 2], F32)
        lv = _alloc(nc, es, "p_lv", [128, TT // 128], F32)
        rv = _alloc(nc, es, "p_rv", [128, TT // 128], F32)
        wsT = _alloc(nc, es, "p_wsT", [128, 4, 128], BF16)
        wsr = _alloc(nc, es, "p_wsr", [128, 4, 128], F32)
        bsr = _alloc(nc, es, "p_bsr", [1, 512], F32)
        bsb = _alloc(nc, es, "p_bsb", [1, 512], BF16)
        onesr = _alloc(nc, es, "p_onesr", [1, 128], BF16)
        onesf = _alloc(nc, es, "p_onesf", [1, 128], F32)
        lrow = _alloc(nc, es, "p_lrow", [1, 1024], F32)
        lng = _alloc(nc, es, "p_lng", [128, 512], F32)
        lnb = _alloc(nc, es, "p_lnb", [128, 512], F32)
        cwr = _alloc(nc, es, "p_cwr", [3, 512], F32)
        cw = _alloc(nc, es, "p_cw", [128, 4, 3], F32)
        b_hc = [Buf("hc") for _ in range(4)]
        b_u = [Buf("u") for _ in range(4)]
        b_vf = [Buf("vf") for _ in range(TT // 128)]
        b_sq, b_rstd, b_st, b_set = Buf("sq"), Buf("rstd"), Buf("st"), Buf("set")
        pS = _palloc(nc, es, "p_pS", [128, 512], F32)
        b_pS = Buf("pS", True)
        pF = Ring([_palloc(nc, es, "p_pF%d" % k, [128, 512], F32) for k in range(4)], "pF", True)
        pV = _palloc(nc, es, "p_pV", [128, 512], F32)
        b_pV = Buf("pV", True)
        pW = _palloc(nc, es, "p_pW", [128, 512], F32)
        b_pW = Buf("pW", True)
        pG = _palloc(nc, es, "p_pG", [128, 512], F32)
        b_pG = Buf("pG", True)
        b_wp = [Buf("wp") for _ in range(NCH)]

        CW = 1544
        for kc in range(NCH):
            for cc in range(WP_COLS // CW):
                sc.dma("pool", wp[:, kc, cc * CW:(cc + 1) * CW],
                       dr.wp[l][kc * 128:(kc + 1) * 128, cc * CW:(cc + 1) * CW], b_wp[kc], writes=[b_wp[kc]])
        sc.dma("sp", wsr[:], dr.gm_ws[l].rearrange("g t s -> t g s"), b_set, writes=[b_set])
        sc.dma("sp", bsr[:], dr.gm_bs[l].rearrange("(o g) t -> o (g t)", o=1), b_set, writes=[b_set])
        sc.dma("sp", lrow[:, 0:512], dr.gm_ln_g[l].rearrange("(o n) -> o n", o=1), b_set, writes=[b_set])
        sc.dma("sp", lrow[:, 512:1024], dr.gm_ln_b[l].rearrange("(o n) -> o n", o=1), b_set, writes=[b_set])
        sc.dma("sp", cwr[:], dr.conv_w[l], b_set, writes=[b_set])
        sc.op("dve", lambda e: e.memset(onesr[:], 1.0), writes=[b_set])
        sc.op("dve", lambda e: e.memset(onesf[:], 1.0), writes=[b_set])
        sc.op("dve", lambda e: e.tensor_copy(out=bsb[:], in_=bsr[:]), reads=[b_set], writes=[b_set])
        sc.op("dve", lambda e: e.memset(hc[:, :, 0:2], 0.0), writes=b_hc)
        for g in range(4):
            sc.op("pe", lambda e, g=g: e.transpose(pG[:, 0:128], wsr[:, g, :], cx.ident[:]),
                  reads=[b_set, cx.b_const], writes=[b_pG])
            sc.op("dve", lambda e, g=g: e.tensor_tensor(out=wsT[:, g, :], in0=pG[:, 0:128], in1=cx.triL[:], op=ALU.mult),
                  reads=[b_pG, cx.b_const], writes=[b_set])
        for hh, dst in ((0, lng), (1, lnb)):
            sc.op("pe", lambda e, hh=hh: e.matmul(pG[:, 0:512], onesf[0:1, :], lrow[0:1, hh * 512:(hh + 1) * 512],
                                                  start=True, stop=True), reads=[b_set], writes=[b_pG])
            sc.op("dve", lambda e, dst=dst: e.tensor_copy(out=dst[:], in_=pG[:, 0:512]), reads=[b_pG], writes=[b_set])
        for cc in range(4):
            sc.op("pe", lambda e, cc=cc: e.transpose(pG[:, 0:3], cwr[:, cc * 128:(cc + 1) * 128], cx.ident[0:3, 0:3]),
                  reads=[b_set, cx.b_const], writes=[b_pG])
            sc.op("dve", lambda e, cc=cc: e.tensor_copy(out=cw[:, cc, :], in_=pG[:, 0:3]), reads=[b_pG], writes=[b_set])

        def fm_chunk(idx, H, b_H):
            P, b_P = pF.next()
            for kc in range(NCH):
                sc.op("pe", lambda e, kc=kc, P=P: e.matmul(P[:, 0:TT], wp[:, kc, idx * 128:(idx + 1) * 128], H[:, kc, :],
                                                          start=(kc == 0), stop=(kc == NCH - 1)),
                      reads=[b_wp[kc], b_H], writes=[b_P])
            return P, b_P

        def load_x(t):
            X, b_X = xt.next()
            src = xT_in[:, t * TT:(t + 1) * TT].rearrange("(c p) n -> p c n", p=128)
            sc.dma("sp", X[:], src, b_X, writes=[b_X])
            return X, b_X

        nxt = load_x(0)
        for t in range(NT):
            X, b_X = nxt
            if t + 1 < NT:
                nxt = load_x(t + 1)
            t0 = t * TT
            RC, b_RC = rc.next()
            RS, b_RS = rs.next()
            sc.dma("sp", RC[:], dr.ropeC[:, t0:t0 + TT], b_RC, writes=[b_RC])
            sc.dma("sp", RS[:], dr.ropeS[:, t0:t0 + TT], b_RS, writes=[b_RS])
            for ch in range(NCH):
                sc.op("act", lambda e, ch=ch, X=X: e.activation(out=sq[:, ch, :], in_=X[:, ch, :], func=AF.Square),
                      reads=[b_X], writes=[b_sq])
            emit_rstd(cx, sq, b_sq, pS, b_pS, rstd, b_rstd, lnt, TT)
            H, b_H = hT.next()
            for ch in range(NCH):
                T1, b_T1 = tmp.next()
                sc.op("dve", lambda e, ch=ch, X=X, T1=T1: e.scalar_tensor_tensor(
                    out=T1[:], in0=X[:, ch, :], scalar=A[:, ch:ch + 1], in1=rstd[:], op0=ALU.mult, op1=ALU.mult),
                    reads=[b_X, b_rstd, cx.b_mod], writes=[b_T1])
                sc.op("act", lambda e, ch=ch, T1=T1, H=H: e.activation(
                    out=H[:, ch, :], in_=T1[:], func=AF.Identity, bias=B[:, ch:ch + 1], scale=1.0),
                    reads=[b_T1, cx.b_mod], writes=[b_H])
            sc.dma("sp", dr.hT[:, t0:t0 + TT].rearrange("(c p) n -> p c n", p=128), H[:], b_H, reads=[b_H])
            for a in range(TT // 128):
                for kc in range(NCH):
                    sc.op("pe", lambda e, kc=kc, a=a, H=H: e.matmul(
                        pV[:, :], H[:, kc, a * 128:(a + 1) * 128], wp[:, kc, TM_OFF:TM_OFF + 512],
                        start=(kc == 0), stop=(kc == NCH - 1)), reads=[b_wp[kc], b_H], writes=[b_pV])
                sc.op("act", lambda e, a=a: e.activation(out=vf[a][:], in_=pV[:], func=AF.Gelu_apprx_tanh),
                      reads=[b_pV], writes=[b_vf[a]])
                sc.op("dve", lambda e, a=a: e.bn_stats(out=st6[:, a, :], in_=vf[a][:]), reads=[b_vf[a]], writes=[b_st])
                sc.op("dve", lambda e, a=a: e.bn_aggr(out=mv[:, a, :], in_=st6[:, a, :]), reads=[b_st], writes=[b_st])
            sc.op("act", lambda e: e.activation(out=lv[:], in_=mv[:, :, 1], func=AF.Ln, scale=1.0, bias=cx.eps_col[:]),
                  reads=[b_st, cx.b_const], writes=[b_st])
            sc.op("act", lambda e: e.activation(out=rv[:], in_=lv[:], func=AF.Exp, scale=-0.5), reads=[b_st], writes=[b_st])
            for cc in range(4):
                P, b_P = fm_chunk(FM_U + cc, H, b_H)
                sc.op("act", lambda e, cc=cc, P=P: e.activation(out=uT[:, cc, :], in_=P[:, 0:TT], func=AF.Gelu_apprx_tanh),
                      reads=[b_P], writes=[b_u[cc]])
            YA, b_YA = yaT.next()
            VS, b_VS = vst.next()
            NG, b_NG = ngst.next()
            for a in range(TT // 128):
                V1, b_V1 = vt.next()
                sc.op("dve", lambda e, a=a, V1=V1: e.tensor_scalar(
                    out=V1[:], in0=vf[a][:], scalar1=mv[:, a, 0:1], scalar2=rv[:, a:a + 1], op0=ALU.subtract, op1=ALU.mult),
                    reads=[b_vf[a], b_st], writes=[b_V1])
                sc.op("pool", lambda e, V1=V1: e.tensor_tensor(out=V1[:], in0=V1[:], in1=lng[:], op=ALU.mult),
                      reads=[b_V1, b_set], writes=[b_V1])
                VN, b_VN = vn.next()
                sc.op("pool", lambda e, V1=V1, VN=VN: e.tensor_tensor(out=VN[:], in0=V1[:], in1=lnb[:], op=ALU.add),
                      reads=[b_V1, b_set], writes=[b_VN])
                for g in range(4):
                    sc.op("pe", lambda e, g=g, VN=VN: e.matmul(pG[:, g * 128:(g + 1) * 128], VN[:, g * 128:(g + 1) * 128],
                                                               wsT[:, g, :], start=True, stop=False),
                          reads=[b_VN, b_set], writes=[b_pG])
                    sc.op("pe", lambda e, g=g: e.matmul(pG[:, g * 128:(g + 1) * 128], onesr[0:1, :],
                                                        bsb[0:1, g * 128:(g + 1) * 128], start=False, stop=True),
                          reads=[b_set], writes=[b_pG])
                sc.op("dve", lambda e, a=a, YA=YA: e.tensor_tensor(
                    out=YA[:, :, a * 128:(a + 1) * 128], in0=pG[:, 0:512].rearrange("p (g t) -> p g t", g=4),
                    in1=uT[:, :, a * 128:(a + 1) * 128], op=ALU.mult),
                    reads=[b_pG] + b_u, writes=[b_YA])
                for kc in range(NCH):
                    sc.op("pe", lambda e, kc=kc, a=a, H=H: e.matmul(
                        pW[:, 0:TM_W], H[:, kc, a * 128:(a + 1) * 128], wp[:, kc, TM_OFF + 512:TM_OFF + 512 + TM_W],
                        start=(kc == 0), stop=(kc == NCH - 1)), reads=[b_wp[kc], b_H], writes=[b_pW])
                sc.op("act", lambda e, a=a, VS=VS: e.copy(out=VS[:, a, :], in_=pW[:, 0:256]), reads=[b_pW], writes=[b_VS])
                sc.op("act", lambda e, a=a, NG=NG: e.activation(out=NG[:, a, :], in_=pW[:, 256:280], func=AF.Sigmoid),
                      reads=[b_pW], writes=[b_NG])
            sc.dma("sp", dr.yaT[:, t0:t0 + TT].rearrange("(c p) n -> p c n", p=128), YA[:], b_YA, reads=[b_YA])
            sc.dma("sp", dr.v[t0:t0 + TT, :].rearrange("(a p) c -> p a c", p=128), VS[:], b_VS, reads=[b_VS])
            sc.dma("sp", dr.ng[t0:t0 + TT, :].rearrange("(a p) c -> p a c", p=128), NG[:], b_NG, reads=[b_NG])
            QR, b_QR = qr.next()
            QN, b_QN = qn.next()

            def rope(Pq, b_Pq, Ps, b_Ps, dst, b_dst):
                T1, b_T1 = tmq.next()
                T2, b_T2 = tmp.next()
                sc.op("dve", lambda e: e.tensor_tensor(out=T1[:], in0=Ps[:, 0:TT], in1=RS[:], op=ALU.mult),
                      reads=[b_Ps, b_RS], writes=[b_T1])
                sc.op("dve", lambda e: e.tensor_tensor(out=T2[:], in0=Pq[:, 0:TT], in1=RC[:], op=ALU.mult),
                      reads=[b_Pq, b_RC], writes=[b_T2])
                sc.op("pool", lambda e: e.tensor_tensor(out=dst, in0=T1[:], in1=T2[:], op=ALU.add),
                      reads=[b_T1, b_T2], writes=[b_dst])

            for qc in range(4):
                Pq, b_Pq = fm_chunk(FM_Q + qc, H, b_H)
                Ps, b_Ps = fm_chunk(FM_QS + qc, H, b_H)
                sc.op("act", lambda e, qc=qc, Pq=Pq, QN=QN: e.copy(out=QN[:, qc, :], in_=Pq[:, 0:TT]),
                      reads=[b_Pq], writes=[b_QN])
                rope(Pq, b_Pq, Ps, b_Ps, QR[:, qc, :], b_QR)
            sc.dma("sp", dr.qrT[:, t0:t0 + TT].rearrange("(c p) n -> p c n", p=128), QR[:], b_QR, reads=[b_QR])
            sc.dma("sp", dr.qnT[:, t0:t0 + TT].rearrange("(c p) n -> p c n", p=128), QN[:], b_QN, reads=[b_QN])
            KC, b_KC = kcm.next()
            KS, b_KS = kst.next()
            for i2, idx in enumerate((FM_KCMP, FM_VCMP)):
                P, b_P = fm_chunk(idx, H, b_H)
                sc.op("act", lambda e, i2=i2, P=P, KC=KC: e.copy(out=KC[:, i2, :], in_=P[:, 0:TT]), reads=[b_P], writes=[b_KC])
            for i2, (idx, idxs) in enumerate(((FM_KSLC, FM_KSLCS), (FM_KWIN, FM_KWINS))):
                Pq, b_Pq = fm_chunk(idx, H, b_H)
                Ps, b_Ps = fm_chunk(idxs, H, b_H)
                rope(Pq, b_Pq, Ps, b_Ps, KS[:, i2, :], b_KS)
            sc.dma("sp", dr.kcmpT[:, :, t0:t0 + TT].rearrange("i p n -> p i n"), KC[:], b_KC, reads=[b_KC])
            sc.dma("sp", dr.kT[:, :, t0:t0 + TT].rearrange("i p n -> p i n"), KS[:], b_KS, reads=[b_KS])
            YC, b_YC = ycT.next()
            for cc in range(4):
                Pc, b_Pc = fm_chunk(FM_C + cc, H, b_H)
                Px, b_Px = fm_chunk(FM_X + cc, H, b_H)
                Pb, b_Pb = fm_chunk(FM_B + cc, H, b_H)
                XS, b_XS = xs.next()
                AC, b_AC = acc.next()
                sc.op("act", lambda e, Px=Px, XS=XS: e.copy(out=XS[:], in_=Px[:, 0:TT]), reads=[b_Px], writes=[b_XS])
                sc.op("dve", lambda e, cc=cc, Pc=Pc, XS=XS: e.tensor_tensor(out=hc[:, cc, 2:2 + TT], in0=Pc[:, 0:TT], in1=XS[:],
                                                                            op=ALU.mult),
                      reads=[b_Pc, b_XS], writes=[b_hc[cc]])
                sc.op("dve", lambda e, cc=cc, AC=AC: e.tensor_scalar(out=AC[:], in0=hc[:, cc, 2:2 + TT], scalar1=cw[:, cc, 2:3],
                                                                     scalar2=None, op0=ALU.mult),
                      reads=[b_hc[cc], b_set], writes=[b_AC])
                sc.op("dve", lambda e, cc=cc, AC=AC: e.scalar_tensor_tensor(
                    out=AC[:], in0=hc[:, cc, 1:1 + TT], scalar=cw[:, cc, 1:2], in1=AC[:], op0=ALU.mult, op1=ALU.add),
                    reads=[b_hc[cc], b_set, b_AC], writes=[b_AC])
                sc.op("dve", lambda e, cc=cc, AC=AC: e.scalar_tensor_tensor(
                    out=AC[:], in0=hc[:, cc, 0:TT], scalar=cw[:, cc, 0:1], in1=AC[:], op0=ALU.mult, op1=ALU.add),
                    reads=[b_hc[cc], b_set, b_AC], writes=[b_AC])
                sc.op("dve", lambda e, cc=cc, AC=AC, Pb=Pb, YC=YC: e.tensor_tensor(out=YC[:, cc, :], in0=Pb[:, 0:TT], in1=AC[:],
                                                                                   op=ALU.mult),
                      reads=[b_Pb, b_AC], writes=[b_YC])
                sc.op("pool", lambda e, cc=cc: e.tensor_copy(out=hc[:, cc, 0:2], in_=hc[:, cc, TT:TT + 2]),
                      reads=[b_hc[cc]], writes=[b_hc[cc]])
            sc.dma("sp", dr.ycT[:, t0:t0 + TT].rearrange("(c p) n -> p c n", p=128), YC[:], b_YC, reads=[b_YC])
        sc.barrier()


def phase_cmp(cx, l, dr):
    nc, sc = cx.nc, cx.sc
    with ExitStack() as es:
        w1s = _alloc(nc, es, "c_w1", [128, 2, 32, 128], BF16)
        w2s = _alloc(nc, es, "c_w2", [128, 2, 64], BF16)
        peT = _alloc(nc, es, "c_pe", [128, 2, 32], BF16)
        kin = _alloc(nc, es, "c_kin", [128, 2, S], BF16)
        hid = _alloc(nc, es, "c_hid", [128, 256], BF16)
        bcol = _alloc(nc, es, "c_bcol", [128, 1], F32)
        kst = _alloc(nc, es, "c_kst", [64, 256], BF16)
        vst = _alloc(nc, es, "c_vst", [128, 2, 64], BF16)
        pc = _palloc(nc, es, "c_pc", [128, 512], F32)
        ph = _palloc(nc, es, "c_ph", [128, 512], F32)
        pk = _palloc(nc, es, "c_pk", [128, 512], F32)
        b_w1, b_w2, b_pe, b_kin, b_hid, b_bcol, b_kst, b_vst = (Buf(n) for n in "w1 w2 pe kin hid bcol kst vst".split())
        b_pc, b_ph, b_pk = Buf("pc", True), Buf("ph", True), Buf("pk", True)
        for kv in range(2):
            for half in range(2):
                sc.dma("pool", w1s[half * 64:(half + 1) * 64, kv, :, :],
                       dr.cmp_w1[l, kv].rearrange("(lp d) h -> d lp h", d=64), b_w1, writes=[b_w1])
                sc.dma("pool", peT[half * 64:(half + 1) * 64, kv, :], dr.cmp_peT[l, kv], b_pe, writes=[b_pe])
            sc.dma("pool", w2s[:, kv, :], dr.cmp_w2[l, kv], b_w2, writes=[b_w2])
        sc.dma("sp", kin[:], dr.kcmpT.rearrange("i p n -> p i n"), b_kin, writes=[b_kin])
        sc.op("dve", lambda e: e.memset(hid[:], 0.0), writes=[b_hid])
        for kv in range(2):
            for g in range(2):
                ps_ = slice(g * 64, (g + 1) * 64)
                for lp in range(32):
                    sc.op("pe", lambda e, lp=lp: e.matmul(pc[:, 0:1], w1s[ps_, kv, lp, :], peT[ps_, kv, lp:lp + 1],
                                                          start=(lp == 0), stop=(lp == 31)),
                          reads=[b_w1, b_pe], writes=[b_pc])
                sc.op("dve", lambda e: e.tensor_copy(out=bcol[:], in_=pc[:, 0:1]), reads=[b_pc], writes=[b_bcol])
                kv3 = kin[:, kv, :].rearrange("p (n s) -> p n s", s=16)
                for lp in range(32):
                    rhs = kv3[ps_, 0:255, lp] if lp < 16 else kv3[ps_, 1:256, lp - 16]
                    sc.op("pe", lambda e, lp=lp, rhs=rhs: e.matmul(ph[:, 0:255], w1s[ps_, kv, lp, :], rhs,
                                                                   start=(lp == 0), stop=(lp == 31)),
                          reads=[b_w1, b_kin], writes=[b_ph])
                sc.op("act", lambda e: e.activation(out=hid[:, 0:255], in_=ph[:, 0:255], func=AF.Gelu_apprx_tanh,
                                                    bias=bcol[:], scale=1.0),
                      reads=[b_ph, b_bcol], writes=[b_hid])
                if kv == 0:
                    sc.op("pe", lambda e: e.matmul(pk[0:64, 0:256], w2s[:, 0, :], hid[:, 0:256], start=True, stop=True),
                          reads=[b_w2, b_hid], writes=[b_pk])
                    sc.op("dve", lambda e: e.tensor_copy(out=kst[:], in_=pk[0:64, 0:256]), reads=[b_pk], writes=[b_kst])
                    sc.dma("sp", dr.kcT[g], kst[:], b_kst, reads=[b_kst])
                else:
                    for nt in range(2):
                        sc.op("pe", lambda e, nt=nt: e.matmul(pk[:, nt * 64:(nt + 1) * 64], hid[:, nt * 128:(nt + 1) * 128],
                                                              w2s[:, 1, :], start=True, stop=True),
                              reads=[b_w2, b_hid], writes=[b_pk])
                    sc.op("dve", lambda e: e.tensor_copy(out=vst[:], in_=pk[:, 0:128].rearrange("p (a d) -> p a d", a=2)),
                          reads=[b_pk], writes=[b_vst])
                    sc.dma("sp", dr.vc[g].rearrange("(a p) d -> p a d", p=128), vst[:], b_vst, reads=[b_vst])
        sc.barrier()


def phase_att(cx, l, dr):
    nc, sc = cx.nc, cx.sc
    NQ = int(os.environ.get("ATT_NQ", S // 128))
    SCALE = 0.125
    NEG = -30000.0
    with ExitStack() as es:
        KE = _alloc(nc, es, "a_KE", [128, S], BF16)
        KW = _alloc(nc, es, "a_KW", [64, S], BF16)
        VS = _alloc(nc, es, "a_VS", [128, 32, 65], BF16)
        VW = _alloc(nc, es, "a_VW", [128, 32, 65], BF16)
        KC = _alloc(nc, es, "a_KC", [64, 256], BF16)
        VC = _alloc(nc, es, "a_VC", [128, 2, 65], BF16)
        SM = _alloc(nc, es, "a_SM", [128, 2, 64], BF16)
        QM = Ring([_alloc(nc, es, "a_QM%d" % k, [128, 4, 128], BF16) for k in range(2)], "QM")
        QN = Ring([_alloc(nc, es, "a_QN%d" % k, [64, 4, 128], BF16) for k in range(2)], "QN")
        NG = Ring([_alloc(nc, es, "a_NG%d" % k, [128, 12], F32) for k in range(2)], "NG")
        CM = Ring([_alloc(nc, es, "a_CM%d" % k, [128, 2, 128], F32) for k in range(2)], "CM")
        CK = Ring([_alloc(nc, es, "a_CK%d" % k, [128, 2, 64], F32) for k in range(2)], "CK")
        PT = Ring([_alloc(nc, es, "a_PT%d" % k, [128, 4, 128], BF16) for k in range(4)], "PT")
        negm = _alloc(nc, es, "a_negm", [128, 128], BF16)
        ybb = _alloc(nc, es, "a_ybb", [128, 256], BF16)
        ybf = _alloc(nc, es, "a_ybf", [128, 256], F32)
        YT = Ring([_alloc(nc, es, "a_YT%d" % k, [128, 2, 128], BF16) for k in range(2)], "YT")
        sm_ = _alloc(nc, es, "a_small", [128, 256], F32)
        identb = _alloc(nc, es, "a_identb", [128, 128], BF16)
        rsum = sm_[:, 0:12].rearrange("p (h r) -> p h r", r=3)
        cf_ = _alloc(nc, es, "a_coef", [128, 12], F32)
        coef = cf_[:, 0:12].rearrange("p (h r) -> p h r", r=3)
        imp = sm_[:, 32:96]
        score = sm_[:, 96:160]
        sc2 = sm_[:, 160:224]
        m1 = sm_[:, 224:232]
        m2 = sm_[:, 232:240]
        pS = Ring([_palloc(nc, es, "a_pS%d" % k, [128, 512], F32) for k in range(2)], "pS", True)
        pOC = _palloc(nc, es, "a_pOC", [128, 4, 128], F32)
        pU = _palloc(nc, es, "a_pU", [128, 4, 128], F32)
        pOS = _palloc(nc, es, "a_pOS", [128, 4, 128], F32)
        pOW = _palloc(nc, es, "a_pOW", [128, 4, 128], F32)
        pT = _palloc(nc, es, "a_pT", [128, 1024], BF16)
        b_pOC, b_pU, b_pOS, b_pOW, b_pT = (Buf(n, True) for n in "pOC pU pOS pOW pT".split())
        b_KE, b_KW, b_VS, b_VW, b_KC, b_VC, b_SM, b_E = (Buf(n) for n in "KE KW VS VW KC VC SM E".split())
        b_negm, b_ybb, b_ybf, b_sm, b_id, b_cf = (Buf(n) for n in "negm ybb ybf sm idb cf".split())

        for q4 in range(4):
            sc.dma("pool", KE[64:128, q4 * 1024:(q4 + 1) * 1024], dr.Esel[:, q4 * 1024:(q4 + 1) * 1024], b_E, writes=[b_E])
        sc.dma("pool", SM[:], dr.slcmap.rearrange("(a p) j -> p a j", p=128), b_SM, writes=[b_SM])
        sc.op("dve", lambda e: e.tensor_copy(out=identb[:], in_=cx.ident[:]), reads=[cx.b_const], writes=[b_id])
        sc.op("dve", lambda e: e.memset(negm[:], 0.0), writes=[b_negm])

        for g in range(2):
            gs = slice(g * 64, (g + 1) * 64)
            sc.dma("sp", KE[0:64, :], dr.kT[0, gs, :], b_KE, writes=[b_KE])
            sc.dma("sp", KW[:], dr.kT[1, gs, :], b_KW, writes=[b_KW])
            sc.dma("sp", VS[:, :, 0:64], dr.v[:, g * 64:(g + 1) * 64].rearrange("(a p) d -> p a d", p=128), b_VS, writes=[b_VS])
            sc.dma("sp", VW[:, :, 0:64], dr.v[:, 128 + g * 64:128 + (g + 1) * 64].rearrange("(a p) d -> p a d", p=128),
                   b_VW, writes=[b_VW])
            sc.dma("sp", KC[:], dr.kcT[g], b_KC, writes=[b_KC])
            sc.dma("sp", VC[:, :, 0:64], dr.vc[g].rearrange("(a p) d -> p a d", p=128), b_VC, writes=[b_VC])
            sc.op("dve", lambda e: e.memset(VS[:, :, 64:65], 1.0), writes=[b_VS])
            sc.op("dve", lambda e: e.memset(VW[:, :, 64:65], 1.0), writes=[b_VW])
            sc.op("dve", lambda e: e.memset(VC[:, :, 64:65], 1.0), writes=[b_VC])

            def load_q(qt):
                t0 = qt * 128
                Q, b_Q = QM.next()
                Qn, b_Qn = QN.next()
                G_, b_G = NG.next()
                C_, b_C = CM.next()
                K_, b_K = CK.next()
                sc.dma("sp", Q[0:64, :, :], dr.qrT[g * 256:(g + 1) * 256, t0:t0 + 128].rearrange("(h d) n -> d h n", d=64),
                       b_Q, writes=[b_Q])
                sc.dma("sp", Qn[:], dr.qnT[g * 256:(g + 1) * 256, t0:t0 + 128].rearrange("(h d) n -> d h n", d=64),
                       b_Qn, writes=[b_Qn])
                sc.dma("sp", G_[:], dr.ng[t0:t0 + 128, g * 12:(g + 1) * 12], b_G, writes=[b_G])
                sc.dma("sp", C_[:], dr.cmpmask[:, t0:t0 + 128].rearrange("(a p) q -> p a q", p=128), b_C, writes=[b_C])
                sc.dma("sp", K_[:, 0, :], dr.cmask[t0:t0 + 128, :], b_K, writes=[b_K])
                sc.dma("sp", K_[:, 1, :], dr.cbias[t0:t0 + 128, :], b_K, writes=[b_K])
                return (Q, b_Q, Qn, b_Qn, G_, b_G, C_, b_C, K_, b_K)

            def pv(Pt, b_Pt, dst, b_dst, V, b_V, kt, first, ncol=65):
                for h in range(4):
                    sc.op("pe", lambda e, h=h: e.matmul(dst[:, h, 0:ncol], Pt[:, h, :], V[:, kt, 0:ncol],
                                                        start=(first and h == 0), stop=False, skip_group_check=True),
                          reads=[b_Pt, b_V], writes=[b_dst])

            def branch(tiles, smat, mask_of, dst, b_dst, V, b_V, vidx_of, extra=None):
                n = len(tiles)
                pend = [None] * n

                def issue_s(i):
                    P_, b_P = pS.next()
                    smat(tiles[i], P_, b_P)
                    pend[i] = (P_, b_P)

                if n:
                    issue_s(0)
                for i in range(n):
                    if i + 1 < n:
                        issue_s(i + 1)
                    P_, b_P = pend[i]
                    Pt, b_Pt = PT.next()
                    sc.op("act", lambda e: e.activation(out=Pt[:].rearrange("p h n -> p (h n)"), in_=P_[:, :], func=AF.Exp,
                                                        scale=SCALE), reads=[b_P], writes=[b_Pt])
                    m = mask_of(tiles[i])
                    if m is not None:
                        msk, b_m = m
                        sc.op("dve", lambda e: e.tensor_tensor(out=Pt[:], in0=Pt[:], in1=msk.unsqueeze(1).to_broadcast([128, 4, 128]),
                                                               op=ALU.mult), reads=[b_Pt, b_m], writes=[b_Pt])
                    pv(Pt, b_Pt, dst, b_dst, V, b_V, vidx_of(tiles[i]), i == 0)
                    if extra is not None:
                        extra(Pt, b_Pt, tiles[i], i == 0)

            fin = [None]
            nxt = load_q(0)
            for qt in range(NQ):
                (Q, b_Q, Qn, b_Qn, G_, b_G, C_, b_C, K_, b_K) = nxt
                if qt + 1 < NQ:
                    nxt = load_q(qt + 1)
                t0 = qt * 128
                G3 = G_[:].rearrange("p (h r) -> p h r", r=3)
                Qf = Q[:].rearrange("p h n -> p (h n)")
                Qr = Q[0:64, :, :].rearrange("p h n -> p (h n)")
                Qnf = Qn[:].rearrange("p h n -> p (h n)")
                nnt = 1 if (t0 + 127) < (16 * 128 + 31) else 2
                branch(list(range(nnt)),
                       lambda nt, P_, b_P: sc.op("pe", lambda e: e.matmul(P_[:, :], KC[:, nt * 128:(nt + 1) * 128], Qnf,
                                                                          start=True, stop=True),
                                                 reads=[b_KC, b_Qn], writes=[b_P]),
                       lambda nt: (C_[:, nt, :], b_C), pOC, b_pOC, VC, b_VC, lambda nt: nt,
                       extra=lambda Pt, b_Pt, nt, first: pv(Pt, b_Pt, pU, b_pU, SM, b_SM, nt, first, ncol=64))
                if fin[0] is not None:
                    fin[0]()
                    fin[0] = None
                sc.op("dve", lambda e: e.tensor_scalar(out=rsum[:, :, 0], in0=pOC[:, :, 64], scalar1=1e-30, scalar2=None,
                                                       op0=ALU.max), reads=[b_pOC], writes=[b_sm])
                sc.op("dve", lambda e: e.reciprocal(out=rsum[:, :, 0], in_=rsum[:, :, 0]), reads=[b_sm], writes=[b_sm])
                sc.op("dve", lambda e: e.tensor_scalar(out=imp, in0=pU[:, 0, 0:64], scalar1=rsum[:, 0, 0:1], scalar2=None,
                                                       op0=ALU.mult), reads=[b_pU, b_sm], writes=[b_sm])
                for h in range(1, 4):
                    sc.op("dve", lambda e, h=h: e.scalar_tensor_tensor(out=imp, in0=pU[:, h, 0:64], scalar=rsum[:, h, 0:1],
                                                                       in1=imp, op0=ALU.mult, op1=ALU.add),
                          reads=[b_pU, b_sm], writes=[b_sm])
                sc.op("dve", lambda e: e.tensor_tensor(out=score, in0=imp, in1=K_[:, 0, :], op=ALU.mult),
                      reads=[b_sm, b_K], writes=[b_sm])
                sc.op("dve", lambda e: e.tensor_tensor(out=score, in0=score, in1=K_[:, 1, :], op=ALU.add),
                      reads=[b_sm, b_K], writes=[b_sm])
                sc.op("dve", lambda e: e.max(out=m1, in_=score), reads=[b_sm], writes=[b_sm])
                sc.op("dve", lambda e: e.match_replace(out=sc2, in_to_replace=m1, in_values=score, imm_value=-1e9),
                      reads=[b_sm], writes=[b_sm])
                sc.op("dve", lambda e: e.max(out=m2, in_=sc2), reads=[b_sm], writes=[b_sm])
                sc.op("dve", lambda e: e.tensor_scalar(out=negm[:, 64:128], in0=score, scalar1=m2[:, 7:8], scalar2=NEG,
                                                       op0=ALU.is_lt, op1=ALU.mult), reads=[b_sm], writes=[b_negm])
                sc.op("dve", lambda e: e.tensor_tensor(out=coef[:, :, 0], in0=rsum[:, :, 0], in1=G3[:, :, 0], op=ALU.mult),
                      reads=[b_sm, b_G], writes=[b_cf])
                for h in range(4):
                    hs = slice(h * 64, (h + 1) * 64)
                    sc.op("dve", lambda e, h=h, hs=hs: e.tensor_scalar(out=ybf[:, hs], in0=pOC[:, h, 0:64], scalar1=coef[:, h, 0:1],
                                                                       scalar2=None, op0=ALU.mult),
                          reads=[b_pOC, b_cf], writes=[b_ybf])
                k0 = max(0, qt - 4)
                branch(list(range(k0, qt + 1)),
                       lambda kt, P_, b_P: sc.op("pe", lambda e: e.matmul(P_[:, :], KW[:, kt * 128:(kt + 1) * 128], Qr,
                                                                          start=True, stop=True),
                                                 reads=[b_KW, b_Q], writes=[b_P]),
                       lambda kt: ((cx.triL[:], cx.b_const) if kt == qt else ((cx.triU[:], cx.b_const) if kt == qt - 4 else None)),
                       pOW, b_pOW, VW, b_VW, lambda kt: kt)
                sc.op("dve", lambda e: e.reciprocal(out=rsum[:, :, 2], in_=pOW[:, :, 64]), reads=[b_pOW], writes=[b_sm])
                sc.op("dve", lambda e: e.tensor_tensor(out=coef[:, :, 2], in0=rsum[:, :, 2], in1=G3[:, :, 2], op=ALU.mult),
                      reads=[b_sm, b_G], writes=[b_cf])
                for h in range(4):
                    hs = slice(h * 64, (h + 1) * 64)
                    sc.op("dve", lambda e, h=h, hs=hs: e.scalar_tensor_tensor(out=ybf[:, hs], in0=pOW[:, h, 0:64], scalar=coef[:, h, 2:3],
                                                                              in1=ybf[:, hs], op0=ALU.mult, op1=ALU.add),
                          reads=[b_pOW, b_cf, b_ybf], writes=[b_ybf])
                sc.op("pe", lambda e: e.transpose(pT[:, 0:128], negm[:], identb[:]), reads=[b_negm, b_id], writes=[b_pT])
                for h in range(4):
                    if h % 2 == 0:
                        sc.op("act", lambda e, h=h: e.copy(out=Q[64:128, h, :], in_=pT[64:128, 0:128]),
                              reads=[b_pT], writes=[b_Q])
                    else:
                        sc.op("dve", lambda e, h=h: e.tensor_copy(out=Q[64:128, h, :], in_=pT[64:128, 0:128]),
                              reads=[b_pT], writes=[b_Q])
                branch(list(range(qt + 1)),
                       lambda kt, P_, b_P: sc.op("pe", lambda e: e.matmul(P_[:, :], KE[:, kt * 128:(kt + 1) * 128], Qf,
                                                                          start=True, stop=True),
                                                 reads=[b_KE, b_E, b_Q], writes=[b_P]),
                       lambda kt: ((cx.triL[:], cx.b_const) if kt == qt else None),
                       pOS, b_pOS, VS, b_VS, lambda kt: kt)
                sc.op("dve", lambda e: e.reciprocal(out=rsum[:, :, 1], in_=pOS[:, :, 64]), reads=[b_pOS], writes=[b_sm])
                sc.op("dve", lambda e: e.tensor_tensor(out=coef[:, :, 1], in0=rsum[:, :, 1], in1=G3[:, :, 1], op=ALU.mult),
                      reads=[b_sm, b_G], writes=[b_cf])
                for h in range(4):
                    hs = slice(h * 64, (h + 1) * 64)
                    sc.op("dve", lambda e, h=h, hs=hs: e.scalar_tensor_tensor(out=ybb[:, hs], in0=pOS[:, h, 0:64], scalar=coef[:, h, 1:2],
                                                                              in1=ybf[:, hs], op0=ALU.mult, op1=ALU.add),
                          reads=[b_pOS, b_cf, b_ybf], writes=[b_ybb])

                def finalize(t0=t0):
                    Y_, b_Y = YT.next()
                    for c2 in range(2):
                        sc.op("pe", lambda e, c2=c2: e.transpose(pT[:, 256 + c2 * 128:256 + (c2 + 1) * 128],
                                                                 ybb[:, c2 * 128:(c2 + 1) * 128], identb[:]),
                              reads=[b_ybb, b_id], writes=[b_pT])
                    sc.op("act", lambda e: e.copy(out=Y_[:], in_=pT[:, 256:512].rearrange("p (c n) -> p c n", c=2)),
                          reads=[b_pT], writes=[b_Y])
                    sc.dma("sp", dr.ybT[g * 256:(g + 1) * 256, t0:t0 + 128].rearrange("(c p) n -> p c n", p=128), Y_[:], b_Y,
                           reads=[b_Y])
                fin[0] = finalize
            if fin[0] is not None:
                fin[0]()
                fin[0] = None
        sc.barrier()


def phase_merge(cx, l, xT_in, xT_out, dr):
    nc, sc = cx.nc, cx.sc
    TT = 256
    NT = int(os.environ.get("MERGE_NT", S // TT))
    C = cx.modC[l][1]
    with ExitStack() as es:
        wg = _alloc(nc, es, "m_wg", [128, NCH, 3 * D], BF16)
        wb = _alloc(nc, es, "m_wb", [128, 3, 4, D], BF16)
        wo = _alloc(nc, es, "m_wo", [128, NCH, D], BF16)
        xt = Ring([_alloc(nc, es, "m_x%d" % k, [128, NCH, TT], F32) for k in range(2)], "x")
        hT = Ring([_alloc(nc, es, "m_h%d" % k, [128, NCH, TT], BF16) for k in range(2)], "h")
        ys = Ring([_alloc(nc, es, "m_ys%d" % k, [128, 3, 4, TT], BF16) for k in range(2)], "ys")
        mg = _alloc(nc, es, "m_mg", [128, NCH, TT], BF16)
        yT = _alloc(nc, es, "m_y", [128, NCH, TT], F32)
        sq = _alloc(nc, es, "m_sq", [128, NCH, TT], BF16)
        rstd = _alloc(nc, es, "m_rstd", [128, TT], F32)
        lnt = _alloc(nc, es, "m_lnt", [128, TT], F32)
        sg = Ring([_alloc(nc, es, "m_sg%d" % k, [128, TT], F32) for k in range(3)], "sg")
        ac = Ring([_alloc(nc, es, "m_ac%d" % k, [128, TT], F32) for k in range(2)], "ac")
        t2 = Ring([_alloc(nc, es, "m_t2%d" % k, [128, TT], F32) for k in range(2)], "t2")
        pG = Ring([_palloc(nc, es, "m_pG%d" % k, [128, 512], F32) for k in range(2)], "pG", True)
        pB = Ring([_palloc(nc, es, "m_pB%d" % k, [128, 512], F32) for k in range(2)], "pB", True)
        pY = Ring([_palloc(nc, es, "m_pY%d" % k, [128, 512], F32) for k in range(2)], "pY", True)
        pS = _palloc(nc, es, "m_pS", [128, 512], F32)
        b_pS = Buf("pS", True)
        b_wg = [Buf("wg") for _ in range(NCH)]
        b_wb = [Buf("wb") for _ in range(3)]
        b_wo = [Buf("wo") for _ in range(NCH)]
        b_mg = [Buf("mg") for _ in range(NCH)]
        b_yc = [Buf("yc") for _ in range(NCH)]
        b_sq, b_rstd = Buf("sq"), Buf("rstd")
        for kc in range(NCH):
            for cc in range(2):
                sc.dma("pool", wg[:, kc, cc * 1536:(cc + 1) * 1536], dr.w_gate[l, kc * 128:(kc + 1) * 128, cc * 1536:(cc + 1) * 1536],
                       b_wg[kc], writes=[b_wg[kc]])
        for n in range(3):
            for k4 in range(4):
                sc.dma("pool", wb[:, n, k4, :], dr.w_branch[l, n, k4 * 128:(k4 + 1) * 128, :], b_wb[n], writes=[b_wb[n]])
        for kc in range(NCH):
            sc.dma("pool", wo[:, kc, :], dr.w_out[l, kc * 128:(kc + 1) * 128, :], b_wo[kc], writes=[b_wo[kc]])
        ysrc = (dr.yaT, dr.ybT, dr.ycT)

        def load(t):
            t0 = t * TT
            X, b_X = xt.next()
            H, b_H = hT.next()
            Y, b_Y = ys.next()
            sc.dma("sp", X[:], xT_in[:, t0:t0 + TT].rearrange("(c p) n -> p c n", p=128), b_X, writes=[b_X])
            sc.dma("sp", H[:], dr.hT[:, t0:t0 + TT].rearrange("(c p) n -> p c n", p=128), b_H, writes=[b_H])
            for n in range(3):
                sc.dma("sp", Y[:, n, :, :], ysrc[n][:, t0:t0 + TT].rearrange("(c p) n -> p c n", p=128), b_Y, writes=[b_Y])
            return X, b_X, H, b_H, Y, b_Y

        nxt = load(0)
        for t in range(NT):
            X, b_X, H, b_H, Y, b_Y = nxt
            if t + 1 < NT:
                nxt = load(t + 1)
            t0 = t * TT
            for oc in range(NCH):
                AC, b_AC = ac.next()
                for n in range(3):
                    G_, b_G = pG.next()
                    for kc in range(NCH):
                        sc.op("pe", lambda e, kc=kc: e.matmul(G_[:, 0:TT], wg[:, kc, n * D + oc * 128:n * D + (oc + 1) * 128], H[:, kc, :],
                                                              start=(kc == 0), stop=(kc == NCH - 1)),
                              reads=[b_wg[kc], b_H], writes=[b_G])
                    B_, b_B = pB.next()
                    for k4 in range(4):
                        sc.op("pe", lambda e, k4=k4: e.matmul(B_[:, 0:TT], wb[:, n, k4, oc * 128:(oc + 1) * 128], Y[:, n, k4, :],
                                                              start=(k4 == 0), stop=(k4 == 3)),
                              reads=[b_wb[n], b_Y], writes=[b_B])
                    SG, b_SG = sg.next()
                    sc.op("act", lambda e: e.activation(out=SG[:], in_=G_[:, 0:TT], func=AF.Sigmoid), reads=[b_G], writes=[b_SG])
                    if n == 0:
                        sc.op("dve", lambda e: e.tensor_tensor(out=AC[:], in0=B_[:, 0:TT], in1=SG[:], op=ALU.mult),
                              reads=[b_B, b_SG], writes=[b_AC])
                    else:
                        T2, b_T2 = t2.next()
                        sc.op("dve", lambda e: e.tensor_tensor(out=T2[:], in0=B_[:, 0:TT], in1=SG[:], op=ALU.mult),
                              reads=[b_B, b_SG], writes=[b_T2])
                        if n == 1:
                            sc.op("pool", lambda e: e.tensor_tensor(out=AC[:], in0=AC[:], in1=T2[:], op=ALU.add),
                                  reads=[b_AC, b_T2], writes=[b_AC])
                        else:
                            sc.op("pool", lambda e: e.tensor_tensor(out=mg[:, oc, :], in0=AC[:], in1=T2[:], op=ALU.add),
                                  reads=[b_AC, b_T2], writes=[b_mg[oc]])
            for oc2 in range(NCH):
                Y_, b_Yp = pY.next()
                for oc in range(NCH):
                    sc.op("pe", lambda e, oc=oc: e.matmul(Y_[:, 0:TT], wo[:, oc, oc2 * 128:(oc2 + 1) * 128], mg[:, oc, :],
                                                          start=(oc == 0), stop=(oc == NCH - 1)),
                          reads=[b_wo[oc], b_mg[oc]], writes=[b_Yp])
                sc.op("dve", lambda e: e.tensor_copy(out=yT[:, oc2, :], in_=Y_[:, 0:TT]), reads=[b_Yp], writes=[b_yc[oc2]])
                sc.op("act", lambda e: e.activation(out=sq[:, oc2, :], in_=yT[:, oc2, :], func=AF.Square),
                      reads=[b_yc[oc2]], writes=[b_sq])
            emit_rstd(cx, sq, b_sq, pS, b_pS, rstd, b_rstd, lnt, TT)
            for ch in range(NCH):
                T2, b_T2 = t2.next()
                sc.op("dve", lambda e: e.scalar_tensor_tensor(out=T2[:], in0=yT[:, ch, :], scalar=C[:, ch:ch + 1], in1=rstd[:],
                                                              op0=ALU.mult, op1=ALU.mult),
                      reads=[b_yc[ch], b_rstd, cx.b_mod], writes=[b_T2])
                sc.op("pool", lambda e: e.tensor_tensor(out=X[:, ch, :], in0=X[:, ch, :], in1=T2[:], op=ALU.add),
                      reads=[b_X, b_T2], writes=[b_X])
            sc.dma("sp", xT_out[:, t0:t0 + TT].rearrange("(c p) n -> p c n", p=128), X[:], b_X, reads=[b_X])
        sc.barrier()


def host_constants():
    ct = {}
    ct["ident_in"] = np.eye(128, dtype=np.float32)
    k = np.arange(128)
    ct["triL_in"] = (k[:, None] <= k[None, :]).astype(np.float32)
    ct["triU_in"] = (k[:, None] > k[None, :]).astype(np.float32)
    pos = np.arange(S, dtype=np.float32)
    inv = (1.0 / (np.float32(500000.0) ** (np.arange(0, 16, 2, dtype=np.float32) / np.float32(16)))).astype(np.float32)
    ang = pos[:, None] * inv[None, :]
    cos, sin = np.cos(ang).astype(np.float32), np.sin(ang).astype(np.float32)
    C = np.ones((64, S), np.float32)
    Sn = np.zeros((64, S), np.float32)
    C[0:8] = cos.T
    C[8:16] = cos.T
    Sn[0:8] = -sin.T
    Sn[8:16] = sin.T
    ct["ropeC"] = np.ascontiguousarray(np.concatenate([C, C], 0))
    ct["ropeS"] = np.ascontiguousarray(np.concatenate([Sn, Sn], 0))
    ct["Esel"] = (np.arange(64)[:, None] == (np.arange(S)[None, :] // 64)).astype(np.float32)
    n = np.arange(256)
    cm = ((n[:, None] * 16 + 31) <= np.arange(S)[None, :]) & (n[:, None] < 255)
    ct["cmpmask"] = cm.astype(np.float32)
    ncb = S // 16 - 1
    cs = np.arange(ncb)[:, None] * 16
    ss = np.arange(64)[None, :] * 64
    ov = np.clip(np.minimum(cs + 32, ss + 64) - np.maximum(cs, ss), 0, None)
    sm = np.zeros((256, 64), np.float32)
    sm[:ncb] = ov / 32.0
    ct["slcmap"] = sm
    t = np.arange(S)
    cur = t // 64
    blk = np.arange(64)
    forced = (blk[None, :] == 0) | (blk[None, :] == cur[:, None]) | (blk[None, :] == cur[:, None] - 1)
    causal = blk[None, :] <= cur[:, None]
    ct["cmask"] = (causal & ~forced).astype(np.float32)
    ct["cbias"] = np.where(forced, 1e4, np.where(causal, 0.0, -1.0)).astype(np.float32)
    return ct


def relayout_w_in(w_in):
    perm = np.arange(64)
    perm[0:8] = np.arange(8, 16)
    perm[8:16] = np.arange(0, 8)
    cols = []
    cols += list(range(0, 512))
    cols += list(range(OFF_Q, OFF_Q + 512))
    cols += [OFF_Q + h * 64 + perm[d] for h in range(8) for d in range(64)]
    kv = lambda i: OFF_KV + i * 128
    cols += list(range(kv(0), kv(0) + 128))
    cols += list(range(kv(1), kv(1) + 128))
    cols += list(range(kv(2), kv(2) + 128))
    cols += [kv(2) + g * 64 + perm[d] for g in range(2) for d in range(64)]
    cols += list(range(kv(4), kv(4) + 128))
    cols += [kv(4) + g * 64 + perm[d] for g in range(2) for d in range(64)]
    cols += list(range(OFF_C + 512, OFF_C + 1024))
    cols += list(range(OFF_C + 1024, OFF_C + 1536))
    cols += list(range(OFF_C, OFF_C + 512))
    cols += list(range(512, 1024))
    cols += list(range(kv(3), kv(3) + 128))
    cols += list(range(kv(5), kv(5) + 128))
    cols += list(range(OFF_NG, OFF_NG + 24))
    cols = np.asarray(cols)
    assert cols.size == WP_COLS
    return np.ascontiguousarray(w_in[:, :, cols])


class DR:
    pass


def build_program(upto="all", debug=False):
    nc = bass.Bass("TRN2", target_bir_lowering=False)
    cx = Ctx()
    cx.nc = nc
    es = ExitStack()
    cx.es = es
    sc = Sched(nc, es)
    cx.sc = sc
    dr = DR()

    def din(name, shape):
        return nc.dram_tensor(name, list(shape), F32, kind="ExternalInput").ap()

    kind_i = "ExternalOutput" if debug else "Internal"

    def dscr(name, shape, dt=BF16):
        return nc.dram_tensor(name, list(shape), dt, kind=kind_i).ap()

    x = din("x", [S, D])
    c = din("c", [1, D])
    mod_w = din("mod_w", [L, D, 9 * D])
    mod_b = din("mod_b", [L, 9 * D])
    norm_g = din("norm_g", [L, 6, D])
    ffn_w13 = din("ffn_w13", [L, 2, D, 2 * DFF])
    ffn_w2 = din("ffn_w2", [L, 2, DFF, D])
    dr.wp = din("wp", [L, D, WP_COLS])
    dr.gm_ln_g = din("gm_ln_g", [L, 512])
    dr.gm_ln_b = din("gm_ln_b", [L, 512])
    dr.gm_ws = din("gm_ws", [L, 4, 128, 128])
    dr.gm_bs = din("gm_bs", [L, 4, 128])
    dr.cmp_peT = din("cmp_peT", [L, 2, 64, 32])
    dr.cmp_w1 = din("cmp_w1", [L, 2, 2048, 128])
    dr.cmp_w2 = din("cmp_w2", [L, 2, 128, 64])
    dr.conv_w = din("conv_w", [L, 3, 512])
    dr.w_branch = din("w_branch", [L, 3, 512, D])
    dr.w_gate = din("w_gate", [L, D, 3 * D])
    dr.w_out = din("w_out", [L, D, D])
    ident_d = din("ident_in", [128, 128])
    triL_d = din("triL_in", [128, 128])
    triU_d = din("triU_in", [128, 128])
    dr.ropeC = din("ropeC", [128, S])
    dr.ropeS = din("ropeS", [128, S])
    dr.Esel = din("Esel", [64, S])
    dr.cmpmask = din("cmpmask", [256, S])
    dr.slcmap = din("slcmap", [256, 64])
    dr.cmask = din("cmask", [S, 64])
    dr.cbias = din("cbias", [S, 64])
    out = nc.dram_tensor("out", [S, D], F32, kind="ExternalOutput").ap()
    xT = [nc.dram_tensor("xT%d" % i, [D, S], F32, kind=kind_i).ap() for i in range(2)]
    dr.hT = dscr("hT_d", [D, S])
    dr.yaT = dscr("yaT_d", [512, S])
    dr.ybT = dscr("ybT_d", [512, S])
    dr.ycT = dscr("ycT_d", [512, S])
    dr.qrT = dscr("qrT_d", [512, S])
    dr.qnT = dscr("qnT_d", [512, S])
    dr.kT = dscr("kT_d", [2, 128, S])
    dr.kcmpT = dscr("kcmpT_d", [2, 128, S])
    dr.v = dscr("v_d", [S, 256])
    dr.ng = dscr("ng_d", [S, 24], F32)
    dr.kcT = dscr("kcT_d", [2, 64, 256])
    dr.vc = dscr("vc_d", [2, 256, 64])
    cx.b_x = Buf("x")
    cx.b_out = Buf("out")
    cx.b_xT = [Buf("xT") for _ in range(S // 512)]
    bx = [Buf("xTd") for _ in range(S // 512)]

    cx.ident = _alloc(nc, es, "ident", [128, 128], F32)
    cx.triL = _alloc(nc, es, "triL", [128, 128], F32)
    cx.triU = _alloc(nc, es, "triU", [128, 128], F32)
    cx.ones_bf = _alloc(nc, es, "ones_bf", [128, 128], BF16)
    cx.eps_col = _alloc(nc, es, "eps_col", [128, 1], F32)
    cx.b_const = Buf("const")
    cx.b_mod = Buf("mod")
    cx.modT = [_alloc(nc, es, "modT%d" % l, [128, 72], F32) for l in range(L)]
    cx.modbT = [_alloc(nc, es, "modbT%d" % l, [128, 72], F32) for l in range(L)]
    cx.normgT = [_alloc(nc, es, "normgT%d" % l, [128, 48], F32) for l in range(L)]
    cx.modA = [[_alloc(nc, es, "modA%d_%d" % (l, s_), [128, 8], F32) for s_ in range(3)] for l in range(L)]
    cx.modB = [[_alloc(nc, es, "modB%d_%d" % (l, s_), [128, 8], F32) for s_ in range(3)] for l in range(L)]
    cx.modC = [[_alloc(nc, es, "modC%d_%d" % (l, s_), [128, 8], F32) for s_ in range(3)] for l in range(L)]
    sc.dma("sp", cx.ident[:], ident_d, cx.b_const, writes=[cx.b_const])
    sc.dma("sp", cx.triL[:], triL_d, cx.b_const, writes=[cx.b_const])
    sc.dma("sp", cx.triU[:], triU_d, cx.b_const, writes=[cx.b_const])
    sc.op("dve", lambda e: e.memset(cx.ones_bf[:], 1.0), writes=[cx.b_const])
    sc.op("dve", lambda e: e.memset(cx.eps_col[:], EPS), writes=[cx.b_const])

    stages = upto.split(",")

    def want(nm):
        return upto == "all" or nm in stages

    phase_transpose_in(cx, x, xT[0])
    phase_mod(cx, c, mod_w, mod_b, norm_g)
    if debug:
        dbgm = nc.dram_tensor("dbg_mod", [128, 72 + 24], F32, kind="ExternalOutput").ap()
        bd = Buf("dbg")
        sc.dma("sp", dbgm[:, 0:72], cx.modT[0][:], bd, reads=[cx.b_mod])
        for s_ in range(3):
            sc.dma("sp", dbgm[:, 72 + s_ * 8:80 + s_ * 8], cx.modA[0][s_][:], bd, reads=[cx.b_mod])
    cur = 0
    for l in range(L):
        if want("ffn%d0" % l):
            phase_ffn(cx, l, 0, ffn_w13[l, 0], ffn_w2[l, 0], xT[cur], bx, xT[1 - cur], bx)
            cur = 1 - cur
        if want("proj%d" % l):
            phase_proj(cx, l, xT[cur], dr)
        if want("cmp%d" % l):
            phase_cmp(cx, l, dr)
        if want("att%d" % l):
            phase_att(cx, l, dr)
        if want("merge%d" % l):
            phase_merge(cx, l, xT[cur], xT[1 - cur], dr)
            cur = 1 - cur
        if want("ffn%d1" % l):
            last = (l == L - 1)
            phase_ffn(cx, l, 1, ffn_w13[l, 1], ffn_w2[l, 1], xT[cur], bx, xT[1 - cur], bx, out_tok=(out if last else None))
            cur = 1 - cur
    sc.finish()
    es.close()
    return nc


def make_in_maps(inputs, cores):
    ct = host_constants()
    wp = relayout_w_in(np.asarray(inputs["w_in"], np.float32))
    peT = np.ascontiguousarray(np.transpose(np.asarray(inputs["cmp_pe"], np.float32), (0, 1, 3, 2)))
    shared = {k: np.ascontiguousarray(np.asarray(inputs[k], np.float32)) for k in
              ("mod_w", "mod_b", "norm_g", "ffn_w13", "ffn_w2", "gm_ln_g", "gm_ln_b", "gm_ws", "gm_bs",
               "cmp_w1", "cmp_w2", "conv_w", "w_branch", "w_gate", "w_out")}
    shared["wp"] = wp
    shared["cmp_peT"] = peT
    shared.update(ct)
    maps = []
    for b in cores:
        m = dict(shared)
        m["x"] = np.ascontiguousarray(np.asarray(inputs["x"][b], np.float32))
        m["c"] = np.ascontiguousarray(np.asarray(inputs["c"][b:b + 1], np.float32))
        maps.append(m)
    return maps


def kernel(**inputs):
    nc = build_program("all", debug=False)
    maps = make_in_maps(inputs, list(range(8)))
    res = run_bass_kernel_spmd(nc, maps, core_ids=list(range(8)))
    return np.stack([np.asarray(r["out"], np.float32) for r in res.results], 0)
```

```python
import os
import numpy as np
from contextlib import ExitStack
import concourse.bass as bass
import concourse.mybir as mybir
from concourse.bass_utils import run_bass_kernel_spmd

F32 = mybir.dt.float32
BF16 = mybir.dt.bfloat16
AF = mybir.ActivationFunctionType
ALU = mybir.AluOpType
AX = mybir.AxisListType

S = 4096
D = 1024
DFF = 2816
L = 2
NCH = D // 128
NJ = DFF // 128
EPS = 1e-6
BW = 512
A_COLS = 1024
Q_COLS = 512
KV_COLS = 768
NG_COLS = 24
C_COLS = 1536
IN_COLS = A_COLS + Q_COLS + KV_COLS + NG_COLS + C_COLS
OFF_Q = A_COLS
OFF_KV = OFF_Q + Q_COLS
OFF_NG = OFF_KV + KV_COLS
OFF_C = OFF_NG + NG_COLS


class Buf:
    __slots__ = ("name", "w", "r", "sem", "last_dma", "excl")

    def __init__(self, name, excl=False):
        self.name = name
        self.excl = excl
        self.w = None
        self.r = []
        self.sem = None
        self.last_dma = None


class Sched:
    ENGS = ("pe", "act", "dve", "pool")

    def __init__(self, nc, es):
        self.nc = nc
        self.es = es
        self.eng = {"pe": nc.tensor, "act": nc.scalar, "dve": nc.vector, "pool": nc.gpsimd, "sp": nc.sync}
        self.cnt = {e: 0 for e in self.ENGS}
        self.esem = {e: es.enter_context(nc.semaphore("sem_" + e)) for e in self.ENGS}
        self.known = {e: {} for e in self.eng}
        self.free_sems = []
        self.all_dsems = []
        self.live_bufs = []
        self.nsem = 0

    def _waits(self, eng, toks):
        need = {}
        for t in toks:
            cur = need.get(t[0])
            if cur is None or cur[1] < t[2]:
                need[t[0]] = (t[1], t[2])
        kn = self.known[eng]
        e = self.eng[eng]
        for key, (sem, val) in need.items():
            if kn.get(key, 0) >= val:
                continue
            kn[key] = val
            e.wait_ge(sem, val)

    def _deps(self, eng, reads, writes):
        toks = []
        for b in reads:
            if b.w is not None and not (eng == "pe" and b.w[3] == "pe"):
                toks.append(b.w)
        for b in writes:
            if b.w is not None and b.w[3] != eng:
                toks.append(b.w)
            for r in b.r:
                if r[3] != eng:
                    toks.append(r)
        return toks

    def _update(self, tok, reads, writes):
        for b in reads:
            if b not in writes:
                b.r.append(tok)
        for b in writes:
            b.w = tok
            b.r = []

    def op(self, eng, fn, reads=(), writes=()):
        if any(b.excl for b in reads):
            writes = list(writes) + [b for b in reads if b.excl]
            reads = [b for b in reads if not b.excl]
        self._waits(eng, self._deps(eng, reads, writes))
        ins = fn(self.eng[eng])
        self.cnt[eng] += 1
        ins.then_inc(self.esem[eng], 1)
        tok = (eng, self.esem[eng], self.cnt[eng], eng)
        self._update(tok, reads, writes)
        return tok

    def _get_sem(self, b):
        if b.sem is None:
            if self.free_sems:
                b.sem = self.free_sems.pop()
            else:
                self.nsem += 1
                s = self.es.enter_context(self.nc.semaphore("dsem%d" % self.nsem))
                b.sem = [s, 0]
                self.all_dsems.append(b.sem)
            self.live_bufs.append(b)
        return b.sem

    def dma(self, queue, out, in_, sbuf, reads=(), writes=(), **kw):
        toks = self._deps("dma", reads, writes)
        if sbuf.last_dma is not None:
            toks.append(sbuf.last_dma)
        self._waits(queue, toks)
        sm = self._get_sem(sbuf)
        ins = self.eng[queue].dma_start(out=out, in_=in_, **kw)
        sm[1] += 16
        ins.then_inc(sm[0], 16)
        tok = (id(sm), sm[0], sm[1], "dma")
        sbuf.last_dma = tok
        self._update(tok, reads, writes)
        return tok

    def barrier(self):
        toks = [(e, self.esem[e], self.cnt[e], e) for e in self.ENGS if self.cnt[e] > 0]
        for sm in self.all_dsems:
            if sm[1] > 0:
                toks.append((id(sm), sm[0], sm[1], "dma"))
        for e in self.eng:
            self._waits(e, [t for t in toks if t[0] != e])
        for b in self.live_bufs:
            if b.sem is not None:
                self.free_sems.append(b.sem)
                b.sem = None
        self.live_bufs = []

    def finish(self):
        toks = []
        for sm in self.all_dsems:
            if sm[1] > 0:
                toks.append((id(sm), sm[0], sm[1], "dma"))
        toks += [(e, self.esem[e], self.cnt[e], e) for e in self.ENGS if self.cnt[e] > 0]
        self._waits("sp", toks)


class Ctx:
    pass


_UID = [0]


def _alloc(nc, es, name, shape, dt):
    _UID[0] += 1
    return es.enter_context(nc.sbuf_tensor("%s_%d" % (name, _UID[0]), list(shape), dt))


def _palloc(nc, es, name, shape, dt):
    _UID[0] += 1
    return es.enter_context(nc.psum_tensor("%s_%d" % (name, _UID[0]), list(shape), dt))


def phase_transpose_in(cx, x_ap, xT_ap, side=None):
    nc, sc = cx.nc, cx.sc
    with ExitStack() as es:
        NB = 2
        xin = [_alloc(nc, es, "ti_x%d" % i, [128, 4, D], F32) for i in range(NB)]
        xo = [_alloc(nc, es, "ti_o%d" % i, [128, NCH, 512], F32) for i in range(NB)]
        ps = [_palloc(nc, es, "ti_p%d" % i, [128, 512], F32) for i in range(4)]
        b_in = [Buf("ti_x") for _ in range(NB)]
        b_o = [Buf("ti_o") for _ in range(NB)]
        b_ps = [Buf("ti_p", True) for _ in range(4)]
        pi = 0
        for t in range(S // 512):
            k = t % NB
            src = x_ap[t * 512:(t + 1) * 512, :].rearrange("(a p) f -> p a f", p=128)
            sc.dma("sp", xin[k][:], src, b_in[k], reads=[cx.b_x], writes=[b_in[k]])
            for ch in range(NCH):
                pb = pi % 4
                pi += 1
                for a in range(4):
                    sc.op("pe", lambda e, a=a, ch=ch, pb=pb, k=k: e.transpose(
                        ps[pb][:, a * 128:(a + 1) * 128], xin[k][:, a, ch * 128:(ch + 1) * 128], cx.ident[:]),
                        reads=[b_in[k], cx.b_const], writes=[b_ps[pb]])
                if ch % 2 == 0:
                    sc.op("dve", lambda e, ch=ch, pb=pb, k=k: e.tensor_copy(out=xo[k][:, ch, :], in_=ps[pb][:]),
                          reads=[b_ps[pb]], writes=[b_o[k]])
                else:
                    sc.op("act", lambda e, ch=ch, pb=pb, k=k: e.copy(out=xo[k][:, ch, :], in_=ps[pb][:]),
                          reads=[b_ps[pb]], writes=[b_o[k]])
            dst = xT_ap[:, t * 512:(t + 1) * 512].rearrange("(c p) n -> p c n", p=128)
            sc.dma("sp", dst, xo[k][:], b_o[k], reads=[b_o[k]], writes=[cx.b_xT[t]])
            if side is not None:
                for _ in range(3):
                    next(side, None)
        if side is not None:
            for _ in side:
                pass
        sc.barrier()


def phase_mod(cx, c_ap, mod_w_ap, mod_b_ap, norm_g_ap):
    nc, sc = cx.nc, cx.sc
    with ExitStack() as es:
        crow = _alloc(nc, es, "md_crow", [8, 128], F32)
        cT = _alloc(nc, es, "md_cT", [128, 8], F32)
        sg = _alloc(nc, es, "md_sg", [128, 8], F32)
        brow = _alloc(nc, es, "md_brow", [72, 128], F32)
        grow = _alloc(nc, es, "md_grow", [48, 128], F32)
        wbuf = [_alloc(nc, es, "md_w%d" % i, [128, NCH, 1024], F32) for i in range(2)]
        pt = _palloc(nc, es, "md_pt", [128, 512], F32)
        pm = _palloc(nc, es, "md_pm", [128, 512], F32)
        b_crow, b_cT, b_brow, b_grow = (Buf(n) for n in ("crow", "cT", "brow", "grow"))
        b_pt, b_pm = Buf("pt", True), Buf("pm", True)
        b_w = [Buf("md_w") for _ in range(2)]
        b_mod = cx.b_mod
        sc.dma("sp", crow[:], c_ap.rearrange("o (a p) -> (o a) p", p=128), b_crow, writes=[b_crow])
        sc.op("pe", lambda e: e.transpose(pt[:, 0:8], crow[:], cx.ident[0:8, 0:8]),
              reads=[b_crow, cx.b_const], writes=[b_pt])
        sc.op("act", lambda e: e.activation(out=sg[:], in_=pt[:, 0:8], func=AF.Sigmoid), reads=[b_pt], writes=[b_cT])
        sc.op("dve", lambda e: e.tensor_tensor(out=cT[:], in0=pt[:, 0:8], in1=sg[:], op=ALU.mult),
              reads=[b_pt, b_cT], writes=[b_cT])
        for l in range(L):
            sc.dma("sp", brow[:], mod_b_ap[l].rearrange("(a p) -> a p", p=128), b_brow, writes=[b_brow])
            sc.dma("sp", grow[:], norm_g_ap[l].rearrange("k (a p) -> (k a) p", p=128), b_grow, writes=[b_grow])
            sc.op("pe", lambda e: e.transpose(pt[:, 0:72], brow[:], cx.ident[0:72, 0:72]),
                  reads=[b_brow, cx.b_const], writes=[b_pt])
            sc.op("dve", lambda e, l=l: e.tensor_copy(out=cx.modbT[l][:], in_=pt[:, 0:72]), reads=[b_pt], writes=[b_mod])
            sc.op("pe", lambda e: e.transpose(pt[:, 0:48], grow[:], cx.ident[0:48, 0:48]),
                  reads=[b_grow, cx.b_const], writes=[b_pt])
            sc.op("dve", lambda e, l=l: e.tensor_copy(out=cx.normgT[l][:], in_=pt[:, 0:48]), reads=[b_pt], writes=[b_mod])
            for v in range(9):
                k = v % 2
                src = mod_w_ap[l][:, v * 1024:(v + 1) * 1024].rearrange("(a p) n -> p a n", p=128)
                sc.dma("pool", wbuf[k][:], src, b_w[k], writes=[b_w[k]])
                for ch in range(NCH):
                    col = v * 8 + ch
                    for kc in range(NCH):
                        sc.op("pe", lambda e, k=k, ch=ch, kc=kc, col=col: e.matmul(
                            pm[:, col:col + 1], wbuf[k][:, kc, ch * 128:(ch + 1) * 128], cT[:, kc:kc + 1],
                            start=(kc == 0), stop=(kc == NCH - 1)),
                            reads=[b_w[k], b_cT], writes=[b_pm])
                yield
            sc.op("dve", lambda e, l=l: e.tensor_tensor(out=cx.modT[l][:], in0=pm[:, 0:72], in1=cx.modbT[l][:], op=ALU.add),
                  reads=[b_pm, b_mod], writes=[b_mod])
            for s_ in range(3):
                g0 = cx.normgT[l][:, (2 * s_) * 8:(2 * s_ + 1) * 8]
                g1 = cx.normgT[l][:, (2 * s_ + 1) * 8:(2 * s_ + 2) * 8]
                shift = cx.modT[l][:, (3 * s_) * 8:(3 * s_ + 1) * 8]
                scale = cx.modT[l][:, (3 * s_ + 1) * 8:(3 * s_ + 2) * 8]
                gate = cx.modT[l][:, (3 * s_ + 2) * 8:(3 * s_ + 3) * 8]
                resw = 1.0 if s_ == 1 else 0.5
                sc.op("dve", lambda e, l=l, s_=s_, scale=scale, g0=g0: e.scalar_tensor_tensor(
                    out=cx.modA[l][s_][:], in0=scale, scalar=1.0, in1=g0, op0=ALU.add, op1=ALU.mult),
                    reads=[b_mod], writes=[b_mod])
                sc.op("dve", lambda e, l=l, s_=s_, shift=shift: e.tensor_copy(out=cx.modB[l][s_][:], in_=shift),
                      reads=[b_mod], writes=[b_mod])
                sc.op("dve", lambda e, l=l, s_=s_, gate=gate, g1=g1, resw=resw: e.scalar_tensor_tensor(
                    out=cx.modC[l][s_][:], in0=gate, scalar=resw, in1=g1, op0=ALU.mult, op1=ALU.mult),
                    reads=[b_mod], writes=[b_mod])
        yield


def emit_rstd(cx, sq, b_sq, ps, b_ps, rstd, b_rstd, lnt, TT):
    sc = cx.sc
    for ch in range(NCH):
        sc.op("pe", lambda e, ch=ch: e.matmul(ps[:, 0:TT], cx.ones_bf[:], sq[:, ch, :],
                                              start=(ch == 0), stop=(ch == NCH - 1)),
              reads=[b_sq, cx.b_const], writes=[b_ps])
    sc.op("act", lambda e: e.activation(out=lnt[:, 0:TT], in_=ps[:, 0:TT], func=AF.Ln, scale=1.0 / D, bias=cx.eps_col[:]),
          reads=[b_ps, cx.b_const], writes=[b_rstd])
    sc.op("act", lambda e: e.activation(out=rstd[:, 0:TT], in_=lnt[:, 0:TT], func=AF.Exp, scale=-0.5),
          reads=[b_rstd], writes=[b_rstd])


def phase_ffn(cx, l, i, w13_ap, w2_ap, xT_in, b_xin, xT_out, b_xout, out_tok=None):
    nc, sc = cx.nc, cx.sc
    TT = 256
    import os
    NT = int(os.environ.get('FFN_NT', S // TT))
    s_ = 0 if i == 0 else 2
    A, B, C = cx.modA[l][s_], cx.modB[l][s_], cx.modC[l][s_]
    with ExitStack() as es:
        w13s = _alloc(nc, es, "f_w13", [128, NCH, 2 * DFF], BF16)
        w2s = _alloc(nc, es, "f_w2", [128, NJ, D], BF16)
        xt = [_alloc(nc, es, "f_x%d" % k, [128, NCH, TT], F32) for k in range(2)]
        hT = _alloc(nc, es, "f_h", [128, NCH, TT], BF16)
        hT2 = _alloc(nc, es, "f_h2", [128, NCH, TT], BF16)
        hid = _alloc(nc, es, "f_hid", [128, NJ, TT], BF16)
        yT = _alloc(nc, es, "f_y", [128, NCH, TT], F32)
        sq = _alloc(nc, es, "f_sq", [128, NCH, TT], BF16)
        rstd = _alloc(nc, es, "f_rstd", [128, TT], F32)
        lnt = _alloc(nc, es, "f_lnt", [128, TT], F32)
        tmp = [_alloc(nc, es, "f_tmp%d" % k, [128, TT], F32) for k in range(2)]
        sgt = [_alloc(nc, es, "f_sg%d" % k, [128, TT], F32) for k in range(2)]
        if out_tok is not None:
            otok = [_alloc(nc, es, "f_ot%d" % k, [128, D], F32) for k in range(2)]
            b_otok = [Buf("otok") for _ in range(2)]
        pG = [_palloc(nc, es, "f_pg%d" % k, [128, 512], F32) for k in range(2)]
        pU = [_palloc(nc, es, "f_pu%d" % k, [128, 512], F32) for k in range(2)]
        pY = [_palloc(nc, es, "f_py%d" % k, [128, 512], F32) for k in range(2)]
        pS = _palloc(nc, es, "f_ps", [128, 512], F32)
        pT = _palloc(nc, es, "f_pt", [128, 512], F32)
        b_w13 = [Buf("w13") for _ in range(NCH)]
        b_w2 = [Buf("w2") for _ in range(NJ)]
        b_x = [Buf("x") for _ in range(2)]
        b_h, b_hid, b_y, b_sq, b_rstd = (Buf(n) for n in ("h", "hid", "y", "sq", "rstd"))
        b_pS, b_pT = Buf("pS", True), Buf("pT", True)
        b_hidj = [Buf("hidj") for _ in range(NJ)]
        b_yc = [Buf("yc") for _ in range(NCH)]
        b_tmp = [Buf("tmp") for _ in range(2)]
        b_sg = [Buf("sg") for _ in range(2)]
        b_pG = [Buf("pG", True) for _ in range(2)]
        b_pU = [Buf("pU", True) for _ in range(2)]
        b_pY = [Buf("pY", True) for _ in range(2)]

        CW = 1408
        for kc in range(NCH):
            for cc in range(2 * DFF // CW):
                sc.dma("pool", w13s[:, kc, cc * CW:(cc + 1) * CW],
                       w13_ap[kc * 128:(kc + 1) * 128, cc * CW:(cc + 1) * CW], b_w13[kc], writes=[b_w13[kc]])
        for j in range(NJ):
            sc.dma("pool", w2s[:, j, :], w2_ap[j * 128:(j + 1) * 128, :], b_w2[j], writes=[b_w2[j]])

        def load_x(t):
            k = t % 2
            src = xT_in[:, t * TT:(t + 1) * TT].rearrange("(c p) n -> p c n", p=128)
            sc.dma("sp", xt[k][:], src, b_x[k], writes=[b_x[k]])

        hTs = [hT, hT2]
        b_hs = [b_h, Buf("h2")]

        def pre(t):
            k = t % 2
            X = xt[k]
            H, bH = hTs[k], b_hs[k]
            for ch in range(NCH):
                sc.op("act", lambda e, ch=ch: e.activation(out=sq[:, ch, :], in_=X[:, ch, :], func=AF.Square),
                      reads=[b_x[k]], writes=[b_sq])
            emit_rstd(cx, sq, b_sq, pS, b_pS, rstd, b_rstd, lnt, TT)
            for ch in range(NCH):
                kk = ch % 2
                sc.op("dve", lambda e, ch=ch, kk=kk: e.scalar_tensor_tensor(
                    out=tmp[kk][:], in0=X[:, ch, :], scalar=A[:, ch:ch + 1], in1=rstd[:], op0=ALU.mult, op1=ALU.mult),
                    reads=[b_x[k], b_rstd, cx.b_mod], writes=[b_tmp[kk]])
                sc.op("act", lambda e, ch=ch, kk=kk: e.activation(
                    out=H[:, ch, :], in_=tmp[kk][:], func=AF.Identity, bias=B[:, ch:ch + 1], scale=1.0),
                    reads=[b_tmp[kk], cx.b_mod], writes=[bH])

        load_x(0)
        pre(0)
        for t in range(NT):
            k = t % 2
            if t + 1 < NT:
                load_x(t + 1)
            X = xt[k]
            H, bH = hTs[k], b_hs[k]
            for j in range(NJ):
                if j == NJ // 2 and t + 1 < NT:
                    pre(t + 1)
                kk = j % 2
                for kc in range(NCH):
                    sc.op("pe", lambda e, j=j, kc=kc, kk=kk: e.matmul(
                        pG[kk][:, 0:TT], w13s[:, kc, j * 128:(j + 1) * 128], H[:, kc, :],
                        start=(kc == 0), stop=(kc == NCH - 1)),
                        reads=[b_w13[kc], bH], writes=[b_pG[kk]])
                for kc in range(NCH):
                    sc.op("pe", lambda e, j=j, kc=kc, kk=kk: e.matmul(
                        pU[kk][:, 0:TT], w13s[:, kc, DFF + j * 128:DFF + (j + 1) * 128], H[:, kc, :],
                        start=(kc == 0), stop=(kc == NCH - 1)),
                        reads=[b_w13[kc], bH], writes=[b_pU[kk]])
                sc.op("act", lambda e, kk=kk: e.activation(out=sgt[kk][:], in_=pG[kk][:, 0:TT], func=AF.Silu),
                      reads=[b_pG[kk]], writes=[b_sg[kk]])
                sc.op("dve", lambda e, j=j, kk=kk: e.tensor_tensor(out=hid[:, j, :], in0=pU[kk][:, 0:TT], in1=sgt[kk][:],
                                                                   op=ALU.mult),
                      reads=[b_pU[kk], b_sg[kk]], writes=[b_hidj[j]])
            STG = 9
            for oc in range(NCH if STG >= 3 else 0):
                kk = oc % 2
                for j in range(NJ):
                    sc.op("pe", lambda e, j=j, oc=oc, kk=kk: e.matmul(
                        pY[kk][:, 0:TT], w2s[:, j, oc * 128:(oc + 1) * 128], hid[:, j, :],
                        start=(j == 0), stop=(j == NJ - 1)),
                        reads=[b_w2[j], b_hidj[j]], writes=[b_pY[kk]])
                sc.op("dve", lambda e, oc=oc, kk=kk: e.tensor_copy(out=yT[:, oc, :], in_=pY[kk][:, 0:TT]),
                      reads=[b_pY[kk]], writes=[b_yc[oc]])
                sc.op("act", lambda e, oc=oc: e.activation(out=sq[:, oc, :], in_=yT[:, oc, :], func=AF.Square),
                      reads=[b_yc[oc]], writes=[b_sq])
            if STG >= 4:
                emit_rstd(cx, sq, b_sq, pS, b_pS, rstd, b_rstd, lnt, TT)
            for ch in range(NCH if STG >= 4 else 0):
                kk = ch % 2
                sc.op("dve", lambda e, ch=ch, kk=kk: e.scalar_tensor_tensor(
                    out=tmp[kk][:], in0=yT[:, ch, :], scalar=C[:, ch:ch + 1], in1=rstd[:], op0=ALU.mult, op1=ALU.mult),
                    reads=[b_yc[ch], b_rstd, cx.b_mod], writes=[b_tmp[kk]])
                sc.op("pool", lambda e, ch=ch, X=X, kk=kk: e.tensor_tensor(out=X[:, ch, :], in0=X[:, ch, :], in1=tmp[kk][:],
                                                                           op=ALU.add),
                      reads=[b_x[k], b_tmp[kk]], writes=[b_x[k]])
            if out_tok is None:
                dst = xT_out[:, t * TT:(t + 1) * TT].rearrange("(c p) n -> p c n", p=128)
                sc.dma("sp", dst, X[:], b_x[k], reads=[b_x[k]])
            else:
                for a in range(TT // 128):
                    ko = (t * (TT // 128) + a) % 2
                    for half in range(2):
                        for c4 in range(4):
                            ch = half * 4 + c4
                            sc.op("pe", lambda e, a=a, ch=ch, c4=c4, X=X: e.transpose(
                                pT[:, c4 * 128:(c4 + 1) * 128], X[:, ch, a * 128:(a + 1) * 128], cx.ident[:]),
                                reads=[b_x[k], cx.b_const], writes=[b_pT])
                        if half == 0:
                            sc.op("dve", lambda e, ko=ko: e.tensor_copy(out=otok[ko][:, 0:512], in_=pT[:]),
                                  reads=[b_pT], writes=[b_otok[ko]])
                        else:
                            sc.op("act", lambda e, ko=ko: e.copy(out=otok[ko][:, 512:1024], in_=pT[:]),
                                  reads=[b_pT], writes=[b_otok[ko]])
                    r0 = t * TT + a * 128
                    sc.dma("sp", out_tok[r0:r0 + 128, :], otok[ko][:], b_otok[ko], reads=[b_otok[ko]], writes=[cx.b_out])
        sc.barrier()


class Ring:
    def __init__(self, tiles, name, excl=False):
        self.tiles = tiles
        self.bufs = [Buf(name, excl) for _ in tiles]
        self.i = 0

    def next(self):
        k = self.i % len(self.tiles)
        self.i += 1
        return self.tiles[k], self.bufs[k]


FM_U, FM_Q, FM_QS, FM_KCMP, FM_VCMP, FM_KSLC, FM_KSLCS, FM_KWIN, FM_KWINS, FM_C, FM_X, FM_B = 0, 4, 8, 12, 13, 14, 15, 16, 17, 18, 22, 26
N_FM = 30
TM_OFF = N_FM * 128
TM_W = 280
WP_COLS = TM_OFF + 512 + TM_W


def phase_proj(cx, l, xT_in, dr):
    nc, sc = cx.nc, cx.sc
    TT = 256
    NT = int(os.environ.get("PROJ_NT", S // TT))
    A, B = cx.modA[l][1], cx.modB[l][1]
    with ExitStack() as es:
        wp = _alloc(nc, es, "p_wp", [128, NCH, WP_COLS], BF16)
        xt = Ring([_alloc(nc, es, "p_x%d" % k, [128, NCH, TT], F32) for k in range(2)], "x")
        sq = _alloc(nc, es, "p_sq", [128, NCH, TT], BF16)
        rstd = _alloc(nc, es, "p_rstd", [128, TT], F32)
        lnt = _alloc(nc, es, "p_lnt", [128, TT], F32)
        tmp = Ring([_alloc(nc, es, "p_tmp%d" % k, [128, TT], F32) for k in range(3)], "tmp")
        tmq = Ring([_alloc(nc, es, "p_tmq%d" % k, [128, TT], F32) for k in range(2)], "tmq")
        hT = Ring([_alloc(nc, es, "p_h%d" % k, [128, NCH, TT], BF16) for k in range(2)], "h")
        uT = _alloc(nc, es, "p_u", [128, 4, TT], BF16)
        qr = Ring([_alloc(nc, es, "p_qr%d" % k, [128, 4, TT], BF16) for k in range(2)], "qr")
        qn = Ring([_alloc(nc, es, "p_qn%d" % k, [128, 4, TT], BF16) for k in range(2)], "qn")
        kst = Ring([_alloc(nc, es, "p_kst%d" % k, [128, 2, TT], BF16) for k in range(2)], "kst")
        kcm = Ring([_alloc(nc, es, "p_kcm%d" % k, [128, 2, TT], BF16) for k in range(2)], "kcm")
        yaT = Ring([_alloc(nc, es, "p_ya%d" % k, [128, 4, TT], BF16) for k in range(2)], "ya")
        ycT = Ring([_alloc(nc, es, "p_yc%d" % k, [128, 4, TT], BF16) for k in range(2)], "yc")
        vst = Ring([_alloc(nc, es, "p_vst%d" % k, [128, TT // 128, 256], BF16) for k in range(2)], "vst")
        ngst = Ring([_alloc(nc, es, "p_ng%d" % k, [128, TT // 128, 24], F32) for k in range(2)], "ng")
        rc = Ring([_alloc(nc, es, "p_rc%d" % k, [128, TT], F32) for k in range(2)], "rc")
        rs = Ring([_alloc(nc, es, "p_rs%d" % k, [128, TT], F32) for k in range(2)], "rs")
        hc = _alloc(nc, es, "p_hc", [128, 4, TT + 2], F32)
        xs = Ring([_alloc(nc, es, "p_xs%d" % k, [128, TT], F32) for k in range(2)], "xs")
        acc = Ring([_alloc(nc, es, "p_acc%d" % k, [128, TT], F32) for k in range(2)], "acc")
        vf = [_alloc(nc, es, "p_vf%d" % k, [128, 512], F32) for k in range(TT // 128)]
        vt = Ring([_alloc(nc, es, "p_vt%d" % k, [128, 512], F32) for k in range(2)], "vt")
        vn = Ring([_alloc(nc, es, "p_vn%d" % k, [128, 512], BF16) for k in range(2)], "vn")
        st6 = _alloc(nc, es, "p_st6", [128, TT // 128, 6], F32)
        mv = _alloc(nc, es, "p_mv", [128, TT // 128, 2], F32)
        lv = _alloc(nc, es, "p_lv", [128, TT // 128], F32)
        rv = _alloc(nc, es, "p_rv", [128, TT // 128], F32)
        wsT = _alloc(nc, es, "p_wsT", [128, 4, 128], BF16)
        wsr = _alloc(nc, es, "p_wsr", [128, 4, 128], F32)
        bsr = _alloc(nc, es, "p_bsr", [1, 512], F32)
        bsb = _alloc(nc, es, "p_bsb", [1, 512], BF16)
        onesr = _alloc(nc, es, "p_onesr", [1, 128], BF16)
        onesf = _alloc(nc, es, "p_onesf", [1, 128], F32)
        lrow = _alloc(nc, es, "p_lrow", [1, 1024], F32)
        lng = _alloc(nc, es, "p_lng", [128, 512], F32)
        lnb = _alloc(nc, es, "p_lnb", [128, 512], F32)
        cwr = _alloc(nc, es, "p_cwr", [3, 512], F32)
        cw = _alloc(nc, es, "p_cw", [128, 4, 3], F32)
        b_hc = [Buf("hc") for _ in range(4)]
        b_u = [Buf("u") for _ in range(4)]
        b_vf = [Buf("vf") for _ in range(TT // 128)]
        b_sq, b_rstd, b_st, b_set = Buf("sq"), Buf("rstd"), Buf("st"), Buf("set")
        pS = _palloc(nc, es, "p_pS", [128, 512], F32)
        b_pS = Buf("pS", True)
        pF = Ring([_palloc(nc, es, "p_pF%d" % k, [128, 512], F32) for k in range(4)], "pF", True)
        pV = _palloc(nc, es, "p_pV", [128, 512], F32)
        b_pV = Buf("pV", True)
        pW = _palloc(nc, es, "p_pW", [128, 512], F32)
        b_pW = Buf("pW", True)
        pG = _palloc(nc, es, "p_pG", [128, 512], F32)
        b_pG = Buf("pG", True)
        b_wp = [Buf("wp") for _ in range(NCH)]

        CW = 1544
        for kc in range(NCH):
            for cc in range(WP_COLS // CW):
                sc.dma("pool", wp[:, kc, cc * CW:(cc + 1) * CW],
                       dr.wp[l][kc * 128:(kc + 1) * 128, cc * CW:(cc + 1) * CW], b_wp[kc], writes=[b_wp[kc]])
        sc.dma("sp", wsr[:], dr.gm_ws[l].rearrange("g t s -> t g s"), b_set, writes=[b_set])
        sc.dma("sp", bsr[:], dr.gm_bs[l].rearrange("(o g) t -> o (g t)", o=1), b_set, writes=[b_set])
        sc.dma("sp", lrow[:, 0:512], dr.gm_ln_g[l].rearrange("(o n) -> o n", o=1), b_set, writes=[b_set])
        sc.dma("sp", lrow[:, 512:1024], dr.gm_ln_b[l].rearrange("(o n) -> o n", o=1), b_set, writes=[b_set])
        sc.dma("sp", cwr[:], dr.conv_w[l], b_set, writes=[b_set])
        sc.op("dve", lambda e: e.memset(onesr[:], 1.0), writes=[b_set])
        sc.op("dve", lambda e: e.memset(onesf[:], 1.0), writes=[b_set])
        sc.op("dve", lambda e: e.tensor_copy(out=bsb[:], in_=bsr[:]), reads=[b_set], writes=[b_set])
        sc.op("dve", lambda e: e.memset(hc[:, :, 0:2], 0.0), writes=b_hc)
        for g in range(4):
            sc.op("pe", lambda e, g=g: e.transpose(pG[:, 0:128], wsr[:, g, :], cx.ident[:]),
                  reads=[b_set, cx.b_const], writes=[b_pG])
            sc.op("dve", lambda e, g=g: e.tensor_tensor(out=wsT[:, g, :], in0=pG[:, 0:128], in1=cx.triL[:], op=ALU.mult),
                  reads=[b_pG, cx.b_const], writes=[b_set])
        for hh, dst in ((0, lng), (1, lnb)):
            sc.op("pe", lambda e, hh=hh: e.matmul(pG[:, 0:512], onesf[0:1, :], lrow[0:1, hh * 512:(hh + 1) * 512],
                                                  start=True, stop=True), reads=[b_set], writes=[b_pG])
            sc.op("dve", lambda e, dst=dst: e.tensor_copy(out=dst[:], in_=pG[:, 0:512]), reads=[b_pG], writes=[b_set])
        for cc in range(4):
            sc.op("pe", lambda e, cc=cc: e.transpose(pG[:, 0:3], cwr[:, cc * 128:(cc + 1) * 128], cx.ident[0:3, 0:3]),
                  reads=[b_set, cx.b_const], writes=[b_pG])
            sc.op("dve", lambda e, cc=cc: e.tensor_copy(out=cw[:, cc, :], in_=pG[:, 0:3]), reads=[b_pG], writes=[b_set])

        def fm_chunk(idx, H, b_H):
            P, b_P = pF.next()
            for kc in range(NCH):
                sc.op("pe", lambda e, kc=kc, P=P: e.matmul(P[:, 0:TT], wp[:, kc, idx * 128:(idx + 1) * 128], H[:, kc, :],
                                                          start=(kc == 0), stop=(kc == NCH - 1)),
                      reads=[b_wp[kc], b_H], writes=[b_P])
            return P, b_P

        def load_x(t):
            X, b_X = xt.next()
            src = xT_in[:, t * TT:(t + 1) * TT].rearrange("(c p) n -> p c n", p=128)
            sc.dma("sp", X[:], src, b_X, writes=[b_X])
            return X, b_X

        def pre(t, X, b_X):
            t0 = t * TT
            for ch in range(NCH):
                sc.op("act", lambda e, ch=ch, X=X: e.activation(out=sq[:, ch, :], in_=X[:, ch, :], func=AF.Square),
                      reads=[b_X], writes=[b_sq])
            emit_rstd(cx, sq, b_sq, pS, b_pS, rstd, b_rstd, lnt, TT)
            H, b_H = hT.next()
            for ch in range(NCH):
                T1, b_T1 = tmp.next()
                sc.op("dve", lambda e, ch=ch, X=X, T1=T1: e.scalar_tensor_tensor(
                    out=T1[:], in0=X[:, ch, :], scalar=A[:, ch:ch + 1], in1=rstd[:], op0=ALU.mult, op1=ALU.mult),
                    reads=[b_X, b_rstd, cx.b_mod], writes=[b_T1])
                sc.op("act", lambda e, ch=ch, T1=T1, H=H: e.activation(
                    out=H[:, ch, :], in_=T1[:], func=AF.Identity, bias=B[:, ch:ch + 1], scale=1.0),
                    reads=[b_T1, cx.b_mod], writes=[b_H])
            sc.dma("sp", dr.hT[:, t0:t0 + TT].rearrange("(c p) n -> p c n", p=128), H[:], b_H, reads=[b_H])
            return H, b_H

        nxt = load_x(0)
        nxtH = pre(0, nxt[0], nxt[1])
        for t in range(NT):
            X, b_X = nxt
            if t + 1 < NT:
                nxt = load_x(t + 1)
            t0 = t * TT
            RC, b_RC = rc.next()
            RS, b_RS = rs.next()
            sc.dma("sp", RC[:], dr.ropeC[:, t0:t0 + TT], b_RC, writes=[b_RC])
            sc.dma("sp", RS[:], dr.ropeS[:, t0:t0 + TT], b_RS, writes=[b_RS])
            H, b_H = nxtH
            for a in range(TT // 128):
                for kc in range(NCH):
                    sc.op("pe", lambda e, kc=kc, a=a, H=H: e.matmul(
                        pV[:, :], H[:, kc, a * 128:(a + 1) * 128], wp[:, kc, TM_OFF:TM_OFF + 512],
                        start=(kc == 0), stop=(kc == NCH - 1)), reads=[b_wp[kc], b_H], writes=[b_pV])
                sc.op("act", lambda e, a=a: e.activation(out=vf[a][:], in_=pV[:], func=AF.Gelu_apprx_tanh),
                      reads=[b_pV], writes=[b_vf[a]])
                sc.op("dve", lambda e, a=a: e.bn_stats(out=st6[:, a, :], in_=vf[a][:]), reads=[b_vf[a]], writes=[b_st])
                sc.op("dve", lambda e, a=a: e.bn_aggr(out=mv[:, a, :], in_=st6[:, a, :]), reads=[b_st], writes=[b_st])
            sc.op("act", lambda e: e.activation(out=lv[:], in_=mv[:, :, 1], func=AF.Ln, scale=1.0, bias=cx.eps_col[:]),
                  reads=[b_st, cx.b_const], writes=[b_st])
            sc.op("act", lambda e: e.activation(out=rv[:], in_=lv[:], func=AF.Exp, scale=-0.5), reads=[b_st], writes=[b_st])
            for cc in range(4):
                P, b_P = fm_chunk(FM_U + cc, H, b_H)
                sc.op("act", lambda e, cc=cc, P=P: e.activation(out=uT[:, cc, :], in_=P[:, 0:TT], func=AF.Gelu_apprx_tanh),
                      reads=[b_P], writes=[b_u[cc]])
            YA, b_YA = yaT.next()
            VS, b_VS = vst.next()
            NG, b_NG = ngst.next()
            for a in range(TT // 128):
                V1, b_V1 = vt.next()
                sc.op("dve", lambda e, a=a, V1=V1: e.tensor_scalar(
                    out=V1[:], in0=vf[a][:], scalar1=mv[:, a, 0:1], scalar2=rv[:, a:a + 1], op0=ALU.subtract, op1=ALU.mult),
                    reads=[b_vf[a], b_st], writes=[b_V1])
                sc.op("pool", lambda e, V1=V1: e.tensor_tensor(out=V1[:], in0=V1[:], in1=lng[:], op=ALU.mult),
                      reads=[b_V1, b_set], writes=[b_V1])
                VN, b_VN = vn.next()
                sc.op("pool", lambda e, V1=V1, VN=VN: e.tensor_tensor(out=VN[:], in0=V1[:], in1=lnb[:], op=ALU.add),
                      reads=[b_V1, b_set], writes=[b_VN])
                for g in range(4):
                    sc.op("pe", lambda e, g=g, VN=VN: e.matmul(pG[:, g * 128:(g + 1) * 128], VN[:, g * 128:(g + 1) * 128],
                                                               wsT[:, g, :], start=True, stop=False),
                          reads=[b_VN, b_set], writes=[b_pG])
                    sc.op("pe", lambda e, g=g: e.matmul(pG[:, g * 128:(g + 1) * 128], onesr[0:1, :],
                                                        bsb[0:1, g * 128:(g + 1) * 128], start=False, stop=True),
                          reads=[b_set], writes=[b_pG])
                sc.op("dve", lambda e, a=a, YA=YA: e.tensor_tensor(
                    out=YA[:, :, a * 128:(a + 1) * 128], in0=pG[:, 0:512].rearrange("p (g t) -> p g t", g=4),
                    in1=uT[:, :, a * 128:(a + 1) * 128], op=ALU.mult),
                    reads=[b_pG] + b_u, writes=[b_YA])
                for kc in range(NCH):
                    sc.op("pe", lambda e, kc=kc, a=a, H=H: e.matmul(
                        pW[:, 0:TM_W], H[:, kc, a * 128:(a + 1) * 128], wp[:, kc, TM_OFF + 512:TM_OFF + 512 + TM_W],
                        start=(kc == 0), stop=(kc == NCH - 1)), reads=[b_wp[kc], b_H], writes=[b_pW])
                sc.op("act", lambda e, a=a, VS=VS: e.copy(out=VS[:, a, :], in_=pW[:, 0:256]), reads=[b_pW], writes=[b_VS])
                sc.op("act", lambda e, a=a, NG=NG: e.activation(out=NG[:, a, :], in_=pW[:, 256:280], func=AF.Sigmoid),
                      reads=[b_pW], writes=[b_NG])
            sc.dma("sp", dr.yaT[:, t0:t0 + TT].rearrange("(c p) n -> p c n", p=128), YA[:], b_YA, reads=[b_YA])
            sc.dma("sp", dr.v[t0:t0 + TT, :].rearrange("(a p) c -> p a c", p=128), VS[:], b_VS, reads=[b_VS])
            sc.dma("sp", dr.ng[t0:t0 + TT, :].rearrange("(a p) c -> p a c", p=128), NG[:], b_NG, reads=[b_NG])
            QR, b_QR = qr.next()
            QN, b_QN = qn.next()

            def rope(Pq, b_Pq, Ps, b_Ps, dst, b_dst):
                T1, b_T1 = tmq.next()
                T2, b_T2 = tmp.next()
                sc.op("dve", lambda e: e.tensor_tensor(out=T1[:], in0=Ps[:, 0:TT], in1=RS[:], op=ALU.mult),
                      reads=[b_Ps, b_RS], writes=[b_T1])
                sc.op("dve", lambda e: e.tensor_tensor(out=T2[:], in0=Pq[:, 0:TT], in1=RC[:], op=ALU.mult),
                      reads=[b_Pq, b_RC], writes=[b_T2])
                sc.op("pool", lambda e: e.tensor_tensor(out=dst, in0=T1[:], in1=T2[:], op=ALU.add),
                      reads=[b_T1, b_T2], writes=[b_dst])

            for qc in range(4):
                Pq, b_Pq = fm_chunk(FM_Q + qc, H, b_H)
                Ps, b_Ps = fm_chunk(FM_QS + qc, H, b_H)
                sc.op("act", lambda e, qc=qc, Pq=Pq, QN=QN: e.copy(out=QN[:, qc, :], in_=Pq[:, 0:TT]),
                      reads=[b_Pq], writes=[b_QN])
                rope(Pq, b_Pq, Ps, b_Ps, QR[:, qc, :], b_QR)
            sc.dma("sp", dr.qrT[:, t0:t0 + TT].rearrange("(c p) n -> p c n", p=128), QR[:], b_QR, reads=[b_QR])
            sc.dma("sp", dr.qnT[:, t0:t0 + TT].rearrange("(c p) n -> p c n", p=128), QN[:], b_QN, reads=[b_QN])
            KC, b_KC = kcm.next()
            KS, b_KS = kst.next()
            for i2, idx in enumerate((FM_KCMP, FM_VCMP)):
                P, b_P = fm_chunk(idx, H, b_H)
                sc.op("act", lambda e, i2=i2, P=P, KC=KC: e.copy(out=KC[:, i2, :], in_=P[:, 0:TT]), reads=[b_P], writes=[b_KC])
            for i2, (idx, idxs) in enumerate(((FM_KSLC, FM_KSLCS), (FM_KWIN, FM_KWINS))):
                Pq, b_Pq = fm_chunk(idx, H, b_H)
                Ps, b_Ps = fm_chunk(idxs, H, b_H)
                rope(Pq, b_Pq, Ps, b_Ps, KS[:, i2, :], b_KS)
            sc.dma("sp", dr.kcmpT[:, :, t0:t0 + TT].rearrange("i p n -> p i n"), KC[:], b_KC, reads=[b_KC])
            sc.dma("sp", dr.kT[:, :, t0:t0 + TT].rearrange("i p n -> p i n"), KS[:], b_KS, reads=[b_KS])
            H_cur, b_H_cur = H, b_H
            if t + 1 < NT:
                nxtH = pre(t + 1, nxt[0], nxt[1])
            H, b_H = H_cur, b_H_cur
            YC, b_YC = ycT.next()
            for cc in range(4):
                Pc, b_Pc = fm_chunk(FM_C + cc, H, b_H)
                Px, b_Px = fm_chunk(FM_X + cc, H, b_H)
                Pb, b_Pb = fm_chunk(FM_B + cc, H, b_H)
                XS, b_XS = xs.next()
                AC, b_AC = acc.next()
                sc.op("act", lambda e, Px=Px, XS=XS: e.copy(out=XS[:], in_=Px[:, 0:TT]), reads=[b_Px], writes=[b_XS])
                sc.op("dve", lambda e, cc=cc, Pc=Pc, XS=XS: e.tensor_tensor(out=hc[:, cc, 2:2 + TT], in0=Pc[:, 0:TT], in1=XS[:],
                                                                            op=ALU.mult),
                      reads=[b_Pc, b_XS], writes=[b_hc[cc]])
                sc.op("dve", lambda e, cc=cc, AC=AC: e.tensor_scalar(out=AC[:], in0=hc[:, cc, 2:2 + TT], scalar1=cw[:, cc, 2:3],
                                                                     scalar2=None, op0=ALU.mult),
                      reads=[b_hc[cc], b_set], writes=[b_AC])
                sc.op("dve", lambda e, cc=cc, AC=AC: e.scalar_tensor_tensor(
                    out=AC[:], in0=hc[:, cc, 1:1 + TT], scalar=cw[:, cc, 1:2], in1=AC[:], op0=ALU.mult, op1=ALU.add),
                    reads=[b_hc[cc], b_set, b_AC], writes=[b_AC])
                sc.op("dve", lambda e, cc=cc, AC=AC: e.scalar_tensor_tensor(
                    out=AC[:], in0=hc[:, cc, 0:TT], scalar=cw[:, cc, 0:1], in1=AC[:], op0=ALU.mult, op1=ALU.add),
                    reads=[b_hc[cc], b_set, b_AC], writes=[b_AC])
                sc.op("dve", lambda e, cc=cc, AC=AC, Pb=Pb, YC=YC: e.tensor_tensor(out=YC[:, cc, :], in0=Pb[:, 0:TT], in1=AC[:],
                                                                                   op=ALU.mult),
                      reads=[b_Pb, b_AC], writes=[b_YC])
                sc.op("pool", lambda e, cc=cc: e.tensor_copy(out=hc[:, cc, 0:2], in_=hc[:, cc, TT:TT + 2]),
                      reads=[b_hc[cc]], writes=[b_hc[cc]])
            sc.dma("sp", dr.ycT[:, t0:t0 + TT].rearrange("(c p) n -> p c n", p=128), YC[:], b_YC, reads=[b_YC])
        sc.barrier()


def phase_cmp(cx, l, dr):
    nc, sc = cx.nc, cx.sc
    with ExitStack() as es:
        w1s = _alloc(nc, es, "c_w1", [128, 2, 32, 128], BF16)
        w2s = _alloc(nc, es, "c_w2", [128, 2, 64], BF16)
        peT = _alloc(nc, es, "c_pe", [128, 2, 32], BF16)
        kin = _alloc(nc, es, "c_kin", [128, 2, S], BF16)
        hid = _alloc(nc, es, "c_hid", [128, 256], BF16)
        bcol = _alloc(nc, es, "c_bcol", [128, 1], F32)
        kst = _alloc(nc, es, "c_kst", [64, 256], BF16)
        vst = _alloc(nc, es, "c_vst", [128, 2, 64], BF16)
        pc = _palloc(nc, es, "c_pc", [128, 512], F32)
        ph = _palloc(nc, es, "c_ph", [128, 512], F32)
        pk = _palloc(nc, es, "c_pk", [128, 512], F32)
        b_w1, b_w2, b_pe, b_kin, b_hid, b_bcol, b_kst, b_vst = (Buf(n) for n in "w1 w2 pe kin hid bcol kst vst".split())
        b_pc, b_ph, b_pk = Buf("pc", True), Buf("ph", True), Buf("pk", True)
        for kv in range(2):
            for half in range(2):
                sc.dma("pool", w1s[half * 64:(half + 1) * 64, kv, :, :],
                       dr.cmp_w1[l, kv].rearrange("(lp d) h -> d lp h", d=64), b_w1, writes=[b_w1])
                sc.dma("pool", peT[half * 64:(half + 1) * 64, kv, :], dr.cmp_peT[l, kv], b_pe, writes=[b_pe])
            sc.dma("pool", w2s[:, kv, :], dr.cmp_w2[l, kv], b_w2, writes=[b_w2])
        sc.dma("sp", kin[:], dr.kcmpT.rearrange("i p n -> p i n"), b_kin, writes=[b_kin])
        sc.op("dve", lambda e: e.memset(hid[:], 0.0), writes=[b_hid])
        for kv in range(2):
            for g in range(2):
                ps_ = slice(g * 64, (g + 1) * 64)
                for lp in range(32):
                    sc.op("pe", lambda e, lp=lp: e.matmul(pc[:, 0:1], w1s[ps_, kv, lp, :], peT[ps_, kv, lp:lp + 1],
                                                          start=(lp == 0), stop=(lp == 31)),
                          reads=[b_w1, b_pe], writes=[b_pc])
                sc.op("dve", lambda e: e.tensor_copy(out=bcol[:], in_=pc[:, 0:1]), reads=[b_pc], writes=[b_bcol])
                kv3 = kin[:, kv, :].rearrange("p (n s) -> p n s", s=16)
                for lp in range(32):
                    rhs = kv3[ps_, 0:255, lp] if lp < 16 else kv3[ps_, 1:256, lp - 16]
                    sc.op("pe", lambda e, lp=lp, rhs=rhs: e.matmul(ph[:, 0:255], w1s[ps_, kv, lp, :], rhs,
                                                                   start=(lp == 0), stop=(lp == 31)),
                          reads=[b_w1, b_kin], writes=[b_ph])
                sc.op("act", lambda e: e.activation(out=hid[:, 0:255], in_=ph[:, 0:255], func=AF.Gelu_apprx_tanh,
                                                    bias=bcol[:], scale=1.0),
                      reads=[b_ph, b_bcol], writes=[b_hid])
                if kv == 0:
                    sc.op("pe", lambda e: e.matmul(pk[0:64, 0:256], w2s[:, 0, :], hid[:, 0:256], start=True, stop=True),
                          reads=[b_w2, b_hid], writes=[b_pk])
                    sc.op("dve", lambda e: e.tensor_copy(out=kst[:], in_=pk[0:64, 0:256]), reads=[b_pk], writes=[b_kst])
                    sc.dma("sp", dr.kcT[g], kst[:], b_kst, reads=[b_kst])
                else:
                    for nt in range(2):
                        sc.op("pe", lambda e, nt=nt: e.matmul(pk[:, nt * 64:(nt + 1) * 64], hid[:, nt * 128:(nt + 1) * 128],
                                                              w2s[:, 1, :], start=True, stop=True),
                              reads=[b_w2, b_hid], writes=[b_pk])
                    sc.op("dve", lambda e: e.tensor_copy(out=vst[:], in_=pk[:, 0:128].rearrange("p (a d) -> p a d", a=2)),
                          reads=[b_pk], writes=[b_vst])
                    sc.dma("sp", dr.vc[g].rearrange("(a p) d -> p a d", p=128), vst[:], b_vst, reads=[b_vst])
        sc.barrier()


def phase_att(cx, l, dr, after_setup=None):
    nc, sc = cx.nc, cx.sc
    NQ = int(os.environ.get("ATT_NQ", S // 128))
    SCALE = 0.125
    NEG = -30000.0
    with ExitStack() as es:
        KE = _alloc(nc, es, "a_KE", [128, S], BF16)
        KW = _alloc(nc, es, "a_KW", [64, S], BF16)
        VS = _alloc(nc, es, "a_VS", [128, 32, 65], BF16)
        VW = _alloc(nc, es, "a_VW", [128, 32, 65], BF16)
        KC = _alloc(nc, es, "a_KC", [64, 256], BF16)
        VC = _alloc(nc, es, "a_VC", [128, 2, 65], BF16)
        SM = _alloc(nc, es, "a_SM", [128, 2, 64], BF16)
        QM = Ring([_alloc(nc, es, "a_QM%d" % k, [128, 4, 128], BF16) for k in range(2)], "QM")
        QN = Ring([_alloc(nc, es, "a_QN%d" % k, [64, 4, 128], BF16) for k in range(2)], "QN")
        NG = Ring([_alloc(nc, es, "a_NG%d" % k, [128, 12], F32) for k in range(2)], "NG")
        CM = Ring([_alloc(nc, es, "a_CM%d" % k, [128, 2, 128], F32) for k in range(2)], "CM")
        CK = Ring([_alloc(nc, es, "a_CK%d" % k, [128, 2, 64], F32) for k in range(2)], "CK")
        PT = Ring([_alloc(nc, es, "a_PT%d" % k, [128, 4, 128], BF16) for k in range(4)], "PT")
        negm = _alloc(nc, es, "a_negm", [128, 128], BF16)
        ybb = _alloc(nc, es, "a_ybb", [128, 256], BF16)
        ybf = _alloc(nc, es, "a_ybf", [128, 256], F32)
        YT = Ring([_alloc(nc, es, "a_YT%d" % k, [128, 2, 128], BF16) for k in range(2)], "YT")
        sm_ = _alloc(nc, es, "a_small", [128, 256], F32)
        identb = _alloc(nc, es, "a_identb", [128, 128], BF16)
        rsum = sm_[:, 0:12].rearrange("p (h r) -> p h r", r=3)
        cf_ = _alloc(nc, es, "a_coef", [128, 12], F32)
        coef = cf_[:, 0:12].rearrange("p (h r) -> p h r", r=3)
        imp = sm_[:, 32:96]
        score = sm_[:, 96:160]
        sc2 = sm_[:, 160:224]
        m1 = sm_[:, 224:232]
        m2 = sm_[:, 232:240]
        pS = Ring([_palloc(nc, es, "a_pS%d" % k, [128, 512], F32) for k in range(2)], "pS", True)
        pOC = _palloc(nc, es, "a_pOC", [128, 4, 128], F32)
        pU = _palloc(nc, es, "a_pU", [128, 4, 128], F32)
        pOS = _palloc(nc, es, "a_pOS", [128, 4, 128], F32)
        pOW = _palloc(nc, es, "a_pOW", [128, 4, 128], F32)
        pT = _palloc(nc, es, "a_pT", [128, 1024], BF16)
        b_pOC, b_pU, b_pOS, b_pOW, b_pT = (Buf(n, True) for n in "pOC pU pOS pOW pT".split())
        b_KE, b_KW, b_VS, b_VW, b_KC, b_VC, b_SM, b_E = (Buf(n) for n in "KE KW VS VW KC VC SM E".split())
        b_negm, b_ybb, b_ybf, b_sm, b_id, b_cf = (Buf(n) for n in "negm ybb ybf sm idb cf".split())

        for q4 in range(4):
            sc.dma("pool", KE[64:128, q4 * 1024:(q4 + 1) * 1024], dr.Esel[:, q4 * 1024:(q4 + 1) * 1024], b_E, writes=[b_E])
        sc.dma("pool", SM[:], dr.slcmap.rearrange("(a p) j -> p a j", p=128), b_SM, writes=[b_SM])
        sc.op("dve", lambda e: e.tensor_copy(out=identb[:], in_=cx.ident[:]), reads=[cx.b_const], writes=[b_id])
        sc.op("dve", lambda e: e.memset(negm[:], 0.0), writes=[b_negm])
        if after_setup is not None:
            after_setup()

        for g in range(2):
            gs = slice(g * 64, (g + 1) * 64)
            sc.dma("sp", KE[0:64, :], dr.kT[0, gs, :], b_KE, writes=[b_KE])
            sc.dma("sp", KW[:], dr.kT[1, gs, :], b_KW, writes=[b_KW])
            sc.dma("sp", VS[:, :, 0:64], dr.v[:, g * 64:(g + 1) * 64].rearrange("(a p) d -> p a d", p=128), b_VS, writes=[b_VS])
            sc.dma("sp", VW[:, :, 0:64], dr.v[:, 128 + g * 64:128 + (g + 1) * 64].rearrange("(a p) d -> p a d", p=128),
                   b_VW, writes=[b_VW])
            sc.dma("sp", KC[:], dr.kcT[g], b_KC, writes=[b_KC])
            sc.dma("sp", VC[:, :, 0:64], dr.vc[g].rearrange("(a p) d -> p a d", p=128), b_VC, writes=[b_VC])
            sc.op("dve", lambda e: e.memset(VS[:, :, 64:65], 1.0), writes=[b_VS])
            sc.op("dve", lambda e: e.memset(VW[:, :, 64:65], 1.0), writes=[b_VW])
            sc.op("dve", lambda e: e.memset(VC[:, :, 64:65], 1.0), writes=[b_VC])

            def load_q(qt):
                t0 = qt * 128
                Q, b_Q = QM.next()
                Qn, b_Qn = QN.next()
                G_, b_G = NG.next()
                C_, b_C = CM.next()
                K_, b_K = CK.next()
                sc.dma("sp", Q[0:64, :, :], dr.qrT[g * 256:(g + 1) * 256, t0:t0 + 128].rearrange("(h d) n -> d h n", d=64),
                       b_Q, writes=[b_Q])
                sc.dma("sp", Qn[:], dr.qnT[g * 256:(g + 1) * 256, t0:t0 + 128].rearrange("(h d) n -> d h n", d=64),
                       b_Qn, writes=[b_Qn])
                sc.dma("sp", G_[:], dr.ng[t0:t0 + 128, g * 12:(g + 1) * 12], b_G, writes=[b_G])
                sc.dma("sp", C_[:], dr.cmpmask[:, t0:t0 + 128].rearrange("(a p) q -> p a q", p=128), b_C, writes=[b_C])
                sc.dma("sp", K_[:, 0, :], dr.cmask[t0:t0 + 128, :], b_K, writes=[b_K])
                sc.dma("sp", K_[:, 1, :], dr.cbias[t0:t0 + 128, :], b_K, writes=[b_K])
                return (Q, b_Q, Qn, b_Qn, G_, b_G, C_, b_C, K_, b_K)

            def pv(Pt, b_Pt, dst, b_dst, V, b_V, kt, first, ncol=65):
                for h in range(4):
                    sc.op("pe", lambda e, h=h: e.matmul(dst[:, h, 0:ncol], Pt[:, h, :], V[:, kt, 0:ncol],
                                                        start=(first and h == 0), stop=False, skip_group_check=True),
                          reads=[b_Pt, b_V], writes=[b_dst])

            def branch(tiles, smat, mask_of, dst, b_dst, V, b_V, vidx_of, extra=None):
                n = len(tiles)
                pend = [None] * n

                def issue_s(i):
                    P_, b_P = pS.next()
                    smat(tiles[i], P_, b_P)
                    pend[i] = (P_, b_P)

                if n:
                    issue_s(0)
                for i in range(n):
                    if i + 1 < n:
                        issue_s(i + 1)
                    P_, b_P = pend[i]
                    Pt, b_Pt = PT.next()
                    sc.op("act", lambda e: e.activation(out=Pt[:].rearrange("p h n -> p (h n)"), in_=P_[:, :], func=AF.Exp,
                                                        scale=SCALE), reads=[b_P], writes=[b_Pt])
                    m = mask_of(tiles[i])
                    if m is not None:
                        msk, b_m = m
                        sc.op("dve", lambda e: e.tensor_tensor(out=Pt[:], in0=Pt[:], in1=msk.unsqueeze(1).to_broadcast([128, 4, 128]),
                                                               op=ALU.mult), reads=[b_Pt, b_m], writes=[b_Pt])
                    pv(Pt, b_Pt, dst, b_dst, V, b_V, vidx_of(tiles[i]), i == 0)
                    if extra is not None:
                        extra(Pt, b_Pt, tiles[i], i == 0)

            fin = [None]
            nxt = load_q(0)
            for qt in range(NQ):
                (Q, b_Q, Qn, b_Qn, G_, b_G, C_, b_C, K_, b_K) = nxt
                if qt + 1 < NQ:
                    nxt = load_q(qt + 1)
                t0 = qt * 128
                G3 = G_[:].rearrange("p (h r) -> p h r", r=3)
                Qf = Q[:].rearrange("p h n -> p (h n)")
                Qr = Q[0:64, :, :].rearrange("p h n -> p (h n)")
                Qnf = Qn[:].rearrange("p h n -> p (h n)")
                nnt = 1 if (t0 + 127) < (16 * 128 + 31) else 2
                branch(list(range(nnt)),
                       lambda nt, P_, b_P: sc.op("pe", lambda e: e.matmul(P_[:, :], KC[:, nt * 128:(nt + 1) * 128], Qnf,
                                                                          start=True, stop=True),
                                                 reads=[b_KC, b_Qn], writes=[b_P]),
                       lambda nt: (C_[:, nt, :], b_C), pOC, b_pOC, VC, b_VC, lambda nt: nt,
                       extra=lambda Pt, b_Pt, nt, first: pv(Pt, b_Pt, pU, b_pU, SM, b_SM, nt, first, ncol=64))
                if fin[0] is not None:
                    fin[0]()
                    fin[0] = None
                sc.op("dve", lambda e: e.tensor_scalar(out=rsum[:, :, 0], in0=pOC[:, :, 64], scalar1=1e-30, scalar2=None,
                                                       op0=ALU.max), reads=[b_pOC], writes=[b_sm])
                sc.op("dve", lambda e: e.reciprocal(out=rsum[:, :, 0], in_=rsum[:, :, 0]), reads=[b_sm], writes=[b_sm])
                sc.op("dve", lambda e: e.tensor_scalar(out=imp, in0=pU[:, 0, 0:64], scalar1=rsum[:, 0, 0:1], scalar2=None,
                                                       op0=ALU.mult), reads=[b_pU, b_sm], writes=[b_sm])
                for h in range(1, 4):
                    sc.op("dve", lambda e, h=h: e.scalar_tensor_tensor(out=imp, in0=pU[:, h, 0:64], scalar=rsum[:, h, 0:1],
                                                                       in1=imp, op0=ALU.mult, op1=ALU.add),
                          reads=[b_pU, b_sm], writes=[b_sm])
                sc.op("dve", lambda e: e.tensor_tensor(out=score, in0=imp, in1=K_[:, 0, :], op=ALU.mult),
                      reads=[b_sm, b_K], writes=[b_sm])
                sc.op("dve", lambda e: e.tensor_tensor(out=score, in0=score, in1=K_[:, 1, :], op=ALU.add),
                      reads=[b_sm, b_K], writes=[b_sm])
                sc.op("dve", lambda e: e.max(out=m1, in_=score), reads=[b_sm], writes=[b_sm])
                sc.op("dve", lambda e: e.match_replace(out=sc2, in_to_replace=m1, in_values=score, imm_value=-1e9),
                      reads=[b_sm], writes=[b_sm])
                sc.op("dve", lambda e: e.max(out=m2, in_=sc2), reads=[b_sm], writes=[b_sm])
                sc.op("dve", lambda e: e.tensor_scalar(out=negm[:, 64:128], in0=score, scalar1=m2[:, 7:8], scalar2=NEG,
                                                       op0=ALU.is_lt, op1=ALU.mult), reads=[b_sm], writes=[b_negm])
                sc.op("dve", lambda e: e.tensor_tensor(out=coef[:, :, 0], in0=rsum[:, :, 0], in1=G3[:, :, 0], op=ALU.mult),
                      reads=[b_sm, b_G], writes=[b_cf])
                for h in range(4):
                    hs = slice(h * 64, (h + 1) * 64)
                    sc.op("dve", lambda e, h=h, hs=hs: e.tensor_scalar(out=ybf[:, hs], in0=pOC[:, h, 0:64], scalar1=coef[:, h, 0:1],
                                                                       scalar2=None, op0=ALU.mult),
                          reads=[b_pOC, b_cf], writes=[b_ybf])
                k0 = max(0, qt - 4)
                branch(list(range(k0, qt + 1)),
                       lambda kt, P_, b_P: sc.op("pe", lambda e: e.matmul(P_[:, :], KW[:, kt * 128:(kt + 1) * 128], Qr,
                                                                          start=True, stop=True),
                                                 reads=[b_KW, b_Q], writes=[b_P]),
                       lambda kt: ((cx.triL[:], cx.b_const) if kt == qt else ((cx.triU[:], cx.b_const) if kt == qt - 4 else None)),
                       pOW, b_pOW, VW, b_VW, lambda kt: kt)
                sc.op("dve", lambda e: e.reciprocal(out=rsum[:, :, 2], in_=pOW[:, :, 64]), reads=[b_pOW], writes=[b_sm])
                sc.op("dve", lambda e: e.tensor_tensor(out=coef[:, :, 2], in0=rsum[:, :, 2], in1=G3[:, :, 2], op=ALU.mult),
                      reads=[b_sm, b_G], writes=[b_cf])
                for h in range(4):
                    hs = slice(h * 64, (h + 1) * 64)
                    sc.op("dve", lambda e, h=h, hs=hs: e.scalar_tensor_tensor(out=ybf[:, hs], in0=pOW[:, h, 0:64], scalar=coef[:, h, 2:3],
                                                                              in1=ybf[:, hs], op0=ALU.mult, op1=ALU.add),
                          reads=[b_pOW, b_cf, b_ybf], writes=[b_ybf])
                sc.op("pe", lambda e: e.transpose(pT[:, 0:128], negm[:], identb[:]), reads=[b_negm, b_id], writes=[b_pT])
                for h in range(4):
                    if h % 2 == 0:
                        sc.op("act", lambda e, h=h: e.copy(out=Q[64:128, h, :], in_=pT[64:128, 0:128]),
                              reads=[b_pT], writes=[b_Q])
                    else:
                        sc.op("dve", lambda e, h=h: e.tensor_copy(out=Q[64:128, h, :], in_=pT[64:128, 0:128]),
                              reads=[b_pT], writes=[b_Q])
                branch(list(range(qt + 1)),
                       lambda kt, P_, b_P: sc.op("pe", lambda e: e.matmul(P_[:, :], KE[:, kt * 128:(kt + 1) * 128], Qf,
                                                                          start=True, stop=True),
                                                 reads=[b_KE, b_E, b_Q], writes=[b_P]),
                       lambda kt: ((cx.triL[:], cx.b_const) if kt == qt else None),
                       pOS, b_pOS, VS, b_VS, lambda kt: kt)
                sc.op("dve", lambda e: e.reciprocal(out=rsum[:, :, 1], in_=pOS[:, :, 64]), reads=[b_pOS], writes=[b_sm])
                sc.op("dve", lambda e: e.tensor_tensor(out=coef[:, :, 1], in0=rsum[:, :, 1], in1=G3[:, :, 1], op=ALU.mult),
                      reads=[b_sm, b_G], writes=[b_cf])
                for h in range(4):
                    hs = slice(h * 64, (h + 1) * 64)
                    sc.op("dve", lambda e, h=h, hs=hs: e.scalar_tensor_tensor(out=ybb[:, hs], in0=pOS[:, h, 0:64], scalar=coef[:, h, 1:2],
                                                                              in1=ybf[:, hs], op0=ALU.mult, op1=ALU.add),
                          reads=[b_pOS, b_cf, b_ybf], writes=[b_ybb])

                def finalize(t0=t0):
                    Y_, b_Y = YT.next()
                    for c2 in range(2):
                        sc.op("pe", lambda e, c2=c2: e.transpose(pT[:, 256 + c2 * 128:256 + (c2 + 1) * 128],
                                                                 ybb[:, c2 * 128:(c2 + 1) * 128], identb[:]),
                              reads=[b_ybb, b_id], writes=[b_pT])
                    sc.op("act", lambda e: e.copy(out=Y_[:], in_=pT[:, 256:512].rearrange("p (c n) -> p c n", c=2)),
                          reads=[b_pT], writes=[b_Y])
                    sc.dma("sp", dr.ybT[g * 256:(g + 1) * 256, t0:t0 + 128].rearrange("(c p) n -> p c n", p=128), Y_[:], b_Y,
                           reads=[b_Y])
                fin[0] = finalize
            if fin[0] is not None:
                fin[0]()
                fin[0] = None
        sc.barrier()


def merge_weights_alloc(cx, es):
    nc = cx.nc
    w = Ctx()
    w.wg = _alloc(nc, es, "m_wg", [128, NCH, 3 * D], BF16)
    w.wb = _alloc(nc, es, "m_wb", [128, 3, 4, D], BF16)
    w.wo = _alloc(nc, es, "m_wo", [128, NCH, D], BF16)
    w.b_wg = [Buf("wg") for _ in range(NCH)]
    w.b_wb = [Buf("wb") for _ in range(3)]
    w.b_wo = [Buf("wo") for _ in range(NCH)]
    return w


def merge_weights_load(cx, l, dr, w):
    sc = cx.sc
    for kc in range(NCH):
        for cc in range(2):
            sc.dma("pool", w.wg[:, kc, cc * 1536:(cc + 1) * 1536],
                   dr.w_gate[l, kc * 128:(kc + 1) * 128, cc * 1536:(cc + 1) * 1536], w.b_wg[kc], writes=[w.b_wg[kc]])
    for n in range(3):
        for k4 in range(4):
            sc.dma("pool", w.wb[:, n, k4, :], dr.w_branch[l, n, k4 * 128:(k4 + 1) * 128, :], w.b_wb[n], writes=[w.b_wb[n]])
    for kc in range(NCH):
        sc.dma("pool", w.wo[:, kc, :], dr.w_out[l, kc * 128:(kc + 1) * 128, :], w.b_wo[kc], writes=[w.b_wo[kc]])


def phase_merge(cx, l, xT_in, xT_out, dr, w=None):
    nc, sc = cx.nc, cx.sc
    TT = 256
    NT = int(os.environ.get("MERGE_NT", S // TT))
    C = cx.modC[l][1]
    with ExitStack() as es:
        if w is None:
            w = merge_weights_alloc(cx, es)
            merge_weights_load(cx, l, dr, w)
        wg, wb, wo, b_wg, b_wb, b_wo = w.wg, w.wb, w.wo, w.b_wg, w.b_wb, w.b_wo
        xt = Ring([_alloc(nc, es, "m_x%d" % k, [128, NCH, TT], F32) for k in range(2)], "x")
        hT = Ring([_alloc(nc, es, "m_h%d" % k, [128, NCH, TT], BF16) for k in range(2)], "h")
        ys = Ring([_alloc(nc, es, "m_ys%d" % k, [128, 3, 4, TT], BF16) for k in range(2)], "ys")
        mg = _alloc(nc, es, "m_mg", [128, NCH, TT], BF16)
        yT = _alloc(nc, es, "m_y", [128, NCH, TT], F32)
        sq = _alloc(nc, es, "m_sq", [128, NCH, TT], BF16)
        rstd = _alloc(nc, es, "m_rstd", [128, TT], F32)
        lnt = _alloc(nc, es, "m_lnt", [128, TT], F32)
        sg = Ring([_alloc(nc, es, "m_sg%d" % k, [128, TT], F32) for k in range(3)], "sg")
        ac = Ring([_alloc(nc, es, "m_ac%d" % k, [128, TT], F32) for k in range(2)], "ac")
        t2 = Ring([_alloc(nc, es, "m_t2%d" % k, [128, TT], F32) for k in range(2)], "t2")
        pG = Ring([_palloc(nc, es, "m_pG%d" % k, [128, 512], F32) for k in range(2)], "pG", True)
        pB = Ring([_palloc(nc, es, "m_pB%d" % k, [128, 512], F32) for k in range(2)], "pB", True)
        pY = Ring([_palloc(nc, es, "m_pY%d" % k, [128, 512], F32) for k in range(2)], "pY", True)
        pS = _palloc(nc, es, "m_pS", [128, 512], F32)
        b_pS = Buf("pS", True)
        b_mg = [Buf("mg") for _ in range(NCH)]
        b_yc = [Buf("yc") for _ in range(NCH)]
        b_sq, b_rstd = Buf("sq"), Buf("rstd")
        ysrc = (dr.yaT, dr.ybT, dr.ycT)

        def load(t):
            t0 = t * TT
            X, b_X = xt.next()
            H, b_H = hT.next()
            Y, b_Y = ys.next()
            sc.dma("sp", X[:], xT_in[:, t0:t0 + TT].rearrange("(c p) n -> p c n", p=128), b_X, writes=[b_X])
            sc.dma("sp", H[:], dr.hT[:, t0:t0 + TT].rearrange("(c p) n -> p c n", p=128), b_H, writes=[b_H])
            for n in range(3):
                sc.dma("sp", Y[:, n, :, :], ysrc[n][:, t0:t0 + TT].rearrange("(c p) n -> p c n", p=128), b_Y, writes=[b_Y])
            return X, b_X, H, b_H, Y, b_Y

        nxt = load(0)
        for t in range(NT):
            X, b_X, H, b_H, Y, b_Y = nxt
            if t + 1 < NT:
                nxt = load(t + 1)
            t0 = t * TT
            for oc in range(NCH):
                AC, b_AC = ac.next()
                for n in range(3):
                    G_, b_G = pG.next()
                    for kc in range(NCH):
                        sc.op("pe", lambda e, kc=kc: e.matmul(G_[:, 0:TT], wg[:, kc, n * D + oc * 128:n * D + (oc + 1) * 128], H[:, kc, :],
                                                              start=(kc == 0), stop=(kc == NCH - 1)),
                              reads=[b_wg[kc], b_H], writes=[b_G])
                    B_, b_B = pB.next()
                    for k4 in range(4):
                        sc.op("pe", lambda e, k4=k4: e.matmul(B_[:, 0:TT], wb[:, n, k4, oc * 128:(oc + 1) * 128], Y[:, n, k4, :],
                                                              start=(k4 == 0), stop=(k4 == 3)),
                              reads=[b_wb[n], b_Y], writes=[b_B])
                    SG, b_SG = sg.next()
                    sc.op("act", lambda e: e.activation(out=SG[:], in_=G_[:, 0:TT], func=AF.Sigmoid), reads=[b_G], writes=[b_SG])
                    if n == 0:
                        sc.op("dve", lambda e: e.tensor_tensor(out=AC[:], in0=B_[:, 0:TT], in1=SG[:], op=ALU.mult),
                              reads=[b_B, b_SG], writes=[b_AC])
                    else:
                        T2, b_T2 = t2.next()
                        sc.op("dve", lambda e: e.tensor_tensor(out=T2[:], in0=B_[:, 0:TT], in1=SG[:], op=ALU.mult),
                              reads=[b_B, b_SG], writes=[b_T2])
                        if n == 1:
                            sc.op("pool", lambda e: e.tensor_tensor(out=AC[:], in0=AC[:], in1=T2[:], op=ALU.add),
                                  reads=[b_AC, b_T2], writes=[b_AC])
                        else:
                            sc.op("pool", lambda e: e.tensor_tensor(out=mg[:, oc, :], in0=AC[:], in1=T2[:], op=ALU.add),
                                  reads=[b_AC, b_T2], writes=[b_mg[oc]])
            for oc2 in range(NCH):
                Y_, b_Yp = pY.next()
                for oc in range(NCH):
                    sc.op("pe", lambda e, oc=oc: e.matmul(Y_[:, 0:TT], wo[:, oc, oc2 * 128:(oc2 + 1) * 128], mg[:, oc, :],
                                                          start=(oc == 0), stop=(oc == NCH - 1)),
                          reads=[b_wo[oc], b_mg[oc]], writes=[b_Yp])
                sc.op("dve", lambda e: e.tensor_copy(out=yT[:, oc2, :], in_=Y_[:, 0:TT]), reads=[b_Yp], writes=[b_yc[oc2]])
                sc.op("act", lambda e: e.activation(out=sq[:, oc2, :], in_=yT[:, oc2, :], func=AF.Square),
                      reads=[b_yc[oc2]], writes=[b_sq])
            emit_rstd(cx, sq, b_sq, pS, b_pS, rstd, b_rstd, lnt, TT)
            for ch in range(NCH):
                T2, b_T2 = t2.next()
                sc.op("dve", lambda e: e.scalar_tensor_tensor(out=T2[:], in0=yT[:, ch, :], scalar=C[:, ch:ch + 1], in1=rstd[:],
                                                              op0=ALU.mult, op1=ALU.mult),
                      reads=[b_yc[ch], b_rstd, cx.b_mod], writes=[b_T2])
                sc.op("pool", lambda e: e.tensor_tensor(out=X[:, ch, :], in0=X[:, ch, :], in1=T2[:], op=ALU.add),
                      reads=[b_X, b_T2], writes=[b_X])
            sc.dma("sp", xT_out[:, t0:t0 + TT].rearrange("(c p) n -> p c n", p=128), X[:], b_X, reads=[b_X])
        sc.barrier()


def host_constants():
    ct = {}
    ct["ident_in"] = np.eye(128, dtype=np.float32)
    k = np.arange(128)
    ct["triL_in"] = (k[:, None] <= k[None, :]).astype(np.float32)
    ct["triU_in"] = (k[:, None] > k[None, :]).astype(np.float32)
    pos = np.arange(S, dtype=np.float32)
    inv = (1.0 / (np.float32(500000.0) ** (np.arange(0, 16, 2, dtype=np.float32) / np.float32(16)))).astype(np.float32)
    ang = pos[:, None] * inv[None, :]
    cos, sin = np.cos(ang).astype(np.float32), np.sin(ang).astype(np.float32)
    C = np.ones((64, S), np.float32)
    Sn = np.zeros((64, S), np.float32)
    C[0:8] = cos.T
    C[8:16] = cos.T
    Sn[0:8] = -sin.T
    Sn[8:16] = sin.T
    ct["ropeC"] = np.ascontiguousarray(np.concatenate([C, C], 0))
    ct["ropeS"] = np.ascontiguousarray(np.concatenate([Sn, Sn], 0))
    ct["Esel"] = (np.arange(64)[:, None] == (np.arange(S)[None, :] // 64)).astype(np.float32)
    n = np.arange(256)
    cm = ((n[:, None] * 16 + 31) <= np.arange(S)[None, :]) & (n[:, None] < 255)
    ct["cmpmask"] = cm.astype(np.float32)
    ncb = S // 16 - 1
    cs = np.arange(ncb)[:, None] * 16
    ss = np.arange(64)[None, :] * 64
    ov = np.clip(np.minimum(cs + 32, ss + 64) - np.maximum(cs, ss), 0, None)
    sm = np.zeros((256, 64), np.float32)
    sm[:ncb] = ov / 32.0
    ct["slcmap"] = sm
    t = np.arange(S)
    cur = t // 64
    blk = np.arange(64)
    forced = (blk[None, :] == 0) | (blk[None, :] == cur[:, None]) | (blk[None, :] == cur[:, None] - 1)
    causal = blk[None, :] <= cur[:, None]
    ct["cmask"] = (causal & ~forced).astype(np.float32)
    ct["cbias"] = np.where(forced, 1e4, np.where(causal, 0.0, -1.0)).astype(np.float32)
    return ct


def relayout_w_in(w_in):
    perm = np.arange(64)
    perm[0:8] = np.arange(8, 16)
    perm[8:16] = np.arange(0, 8)
    cols = []
    cols += list(range(0, 512))
    cols += list(range(OFF_Q, OFF_Q + 512))
    cols += [OFF_Q + h * 64 + perm[d] for h in range(8) for d in range(64)]
    kv = lambda i: OFF_KV + i * 128
    cols += list(range(kv(0), kv(0) + 128))
    cols += list(range(kv(1), kv(1) + 128))
    cols += list(range(kv(2), kv(2) + 128))
    cols += [kv(2) + g * 64 + perm[d] for g in range(2) for d in range(64)]
    cols += list(range(kv(4), kv(4) + 128))
    cols += [kv(4) + g * 64 + perm[d] for g in range(2) for d in range(64)]
    cols += list(range(OFF_C + 512, OFF_C + 1024))
    cols += list(range(OFF_C + 1024, OFF_C + 1536))
    cols += list(range(OFF_C, OFF_C + 512))
    cols += list(range(512, 1024))
    cols += list(range(kv(3), kv(3) + 128))
    cols += list(range(kv(5), kv(5) + 128))
    cols += list(range(OFF_NG, OFF_NG + 24))
    cols = np.asarray(cols)
    assert cols.size == WP_COLS
    return np.ascontiguousarray(w_in[:, :, cols])


class DR:
    pass


def build_program(upto="all", debug=False):
    nc = bass.Bass("TRN2", target_bir_lowering=False)
    cx = Ctx()
    cx.nc = nc
    es = ExitStack()
    cx.es = es
    sc = Sched(nc, es)
    cx.sc = sc
    dr = DR()

    def din(name, shape):
        return nc.dram_tensor(name, list(shape), F32, kind="ExternalInput").ap()

    kind_i = "ExternalOutput" if debug else "Internal"

    def dscr(name, shape, dt=BF16):
        return nc.dram_tensor(name, list(shape), dt, kind=kind_i).ap()

    x = din("x", [S, D])
    c = din("c", [1, D])
    mod_w = din("mod_w", [L, D, 9 * D])
    mod_b = din("mod_b", [L, 9 * D])
    norm_g = din("norm_g", [L, 6, D])
    ffn_w13 = din("ffn_w13", [L, 2, D, 2 * DFF])
    ffn_w2 = din("ffn_w2", [L, 2, DFF, D])
    dr.wp = din("wp", [L, D, WP_COLS])
    dr.gm_ln_g = din("gm_ln_g", [L, 512])
    dr.gm_ln_b = din("gm_ln_b", [L, 512])
    dr.gm_ws = din("gm_ws", [L, 4, 128, 128])
    dr.gm_bs = din("gm_bs", [L, 4, 128])
    dr.cmp_peT = din("cmp_peT", [L, 2, 64, 32])
    dr.cmp_w1 = din("cmp_w1", [L, 2, 2048, 128])
    dr.cmp_w2 = din("cmp_w2", [L, 2, 128, 64])
    dr.conv_w = din("conv_w", [L, 3, 512])
    dr.w_branch = din("w_branch", [L, 3, 512, D])
    dr.w_gate = din("w_gate", [L, D, 3 * D])
    dr.w_out = din("w_out", [L, D, D])
    ident_d = din("ident_in", [128, 128])
    triL_d = din("triL_in", [128, 128])
    triU_d = din("triU_in", [128, 128])
    dr.ropeC = din("ropeC", [128, S])
    dr.ropeS = din("ropeS", [128, S])
    dr.Esel = din("Esel", [64, S])
    dr.cmpmask = din("cmpmask", [256, S])
    dr.slcmap = din("slcmap", [256, 64])
    dr.cmask = din("cmask", [S, 64])
    dr.cbias = din("cbias", [S, 64])
    out = nc.dram_tensor("out", [S, D], F32, kind="ExternalOutput").ap()
    xT = [nc.dram_tensor("xT%d" % i, [D, S], F32, kind=kind_i).ap() for i in range(2)]
    dr.hT = dscr("hT_d", [D, S])
    dr.yaT = dscr("yaT_d", [512, S])
    dr.ybT = dscr("ybT_d", [512, S])
    dr.ycT = dscr("ycT_d", [512, S])
    dr.qrT = dscr("qrT_d", [512, S])
    dr.qnT = dscr("qnT_d", [512, S])
    dr.kT = dscr("kT_d", [2, 128, S])
    dr.kcmpT = dscr("kcmpT_d", [2, 128, S])
    dr.v = dscr("v_d", [S, 256])
    dr.ng = dscr("ng_d", [S, 24], F32)
    dr.kcT = dscr("kcT_d", [2, 64, 256])
    dr.vc = dscr("vc_d", [2, 256, 64])
    cx.b_x = Buf("x")
    cx.b_out = Buf("out")
    cx.b_xT = [Buf("xT") for _ in range(S // 512)]
    bx = [Buf("xTd") for _ in range(S // 512)]

    cx.ident = _alloc(nc, es, "ident", [128, 128], F32)
    cx.triL = _alloc(nc, es, "triL", [128, 128], F32)
    cx.triU = _alloc(nc, es, "triU", [128, 128], F32)
    cx.ones_bf = _alloc(nc, es, "ones_bf", [128, 128], BF16)
    cx.eps_col = _alloc(nc, es, "eps_col", [128, 1], F32)
    cx.b_const = Buf("const")
    cx.b_mod = Buf("mod")
    cx.modT = [_alloc(nc, es, "modT%d" % l, [128, 72], F32) for l in range(L)]
    cx.modbT = [_alloc(nc, es, "modbT%d" % l, [128, 72], F32) for l in range(L)]
    cx.normgT = [_alloc(nc, es, "normgT%d" % l, [128, 48], F32) for l in range(L)]
    cx.modA = [[_alloc(nc, es, "modA%d_%d" % (l, s_), [128, 8], F32) for s_ in range(3)] for l in range(L)]
    cx.modB = [[_alloc(nc, es, "modB%d_%d" % (l, s_), [128, 8], F32) for s_ in range(3)] for l in range(L)]
    cx.modC = [[_alloc(nc, es, "modC%d_%d" % (l, s_), [128, 8], F32) for s_ in range(3)] for l in range(L)]
    sc.dma("sp", cx.ident[:], ident_d, cx.b_const, writes=[cx.b_const])
    sc.dma("sp", cx.triL[:], triL_d, cx.b_const, writes=[cx.b_const])
    sc.dma("sp", cx.triU[:], triU_d, cx.b_const, writes=[cx.b_const])
    sc.op("dve", lambda e: e.memset(cx.ones_bf[:], 1.0), writes=[cx.b_const])
    sc.op("dve", lambda e: e.memset(cx.eps_col[:], EPS), writes=[cx.b_const])

    stages = upto.split(",")

    def want(nm):
        return upto == "all" or nm in stages

    phase_transpose_in(cx, x, xT[0], side=phase_mod(cx, c, mod_w, mod_b, norm_g))
    if debug:
        dbgm = nc.dram_tensor("dbg_mod", [128, 72 + 24], F32, kind="ExternalOutput").ap()
        bd = Buf("dbg")
        sc.dma("sp", dbgm[:, 0:72], cx.modT[0][:], bd, reads=[cx.b_mod])
        for s_ in range(3):
            sc.dma("sp", dbgm[:, 72 + s_ * 8:80 + s_ * 8], cx.modA[0][s_][:], bd, reads=[cx.b_mod])
    cur = 0
    for l in range(L):
        if want("ffn%d0" % l):
            phase_ffn(cx, l, 0, ffn_w13[l, 0], ffn_w2[l, 0], xT[cur], bx, xT[1 - cur], bx)
            cur = 1 - cur
        if want("proj%d" % l):
            phase_proj(cx, l, xT[cur], dr)
        if want("cmp%d" % l):
            phase_cmp(cx, l, dr)
        if want("att%d" % l) and want("merge%d" % l):
            with ExitStack() as es2:
                mw = merge_weights_alloc(cx, es2)
                phase_att(cx, l, dr, after_setup=lambda: merge_weights_load(cx, l, dr, mw))
                phase_merge(cx, l, xT[cur], xT[1 - cur], dr, w=mw)
            cur = 1 - cur
        else:
            if want("att%d" % l):
                phase_att(cx, l, dr)
            if want("merge%d" % l):
                phase_merge(cx, l, xT[cur], xT[1 - cur], dr)
                cur = 1 - cur
        if want("ffn%d1" % l):
            last = (l == L - 1)
            phase_ffn(cx, l, 1, ffn_w13[l, 1], ffn_w2[l, 1], xT[cur], bx, xT[1 - cur], bx, out_tok=(out if last else None))
            cur = 1 - cur
    sc.finish()
    es.close()
    return nc


def make_in_maps(inputs, cores):
    ct = host_constants()
    wp = relayout_w_in(np.asarray(inputs["w_in"], np.float32))
    peT = np.ascontiguousarray(np.transpose(np.asarray(inputs["cmp_pe"], np.float32), (0, 1, 3, 2)))
    shared = {k: np.ascontiguousarray(np.asarray(inputs[k], np.float32)) for k in
              ("mod_w", "mod_b", "norm_g", "ffn_w13", "ffn_w2", "gm_ln_g", "gm_ln_b", "gm_ws", "gm_bs",
               "cmp_w1", "cmp_w2", "conv_w", "w_branch", "w_gate", "w_out")}
    shared["wp"] = wp
    shared["cmp_peT"] = peT
    shared.update(ct)
    maps = []
    for b in cores:
        m = dict(shared)
        m["x"] = np.ascontiguousarray(np.asarray(inputs["x"][b], np.float32))
        m["c"] = np.ascontiguousarray(np.asarray(inputs["c"][b:b + 1], np.float32))
        maps.append(m)
    return maps


def kernel(**inputs):
    nc = build_program("all", debug=False)
    maps = make_in_maps(inputs, list(range(8)))
    res = run_bass_kernel_spmd(nc, maps, core_ids=list(range(8)))
    return np.stack([np.asarray(r["out"], np.float32) for r in res.results], 0)
```

```python
import os
import numpy as np
from contextlib import ExitStack
import concourse.bass as bass
import concourse.mybir as mybir
from concourse.bass_utils import run_bass_kernel_spmd

F32 = mybir.dt.float32
BF16 = mybir.dt.bfloat16
AF = mybir.ActivationFunctionType
ALU = mybir.AluOpType
AX = mybir.AxisListType

S = 4096
D = 1024
DFF = 2816
L = 2
NCH = D // 128
NJ = DFF // 128
EPS = 1e-6
BW = 512
A_COLS = 1024
Q_COLS = 512
KV_COLS = 768
NG_COLS = 24
C_COLS = 1536
IN_COLS = A_COLS + Q_COLS + KV_COLS + NG_COLS + C_COLS
OFF_Q = A_COLS
OFF_KV = OFF_Q + Q_COLS
OFF_NG = OFF_KV + KV_COLS
OFF_C = OFF_NG + NG_COLS


class Buf:
    __slots__ = ("name", "w", "r", "sem", "last_dma", "excl")

    def __init__(self, name, excl=False):
        self.name = name
        self.excl = excl
        self.w = None
        self.r = []
        self.sem = None
        self.last_dma = None


class Sched:
    ENGS = ("pe", "act", "dve", "pool")

    def __init__(self, nc, es):
        self.nc = nc
        self.es = es
        self.eng = {"pe": nc.tensor, "act": nc.scalar, "dve": nc.vector, "pool": nc.gpsimd, "sp": nc.sync}
        self.cnt = {e: 0 for e in self.ENGS}
        self.esem = {e: es.enter_context(nc.semaphore("sem_" + e)) for e in self.ENGS}
        self.known = {e: {} for e in self.eng}
        self.free_sems = []
        self.all_dsems = []
        self.live_bufs = []
        self.nsem = 0

    def _waits(self, eng, toks):
        need = {}
        for t in toks:
            cur = need.get(t[0])
            if cur is None or cur[1] < t[2]:
                need[t[0]] = (t[1], t[2])
        kn = self.known[eng]
        e = self.eng[eng]
        for key, (sem, val) in need.items():
            if kn.get(key, 0) >= val:
                continue
            kn[key] = val
            e.wait_ge(sem, val)

    def _deps(self, eng, reads, writes):
        toks = []
        for b in reads:
            if b.w is not None and not (eng == "pe" and b.w[3] == "pe"):
                toks.append(b.w)
        for b in writes:
            if b.w is not None and b.w[3] != eng:
                toks.append(b.w)
            for r in b.r:
                if r[3] != eng:
                    toks.append(r)
        return toks

    def _update(self, tok, reads, writes):
        for b in reads:
            if b not in writes:
                b.r.append(tok)
        for b in writes:
            b.w = tok
            b.r = []

    def op(self, eng, fn, reads=(), writes=()):
        if any(b.excl for b in reads):
            writes = list(writes) + [b for b in reads if b.excl]
            reads = [b for b in reads if not b.excl]
        self._waits(eng, self._deps(eng, reads, writes))
        ins = fn(self.eng[eng])
        self.cnt[eng] += 1
        ins.then_inc(self.esem[eng], 1)
        tok = (eng, self.esem[eng], self.cnt[eng], eng)
        self._update(tok, reads, writes)
        return tok

    def _get_sem(self, b):
        if b.sem is None:
            if self.free_sems:
                b.sem = self.free_sems.pop()
            else:
                self.nsem += 1
                s = self.es.enter_context(self.nc.semaphore("dsem%d" % self.nsem))
                b.sem = [s, 0]
                self.all_dsems.append(b.sem)
            self.live_bufs.append(b)
        return b.sem

    def dma(self, queue, out, in_, sbuf, reads=(), writes=(), **kw):
        toks = self._deps("dma", reads, writes)
        if sbuf.last_dma is not None:
            toks.append(sbuf.last_dma)
        self._waits(queue, toks)
        sm = self._get_sem(sbuf)
        ins = self.eng[queue].dma_start(out=out, in_=in_, **kw)
        sm[1] += 16
        ins.then_inc(sm[0], 16)
        tok = (id(sm), sm[0], sm[1], "dma")
        sbuf.last_dma = tok
        self._update(tok, reads, writes)
        return tok

    def barrier(self):
        toks = [(e, self.esem[e], self.cnt[e], e) for e in self.ENGS if self.cnt[e] > 0]
        for sm in self.all_dsems:
            if sm[1] > 0:
                toks.append((id(sm), sm[0], sm[1], "dma"))
        for e in self.eng:
            self._waits(e, [t for t in toks if t[0] != e])
        for b in self.live_bufs:
            if b.sem is not None:
                self.free_sems.append(b.sem)
                b.sem = None
        self.live_bufs = []

    def finish(self):
        toks = []
        for sm in self.all_dsems:
            if sm[1] > 0:
                toks.append((id(sm), sm[0], sm[1], "dma"))
        toks += [(e, self.esem[e], self.cnt[e], e) for e in self.ENGS if self.cnt[e] > 0]
        self._waits("sp", toks)


class Ctx:
    pass


_UID = [0]


def _alloc(nc, es, name, shape, dt):
    _UID[0] += 1
    return es.enter_context(nc.sbuf_tensor("%s_%d" % (name, _UID[0]), list(shape), dt))


def _palloc(nc, es, name, shape, dt):
    _UID[0] += 1
    return es.enter_context(nc.psum_tensor("%s_%d" % (name, _UID[0]), list(shape), dt))


def phase_transpose_in(cx, x_ap, xT_ap, side=None):
    nc, sc = cx.nc, cx.sc
    with ExitStack() as es:
        NB = 2
        xin = [_alloc(nc, es, "ti_x%d" % i, [128, 4, D], F32) for i in range(NB)]
        xo = [_alloc(nc, es, "ti_o%d" % i, [128, NCH, 512], F32) for i in range(NB)]
        ps = [_palloc(nc, es, "ti_p%d" % i, [128, 512], F32) for i in range(4)]
        b_in = [Buf("ti_x") for _ in range(NB)]
        b_o = [Buf("ti_o") for _ in range(NB)]
        b_ps = [Buf("ti_p", True) for _ in range(4)]
        pi = 0
        for t in range(S // 512):
            k = t % NB
            src = x_ap[t * 512:(t + 1) * 512, :].rearrange("(a p) f -> p a f", p=128)
            sc.dma("sp", xin[k][:], src, b_in[k], reads=[cx.b_x], writes=[b_in[k]])
            for ch in range(NCH):
                pb = pi % 4
                pi += 1
                for a in range(4):
                    sc.op("pe", lambda e, a=a, ch=ch, pb=pb, k=k: e.transpose(
                        ps[pb][:, a * 128:(a + 1) * 128], xin[k][:, a, ch * 128:(ch + 1) * 128], cx.ident[:]),
                        reads=[b_in[k], cx.b_const], writes=[b_ps[pb]])
                if ch % 2 == 0:
                    sc.op("dve", lambda e, ch=ch, pb=pb, k=k: e.tensor_copy(out=xo[k][:, ch, :], in_=ps[pb][:]),
                          reads=[b_ps[pb]], writes=[b_o[k]])
                else:
                    sc.op("act", lambda e, ch=ch, pb=pb, k=k: e.copy(out=xo[k][:, ch, :], in_=ps[pb][:]),
                          reads=[b_ps[pb]], writes=[b_o[k]])
            dst = xT_ap[:, t * 512:(t + 1) * 512].rearrange("(c p) n -> p c n", p=128)
            sc.dma("sp", dst, xo[k][:], b_o[k], reads=[b_o[k]], writes=[cx.b_xT[t]])
            if side is not None:
                for _ in range(3):
                    next(side, None)
        if side is not None:
            for _ in side:
                pass
        sc.barrier()


def phase_mod(cx, c_ap, mod_w_ap, mod_b_ap, norm_g_ap):
    nc, sc = cx.nc, cx.sc
    with ExitStack() as es:
        crow = _alloc(nc, es, "md_crow", [8, 128], F32)
        cT = _alloc(nc, es, "md_cT", [128, 8], F32)
        sg = _alloc(nc, es, "md_sg", [128, 8], F32)
        brow = _alloc(nc, es, "md_brow", [72, 128], F32)
        grow = _alloc(nc, es, "md_grow", [48, 128], F32)
        wbuf = [_alloc(nc, es, "md_w%d" % i, [128, NCH, 1024], F32) for i in range(2)]
        pt = _palloc(nc, es, "md_pt", [128, 512], F32)
        pm = _palloc(nc, es, "md_pm", [128, 512], F32)
        b_crow, b_cT, b_brow, b_grow = (Buf(n) for n in ("crow", "cT", "brow", "grow"))
        b_pt, b_pm = Buf("pt", True), Buf("pm", True)
        b_w = [Buf("md_w") for _ in range(2)]
        b_mod = cx.b_mod
        sc.dma("sp", crow[:], c_ap.rearrange("o (a p) -> (o a) p", p=128), b_crow, writes=[b_crow])
        sc.op("pe", lambda e: e.transpose(pt[:, 0:8], crow[:], cx.ident[0:8, 0:8]),
              reads=[b_crow, cx.b_const], writes=[b_pt])
        sc.op("act", lambda e: e.activation(out=sg[:], in_=pt[:, 0:8], func=AF.Sigmoid), reads=[b_pt], writes=[b_cT])
        sc.op("dve", lambda e: e.tensor_tensor(out=cT[:], in0=pt[:, 0:8], in1=sg[:], op=ALU.mult),
              reads=[b_pt, b_cT], writes=[b_cT])
        for l in range(L):
            sc.dma("sp", brow[:], mod_b_ap[l].rearrange("(a p) -> a p", p=128), b_brow, writes=[b_brow])
            sc.dma("sp", grow[:], norm_g_ap[l].rearrange("k (a p) -> (k a) p", p=128), b_grow, writes=[b_grow])
            sc.op("pe", lambda e: e.transpose(pt[:, 0:72], brow[:], cx.ident[0:72, 0:72]),
                  reads=[b_brow, cx.b_const], writes=[b_pt])
            sc.op("dve", lambda e, l=l: e.tensor_copy(out=cx.modbT[l][:], in_=pt[:, 0:72]), reads=[b_pt], writes=[b_mod])
            sc.op("pe", lambda e: e.transpose(pt[:, 0:48], grow[:], cx.ident[0:48, 0:48]),
                  reads=[b_grow, cx.b_const], writes=[b_pt])
            sc.op("dve", lambda e, l=l: e.tensor_copy(out=cx.normgT[l][:], in_=pt[:, 0:48]), reads=[b_pt], writes=[b_mod])
            for v in range(9):
                k = v % 2
                src = mod_w_ap[l][:, v * 1024:(v + 1) * 1024].rearrange("(a p) n -> p a n", p=128)
                sc.dma("pool", wbuf[k][:], src, b_w[k], writes=[b_w[k]])
                for ch in range(NCH):
                    col = v * 8 + ch
                    for kc in range(NCH):
                        sc.op("pe", lambda e, k=k, ch=ch, kc=kc, col=col: e.matmul(
                            pm[:, col:col + 1], wbuf[k][:, kc, ch * 128:(ch + 1) * 128], cT[:, kc:kc + 1],
                            start=(kc == 0), stop=(kc == NCH - 1)),
                            reads=[b_w[k], b_cT], writes=[b_pm])
                yield
            sc.op("dve", lambda e, l=l: e.tensor_tensor(out=cx.modT[l][:], in0=pm[:, 0:72], in1=cx.modbT[l][:], op=ALU.add),
                  reads=[b_pm, b_mod], writes=[b_mod])
            for s_ in range(3):
                g0 = cx.normgT[l][:, (2 * s_) * 8:(2 * s_ + 1) * 8]
                g1 = cx.normgT[l][:, (2 * s_ + 1) * 8:(2 * s_ + 2) * 8]
                shift = cx.modT[l][:, (3 * s_) * 8:(3 * s_ + 1) * 8]
                scale = cx.modT[l][:, (3 * s_ + 1) * 8:(3 * s_ + 2) * 8]
                gate = cx.modT[l][:, (3 * s_ + 2) * 8:(3 * s_ + 3) * 8]
                resw = 1.0 if s_ == 1 else 0.5
                sc.op("dve", lambda e, l=l, s_=s_, scale=scale, g0=g0: e.scalar_tensor_tensor(
                    out=cx.modA[l][s_][:], in0=scale, scalar=1.0, in1=g0, op0=ALU.add, op1=ALU.mult),
                    reads=[b_mod], writes=[b_mod])
                sc.op("dve", lambda e, l=l, s_=s_, shift=shift: e.tensor_copy(out=cx.modB[l][s_][:], in_=shift),
                      reads=[b_mod], writes=[b_mod])
                sc.op("dve", lambda e, l=l, s_=s_, gate=gate, g1=g1, resw=resw: e.scalar_tensor_tensor(
                    out=cx.modC[l][s_][:], in0=gate, scalar=resw, in1=g1, op0=ALU.mult, op1=ALU.mult),
                    reads=[b_mod], writes=[b_mod])
        yield


def emit_rstd(cx, sq, b_sq, ps, b_ps, rstd, b_rstd, lnt, TT):
    sc = cx.sc
    for ch in range(NCH):
        sc.op("pe", lambda e, ch=ch: e.matmul(ps[:, 0:TT], cx.ones_bf[:], sq[:, ch, :],
                                              start=(ch == 0), stop=(ch == NCH - 1)),
              reads=[b_sq, cx.b_const], writes=[b_ps])
    sc.op("act", lambda e: e.activation(out=lnt[:, 0:TT], in_=ps[:, 0:TT], func=AF.Ln, scale=1.0 / D, bias=cx.eps_col[:]),
          reads=[b_ps, cx.b_const], writes=[b_rstd])
    sc.op("act", lambda e: e.activation(out=rstd[:, 0:TT], in_=lnt[:, 0:TT], func=AF.Exp, scale=-0.5),
          reads=[b_rstd], writes=[b_rstd])


def phase_ffn(cx, l, i, w13_ap, w2_ap, xT_in, b_xin, xT_out, b_xout, out_tok=None):
    nc, sc = cx.nc, cx.sc
    TT = 256
    import os
    NT = int(os.environ.get('FFN_NT', S // TT))
    s_ = 0 if i == 0 else 2
    A, B, C = cx.modA[l][s_], cx.modB[l][s_], cx.modC[l][s_]
    with ExitStack() as es:
        w13s = _alloc(nc, es, "f_w13", [128, NCH, 2 * DFF], BF16)
        w2s = _alloc(nc, es, "f_w2", [128, NJ, D], BF16)
        xt = [_alloc(nc, es, "f_x%d" % k, [128, NCH, TT], F32) for k in range(2)]
        hT = _alloc(nc, es, "f_h", [128, NCH, TT], BF16)
        hT2 = _alloc(nc, es, "f_h2", [128, NCH, TT], BF16)
        hid = _alloc(nc, es, "f_hid", [128, NJ, TT], BF16)
        yT = _alloc(nc, es, "f_y", [128, NCH, TT], F32)
        sq = _alloc(nc, es, "f_sq", [128, NCH, TT], BF16)
        rstd = _alloc(nc, es, "f_rstd", [128, TT], F32)
        lnt = _alloc(nc, es, "f_lnt", [128, TT], F32)
        tmp = [_alloc(nc, es, "f_tmp%d" % k, [128, TT], F32) for k in range(2)]
        sgt = [_alloc(nc, es, "f_sg%d" % k, [128, TT], F32) for k in range(2)]
        if out_tok is not None:
            otok = [_alloc(nc, es, "f_ot%d" % k, [128, D], F32) for k in range(2)]
            b_otok = [Buf("otok") for _ in range(2)]
        pG = [_palloc(nc, es, "f_pg%d" % k, [128, 512], F32) for k in range(2)]
        pU = [_palloc(nc, es, "f_pu%d" % k, [128, 512], F32) for k in range(2)]
        pY = [_palloc(nc, es, "f_py%d" % k, [128, 512], F32) for k in range(2)]
        pS = _palloc(nc, es, "f_ps", [128, 512], F32)
        pT = _palloc(nc, es, "f_pt", [128, 512], F32)
        b_w13 = [Buf("w13") for _ in range(NCH)]
        b_w2 = [Buf("w2") for _ in range(NJ)]
        b_x = [Buf("x") for _ in range(2)]
        b_h, b_hid, b_y, b_sq, b_rstd = (Buf(n) for n in ("h", "hid", "y", "sq", "rstd"))
        b_pS, b_pT = Buf("pS", True), Buf("pT", True)
        b_hidj = [Buf("hidj") for _ in range(NJ)]
        b_yc = [Buf("yc") for _ in range(NCH)]
        b_tmp = [Buf("tmp") for _ in range(2)]
        b_sg = [Buf("sg") for _ in range(2)]
        b_pG = [Buf("pG", True) for _ in range(2)]
        b_pU = [Buf("pU", True) for _ in range(2)]
        b_pY = [Buf("pY", True) for _ in range(2)]

        CW = 1408
        b_w13 = [[Buf("w13") for _ in range(4)] for _ in range(NCH)]
        for cc in (0, 2, 1, 3):
            for kc in range(NCH):
                sc.dma("pool", w13s[:, kc, cc * CW:(cc + 1) * CW],
                       w13_ap[kc * 128:(kc + 1) * 128, cc * CW:(cc + 1) * CW], b_w13[kc][cc], writes=[b_w13[kc][cc]])
        for j in range(NJ):
            sc.dma("pool", w2s[:, j, :], w2_ap[j * 128:(j + 1) * 128, :], b_w2[j], writes=[b_w2[j]])

        def load_x(t):
            k = t % 2
            src = xT_in[:, t * TT:(t + 1) * TT].rearrange("(c p) n -> p c n", p=128)
            sc.dma("sp", xt[k][:], src, b_x[k], writes=[b_x[k]])

        hTs = [hT, hT2]
        b_hs = [b_h, Buf("h2")]

        def pre(t):
            k = t % 2
            X = xt[k]
            H, bH = hTs[k], b_hs[k]
            for ch in range(NCH):
                sc.op("pool", lambda e, ch=ch: e.tensor_tensor(out=sq[:, ch, :], in0=X[:, ch, :], in1=X[:, ch, :], op=ALU.mult),
                      reads=[b_x[k]], writes=[b_sq])
            emit_rstd(cx, sq, b_sq, pS, b_pS, rstd, b_rstd, lnt, TT)
            for ch in range(NCH):
                kk = ch % 2
                sc.op("dve", lambda e, ch=ch, kk=kk: e.scalar_tensor_tensor(
                    out=tmp[kk][:], in0=X[:, ch, :], scalar=A[:, ch:ch + 1], in1=rstd[:], op0=ALU.mult, op1=ALU.mult),
                    reads=[b_x[k], b_rstd, cx.b_mod], writes=[b_tmp[kk]])
                sc.op("dve", lambda e, ch=ch, kk=kk: e.tensor_scalar(
                    out=H[:, ch, :], in0=tmp[kk][:], scalar1=B[:, ch:ch + 1], scalar2=None, op0=ALU.add),
                    reads=[b_tmp[kk], cx.b_mod], writes=[bH])

        load_x(0)
        pre(0)
        for t in range(NT):
            k = t % 2
            if t + 1 < NT:
                load_x(t + 1)
            X = xt[k]
            H, bH = hTs[k], b_hs[k]
            for j in range(NJ):
                if j == NJ // 2 and t + 1 < NT:
                    pre(t + 1)
                kk = j % 2
                for kc in range(NCH):
                    sc.op("pe", lambda e, j=j, kc=kc, kk=kk: e.matmul(
                        pG[kk][:, 0:TT], w13s[:, kc, j * 128:(j + 1) * 128], H[:, kc, :],
                        start=(kc == 0), stop=(kc == NCH - 1)),
                        reads=[b_w13[kc][(j * 128) // 1408], bH], writes=[b_pG[kk]])
                for kc in range(NCH):
                    sc.op("pe", lambda e, j=j, kc=kc, kk=kk: e.matmul(
                        pU[kk][:, 0:TT], w13s[:, kc, DFF + j * 128:DFF + (j + 1) * 128], H[:, kc, :],
                        start=(kc == 0), stop=(kc == NCH - 1)),
                        reads=[b_w13[kc][2 + (j * 128) // 1408], bH], writes=[b_pU[kk]])
                sc.op("act", lambda e, kk=kk: e.activation(out=sgt[kk][:], in_=pG[kk][:, 0:TT], func=AF.Silu),
                      reads=[b_pG[kk]], writes=[b_sg[kk]])
                sc.op("dve", lambda e, j=j, kk=kk: e.tensor_tensor(out=hid[:, j, :], in0=pU[kk][:, 0:TT], in1=sgt[kk][:],
                                                                   op=ALU.mult),
                      reads=[b_pU[kk], b_sg[kk]], writes=[b_hidj[j]])
            STG = 9
            for oc in range(NCH if STG >= 3 else 0):
                kk = oc % 2
                for j in range(NJ):
                    sc.op("pe", lambda e, j=j, oc=oc, kk=kk: e.matmul(
                        pY[kk][:, 0:TT], w2s[:, j, oc * 128:(oc + 1) * 128], hid[:, j, :],
                        start=(j == 0), stop=(j == NJ - 1)),
                        reads=[b_w2[j], b_hidj[j]], writes=[b_pY[kk]])
                sc.op("dve", lambda e, oc=oc, kk=kk: e.tensor_copy(out=yT[:, oc, :], in_=pY[kk][:, 0:TT]),
                      reads=[b_pY[kk]], writes=[b_yc[oc]])
                sc.op("act", lambda e, oc=oc: e.activation(out=sq[:, oc, :], in_=yT[:, oc, :], func=AF.Square),
                      reads=[b_yc[oc]], writes=[b_sq])
            if STG >= 4:
                emit_rstd(cx, sq, b_sq, pS, b_pS, rstd, b_rstd, lnt, TT)
            for ch in range(NCH if STG >= 4 else 0):
                kk = ch % 2
                sc.op("dve", lambda e, ch=ch, kk=kk: e.scalar_tensor_tensor(
                    out=tmp[kk][:], in0=yT[:, ch, :], scalar=C[:, ch:ch + 1], in1=rstd[:], op0=ALU.mult, op1=ALU.mult),
                    reads=[b_yc[ch], b_rstd, cx.b_mod], writes=[b_tmp[kk]])
                sc.op("pool", lambda e, ch=ch, X=X, kk=kk: e.tensor_tensor(out=X[:, ch, :], in0=X[:, ch, :], in1=tmp[kk][:],
                                                                           op=ALU.add),
                      reads=[b_x[k], b_tmp[kk]], writes=[b_x[k]])
            if out_tok is None:
                dst = xT_out[:, t * TT:(t + 1) * TT].rearrange("(c p) n -> p c n", p=128)
                sc.dma("sp", dst, X[:], b_x[k], reads=[b_x[k]])
            else:
                for a in range(TT // 128):
                    ko = (t * (TT // 128) + a) % 2
                    for half in range(2):
                        for c4 in range(4):
                            ch = half * 4 + c4
                            sc.op("pe", lambda e, a=a, ch=ch, c4=c4, X=X: e.transpose(
                                pT[:, c4 * 128:(c4 + 1) * 128], X[:, ch, a * 128:(a + 1) * 128], cx.ident[:]),
                                reads=[b_x[k], cx.b_const], writes=[b_pT])
                        if half == 0:
                            sc.op("dve", lambda e, ko=ko: e.tensor_copy(out=otok[ko][:, 0:512], in_=pT[:]),
                                  reads=[b_pT], writes=[b_otok[ko]])
                        else:
                            sc.op("act", lambda e, ko=ko: e.copy(out=otok[ko][:, 512:1024], in_=pT[:]),
                                  reads=[b_pT], writes=[b_otok[ko]])
                    r0 = t * TT + a * 128
                    sc.dma("sp", out_tok[r0:r0 + 128, :], otok[ko][:], b_otok[ko], reads=[b_otok[ko]], writes=[cx.b_out])
        sc.barrier()


class Ring:
    def __init__(self, tiles, name, excl=False):
        self.tiles = tiles
        self.bufs = [Buf(name, excl) for _ in tiles]
        self.i = 0

    def next(self):
        k = self.i % len(self.tiles)
        self.i += 1
        return self.tiles[k], self.bufs[k]


FM_U, FM_Q, FM_QS, FM_KCMP, FM_VCMP, FM_KSLC, FM_KSLCS, FM_KWIN, FM_KWINS, FM_C, FM_X, FM_B = 0, 4, 8, 12, 13, 14, 15, 16, 17, 18, 22, 26
N_FM = 30
TM_OFF = N_FM * 128
TM_W = 280
WP_COLS = TM_OFF + 512 + TM_W


def phase_proj(cx, l, xT_in, dr):
    nc, sc = cx.nc, cx.sc
    TT = 256
    NT = int(os.environ.get("PROJ_NT", S // TT))
    A, B = cx.modA[l][1], cx.modB[l][1]
    with ExitStack() as es:
        wp = _alloc(nc, es, "p_wp", [128, NCH, WP_COLS], BF16)
        xt = Ring([_alloc(nc, es, "p_x%d" % k, [128, NCH, TT], F32) for k in range(2)], "x")
        sq = _alloc(nc, es, "p_sq", [128, NCH, TT], BF16)
        rstd = _alloc(nc, es, "p_rstd", [128, TT], F32)
        lnt = _alloc(nc, es, "p_lnt", [128, TT], F32)
        tmp = Ring([_alloc(nc, es, "p_tmp%d" % k, [128, TT], F32) for k in range(3)], "tmp")
        tmq = Ring([_alloc(nc, es, "p_tmq%d" % k, [128, TT], F32) for k in range(2)], "tmq")
        hT = Ring([_alloc(nc, es, "p_h%d" % k, [128, NCH, TT], BF16) for k in range(2)], "h")
        uT = _alloc(nc, es, "p_u", [128, 4, TT], BF16)
        qr = Ring([_alloc(nc, es, "p_qr%d" % k, [128, 4, TT], BF16) for k in range(2)], "qr")
        qn = Ring([_alloc(nc, es, "p_qn%d" % k, [128, 4, TT], BF16) for k in range(2)], "qn")
        kst = Ring([_alloc(nc, es, "p_kst%d" % k, [128, 2, TT], BF16) for k in range(2)], "kst")
        kcm = Ring([_alloc(nc, es, "p_kcm%d" % k, [128, 2, TT], BF16) for k in range(2)], "kcm")
        yaT = Ring([_alloc(nc, es, "p_ya%d" % k, [128, 4, TT], BF16) for k in range(2)], "ya")
        ycT = Ring([_alloc(nc, es, "p_yc%d" % k, [128, 4, TT], BF16) for k in range(2)], "yc")
        vst = Ring([_alloc(nc, es, "p_vst%d" % k, [128, TT // 128, 256], BF16) for k in range(2)], "vst")
        ngst = Ring([_alloc(nc, es, "p_ng%d" % k, [128, TT // 128, 24], F32) for k in range(2)], "ng")
        rc = Ring([_alloc(nc, es, "p_rc%d" % k, [128, TT], F32) for k in range(2)], "rc")
        rs = Ring([_alloc(nc, es, "p_rs%d" % k, [128, TT], F32) for k in range(2)], "rs")
        hc = _alloc(nc, es, "p_hc", [128, 4, TT + 2], F32)
        xs = Ring([_alloc(nc, es, "p_xs%d" % k, [128, TT], F32) for k in range(2)], "xs")
        acc = Ring([_alloc(nc, es, "p_acc%d" % k, [128, TT], F32) for k in range(2)], "acc")
        vf = [_alloc(nc, es, "p_vf%d" % k, [128, 512], F32) for k in range(TT // 128)]
        vt = Ring([_alloc(nc, es, "p_vt%d" % k, [128, 512], F32) for k in range(2)], "vt")
        vn = Ring([_alloc(nc, es, "p_vn%d" % k, [128, 512], BF16) for k in range(2)], "vn")
        st6 = _alloc(nc, es, "p_st6", [128, TT // 128, 6], F32)
        mv = _alloc(nc, es, "p_mv", [128, TT // 128, 2], F32)
        lv = _alloc(nc, es, "p_lv", [128, TT // 128], F32)
        rv = _alloc(nc, es, "p_rv", [128, TT // 128], F32)
        wsT = _alloc(nc, es, "p_wsT", [128, 4, 128], BF16)
        wsr = _alloc(nc, es, "p_wsr", [128, 4, 128], F32)
        bsr = _alloc(nc, es, "p_bsr", [1, 512], F32)
        bsb = _alloc(nc, es, "p_bsb", [1, 512], BF16)
        onesr = _alloc(nc, es, "p_onesr", [1, 128], BF16)
        onesf = _alloc(nc, es, "p_onesf", [1, 128], F32)
        lrow = _alloc(nc, es, "p_lrow", [1, 1024], F32)
        lng = _alloc(nc, es, "p_lng", [128, 512], F32)
        lnb = _alloc(nc, es, "p_lnb", [128, 512], F32)
        cwr = _alloc(nc, es, "p_cwr", [3, 512], F32)
        cw = _alloc(nc, es, "p_cw", [128, 4, 3], F32)
        b_hc = [Buf("hc") for _ in range(4)]
        b_u = [Buf("u") for _ in range(4)]
        b_vf = [Buf("vf") for _ in range(TT // 128)]
        b_sq, b_rstd, b_st, b_set = Buf("sq"), Buf("rstd"), Buf("st"), Buf("set")
        pS = _palloc(nc, es, "p_pS", [128, 512], F32)
        b_pS = Buf("pS", True)
        pF = Ring([_palloc(nc, es, "p_pF%d" % k, [128, 512], F32) for k in range(4)], "pF", True)
        pV = _palloc(nc, es, "p_pV", [128, 512], F32)
        b_pV = Buf("pV", True)
        pW = _palloc(nc, es, "p_pW", [128, 512], F32)
        b_pW = Buf("pW", True)
        pG = _palloc(nc, es, "p_pG", [128, 512], F32)
        b_pG = Buf("pG", True)
        b_wp = [Buf("wp") for _ in range(NCH)]

        CW = 1544
        b_wpp = [[Buf("wp") for _ in range(WP_COLS // CW)] for _ in range(NCH)]
        for kc in range(NCH):
            for cc in range(WP_COLS // CW):
                sc.dma("pool", wp[:, kc, cc * CW:(cc + 1) * CW],
                       dr.wp[l][kc * 128:(kc + 1) * 128, cc * CW:(cc + 1) * CW], b_wpp[kc][cc], writes=[b_wpp[kc][cc]])
        sc.dma("sp", wsr[:], dr.gm_ws[l].rearrange("g t s -> t g s"), b_set, writes=[b_set])
        sc.dma("sp", bsr[:], dr.gm_bs[l].rearrange("(o g) t -> o (g t)", o=1), b_set, writes=[b_set])
        sc.dma("sp", lrow[:, 0:512], dr.gm_ln_g[l].rearrange("(o n) -> o n", o=1), b_set, writes=[b_set])
        sc.dma("sp", lrow[:, 512:1024], dr.gm_ln_b[l].rearrange("(o n) -> o n", o=1), b_set, writes=[b_set])
        sc.dma("sp", cwr[:], dr.conv_w[l], b_set, writes=[b_set])
        sc.op("dve", lambda e: e.memset(onesr[:], 1.0), writes=[b_set])
        sc.op("dve", lambda e: e.memset(onesf[:], 1.0), writes=[b_set])
        sc.op("dve", lambda e: e.tensor_copy(out=bsb[:], in_=bsr[:]), reads=[b_set], writes=[b_set])
        sc.op("dve", lambda e: e.memset(hc[:, :, 0:2], 0.0), writes=b_hc)
        for g in range(4):
            sc.op("pe", lambda e, g=g: e.transpose(pG[:, 0:128], wsr[:, g, :], cx.ident[:]),
                  reads=[b_set, cx.b_const], writes=[b_pG])
            sc.op("dve", lambda e, g=g: e.tensor_tensor(out=wsT[:, g, :], in0=pG[:, 0:128], in1=cx.triL[:], op=ALU.mult),
                  reads=[b_pG, cx.b_const], writes=[b_set])
        for hh, dst in ((0, lng), (1, lnb)):
            sc.op("pe", lambda e, hh=hh: e.matmul(pG[:, 0:512], onesf[0:1, :], lrow[0:1, hh * 512:(hh + 1) * 512],
                                                  start=True, stop=True), reads=[b_set], writes=[b_pG])
            sc.op("dve", lambda e, dst=dst: e.tensor_copy(out=dst[:], in_=pG[:, 0:512]), reads=[b_pG], writes=[b_set])
        for cc in range(4):
            sc.op("pe", lambda e, cc=cc: e.transpose(pG[:, 0:3], cwr[:, cc * 128:(cc + 1) * 128], cx.ident[0:3, 0:3]),
                  reads=[b_set, cx.b_const], writes=[b_pG])
            sc.op("dve", lambda e, cc=cc: e.tensor_copy(out=cw[:, cc, :], in_=pG[:, 0:3]), reads=[b_pG], writes=[b_set])

        def fm_chunk(idx, H, b_H):
            P, b_P = pF.next()
            for kc in range(NCH):
                sc.op("pe", lambda e, kc=kc, P=P: e.matmul(P[:, 0:TT], wp[:, kc, idx * 128:(idx + 1) * 128], H[:, kc, :],
                                                          start=(kc == 0), stop=(kc == NCH - 1)),
                      reads=b_wpp[kc] + [b_H], writes=[b_P])
            return P, b_P

        def load_x(t):
            X, b_X = xt.next()
            src = xT_in[:, t * TT:(t + 1) * TT].rearrange("(c p) n -> p c n", p=128)
            sc.dma("sp", X[:], src, b_X, writes=[b_X])
            return X, b_X

        def pre(t, X, b_X):
            t0 = t * TT
            for ch in range(NCH):
                sc.op("act", lambda e, ch=ch, X=X: e.activation(out=sq[:, ch, :], in_=X[:, ch, :], func=AF.Square),
                      reads=[b_X], writes=[b_sq])
            emit_rstd(cx, sq, b_sq, pS, b_pS, rstd, b_rstd, lnt, TT)
            H, b_H = hT.next()
            for ch in range(NCH):
                T1, b_T1 = tmp.next()
                sc.op("dve", lambda e, ch=ch, X=X, T1=T1: e.scalar_tensor_tensor(
                    out=T1[:], in0=X[:, ch, :], scalar=A[:, ch:ch + 1], in1=rstd[:], op0=ALU.mult, op1=ALU.mult),
                    reads=[b_X, b_rstd, cx.b_mod], writes=[b_T1])
                sc.op("act", lambda e, ch=ch, T1=T1, H=H: e.activation(
                    out=H[:, ch, :], in_=T1[:], func=AF.Identity, bias=B[:, ch:ch + 1], scale=1.0),
                    reads=[b_T1, cx.b_mod], writes=[b_H])
            sc.dma("sp", dr.hT[:, t0:t0 + TT].rearrange("(c p) n -> p c n", p=128), H[:], b_H, reads=[b_H])
            return H, b_H

        nxt = load_x(0)
        nxtH = pre(0, nxt[0], nxt[1])
        for t in range(NT):
            X, b_X = nxt
            if t + 1 < NT:
                nxt = load_x(t + 1)
            t0 = t * TT
            RC, b_RC = rc.next()
            RS, b_RS = rs.next()
            sc.dma("sp", RC[:], dr.ropeC[:, t0:t0 + TT], b_RC, writes=[b_RC])
            sc.dma("sp", RS[:], dr.ropeS[:, t0:t0 + TT], b_RS, writes=[b_RS])
            H, b_H = nxtH
            for a in range(TT // 128):
                for kc in range(NCH):
                    sc.op("pe", lambda e, kc=kc, a=a, H=H: e.matmul(
                        pV[:, :], H[:, kc, a * 128:(a + 1) * 128], wp[:, kc, TM_OFF:TM_OFF + 512],
                        start=(kc == 0), stop=(kc == NCH - 1)), reads=b_wpp[kc] + [b_H], writes=[b_pV])
                sc.op("act", lambda e, a=a: e.activation(out=vf[a][:], in_=pV[:], func=AF.Gelu_apprx_tanh),
                      reads=[b_pV], writes=[b_vf[a]])
                sc.op("dve", lambda e, a=a: e.bn_stats(out=st6[:, a, :], in_=vf[a][:]), reads=[b_vf[a]], writes=[b_st])
                sc.op("dve", lambda e, a=a: e.bn_aggr(out=mv[:, a, :], in_=st6[:, a, :]), reads=[b_st], writes=[b_st])
            sc.op("act", lambda e: e.activation(out=lv[:], in_=mv[:, :, 1], func=AF.Ln, scale=1.0, bias=cx.eps_col[:]),
                  reads=[b_st, cx.b_const], writes=[b_st])
            sc.op("act", lambda e: e.activation(out=rv[:], in_=lv[:], func=AF.Exp, scale=-0.5), reads=[b_st], writes=[b_st])
            for cc in range(4):
                P, b_P = fm_chunk(FM_U + cc, H, b_H)
                sc.op("act", lambda e, cc=cc, P=P: e.activation(out=uT[:, cc, :], in_=P[:, 0:TT], func=AF.Gelu_apprx_tanh),
                      reads=[b_P], writes=[b_u[cc]])
            YA, b_YA = yaT.next()
            VS, b_VS = vst.next()
            NG, b_NG = ngst.next()
            for a in range(TT // 128):
                V1, b_V1 = vt.next()
                sc.op("dve", lambda e, a=a, V1=V1: e.tensor_scalar(
                    out=V1[:], in0=vf[a][:], scalar1=mv[:, a, 0:1], scalar2=rv[:, a:a + 1], op0=ALU.subtract, op1=ALU.mult),
                    reads=[b_vf[a], b_st], writes=[b_V1])
                sc.op("pool", lambda e, V1=V1: e.tensor_tensor(out=V1[:], in0=V1[:], in1=lng[:], op=ALU.mult),
                      reads=[b_V1, b_set], writes=[b_V1])
                VN, b_VN = vn.next()
                sc.op("pool", lambda e, V1=V1, VN=VN: e.tensor_tensor(out=VN[:], in0=V1[:], in1=lnb[:], op=ALU.add),
                      reads=[b_V1, b_set], writes=[b_VN])
                for g in range(4):
                    sc.op("pe", lambda e, g=g, VN=VN: e.matmul(pG[:, g * 128:(g + 1) * 128], VN[:, g * 128:(g + 1) * 128],
                                                               wsT[:, g, :], start=True, stop=False),
                          reads=[b_VN, b_set], writes=[b_pG])
                    sc.op("pe", lambda e, g=g: e.matmul(pG[:, g * 128:(g + 1) * 128], onesr[0:1, :],
                                                        bsb[0:1, g * 128:(g + 1) * 128], start=False, stop=True),
                          reads=[b_set], writes=[b_pG])
                sc.op("dve", lambda e, a=a, YA=YA: e.tensor_tensor(
                    out=YA[:, :, a * 128:(a + 1) * 128], in0=pG[:, 0:512].rearrange("p (g t) -> p g t", g=4),
                    in1=uT[:, :, a * 128:(a + 1) * 128], op=ALU.mult),
                    reads=[b_pG] + b_u, writes=[b_YA])
                for kc in range(NCH):
                    sc.op("pe", lambda e, kc=kc, a=a, H=H: e.matmul(
                        pW[:, 0:TM_W], H[:, kc, a * 128:(a + 1) * 128], wp[:, kc, TM_OFF + 512:TM_OFF + 512 + TM_W],
                        start=(kc == 0), stop=(kc == NCH - 1)), reads=b_wpp[kc] + [b_H], writes=[b_pW])
                sc.op("act", lambda e, a=a, VS=VS: e.copy(out=VS[:, a, :], in_=pW[:, 0:256]), reads=[b_pW], writes=[b_VS])
                sc.op("act", lambda e, a=a, NG=NG: e.activation(out=NG[:, a, :], in_=pW[:, 256:280], func=AF.Sigmoid),
                      reads=[b_pW], writes=[b_NG])
            sc.dma("sp", dr.yaT[:, t0:t0 + TT].rearrange("(c p) n -> p c n", p=128), YA[:], b_YA, reads=[b_YA])
            sc.dma("sp", dr.v[t0:t0 + TT, :].rearrange("(a p) c -> p a c", p=128), VS[:], b_VS, reads=[b_VS])
            sc.dma("sp", dr.ng[t0:t0 + TT, :].rearrange("(a p) c -> p a c", p=128), NG[:], b_NG, reads=[b_NG])
            QR, b_QR = qr.next()
            QN, b_QN = qn.next()

            def rope(Pq, b_Pq, Ps, b_Ps, dst, b_dst):
                T1, b_T1 = tmq.next()
                T2, b_T2 = tmp.next()
                sc.op("dve", lambda e: e.tensor_tensor(out=T1[:], in0=Ps[:, 0:TT], in1=RS[:], op=ALU.mult),
                      reads=[b_Ps, b_RS], writes=[b_T1])
                sc.op("dve", lambda e: e.tensor_tensor(out=T2[:], in0=Pq[:, 0:TT], in1=RC[:], op=ALU.mult),
                      reads=[b_Pq, b_RC], writes=[b_T2])
                sc.op("pool", lambda e: e.tensor_tensor(out=dst, in0=T1[:], in1=T2[:], op=ALU.add),
                      reads=[b_T1, b_T2], writes=[b_dst])

            for qc in range(4):
                Pq, b_Pq = fm_chunk(FM_Q + qc, H, b_H)
                Ps, b_Ps = fm_chunk(FM_QS + qc, H, b_H)
                sc.op("act", lambda e, qc=qc, Pq=Pq, QN=QN: e.copy(out=QN[:, qc, :], in_=Pq[:, 0:TT]),
                      reads=[b_Pq], writes=[b_QN])
                rope(Pq, b_Pq, Ps, b_Ps, QR[:, qc, :], b_QR)
            sc.dma("sp", dr.qrT[:, t0:t0 + TT].rearrange("(c p) n -> p c n", p=128), QR[:], b_QR, reads=[b_QR])
            sc.dma("sp", dr.qnT[:, t0:t0 + TT].rearrange("(c p) n -> p c n", p=128), QN[:], b_QN, reads=[b_QN])
            KC, b_KC = kcm.next()
            KS, b_KS = kst.next()
            for i2, idx in enumerate((FM_KCMP, FM_VCMP)):
                P, b_P = fm_chunk(idx, H, b_H)
                sc.op("act", lambda e, i2=i2, P=P, KC=KC: e.copy(out=KC[:, i2, :], in_=P[:, 0:TT]), reads=[b_P], writes=[b_KC])
            for i2, (idx, idxs) in enumerate(((FM_KSLC, FM_KSLCS), (FM_KWIN, FM_KWINS))):
                Pq, b_Pq = fm_chunk(idx, H, b_H)
                Ps, b_Ps = fm_chunk(idxs, H, b_H)
                rope(Pq, b_Pq, Ps, b_Ps, KS[:, i2, :], b_KS)
            sc.dma("sp", dr.kcmpT[:, :, t0:t0 + TT].rearrange("i p n -> p i n"), KC[:], b_KC, reads=[b_KC])
            sc.dma("sp", dr.kT[:, :, t0:t0 + TT].rearrange("i p n -> p i n"), KS[:], b_KS, reads=[b_KS])
            H_cur, b_H_cur = H, b_H
            if t + 1 < NT:
                nxtH = pre(t + 1, nxt[0], nxt[1])
            H, b_H = H_cur, b_H_cur
            YC, b_YC = ycT.next()
            for cc in range(4):
                Pc, b_Pc = fm_chunk(FM_C + cc, H, b_H)
                Px, b_Px = fm_chunk(FM_X + cc, H, b_H)
                Pb, b_Pb = fm_chunk(FM_B + cc, H, b_H)
                XS, b_XS = xs.next()
                AC, b_AC = acc.next()
                sc.op("act", lambda e, Px=Px, XS=XS: e.copy(out=XS[:], in_=Px[:, 0:TT]), reads=[b_Px], writes=[b_XS])
                sc.op("dve", lambda e, cc=cc, Pc=Pc, XS=XS: e.tensor_tensor(out=hc[:, cc, 2:2 + TT], in0=Pc[:, 0:TT], in1=XS[:],
                                                                            op=ALU.mult),
                      reads=[b_Pc, b_XS], writes=[b_hc[cc]])
                sc.op("dve", lambda e, cc=cc, AC=AC: e.tensor_scalar(out=AC[:], in0=hc[:, cc, 2:2 + TT], scalar1=cw[:, cc, 2:3],
                                                                     scalar2=None, op0=ALU.mult),
                      reads=[b_hc[cc], b_set], writes=[b_AC])
                sc.op("dve", lambda e, cc=cc, AC=AC: e.scalar_tensor_tensor(
                    out=AC[:], in0=hc[:, cc, 1:1 + TT], scalar=cw[:, cc, 1:2], in1=AC[:], op0=ALU.mult, op1=ALU.add),
                    reads=[b_hc[cc], b_set, b_AC], writes=[b_AC])
                sc.op("dve", lambda e, cc=cc, AC=AC: e.scalar_tensor_tensor(
                    out=AC[:], in0=hc[:, cc, 0:TT], scalar=cw[:, cc, 0:1], in1=AC[:], op0=ALU.mult, op1=ALU.add),
                    reads=[b_hc[cc], b_set, b_AC], writes=[b_AC])
                sc.op("dve", lambda e, cc=cc, AC=AC, Pb=Pb, YC=YC: e.tensor_tensor(out=YC[:, cc, :], in0=Pb[:, 0:TT], in1=AC[:],
                                                                                   op=ALU.mult),
                      reads=[b_Pb, b_AC], writes=[b_YC])
                sc.op("pool", lambda e, cc=cc: e.tensor_copy(out=hc[:, cc, 0:2], in_=hc[:, cc, TT:TT + 2]),
                      reads=[b_hc[cc]], writes=[b_hc[cc]])
            sc.dma("sp", dr.ycT[:, t0:t0 + TT].rearrange("(c p) n -> p c n", p=128), YC[:], b_YC, reads=[b_YC])
        sc.barrier()


def phase_cmp(cx, l, dr):
    nc, sc = cx.nc, cx.sc
    with ExitStack() as es:
        w1s = _alloc(nc, es, "c_w1", [128, 2, 32, 128], BF16)
        w2s = _alloc(nc, es, "c_w2", [128, 2, 64], BF16)
        peT = _alloc(nc, es, "c_pe", [128, 2, 32], BF16)
        kin = _alloc(nc, es, "c_kin", [128, 2, S], BF16)
        hid = _alloc(nc, es, "c_hid", [128, 256], BF16)
        bcol = _alloc(nc, es, "c_bcol", [128, 1], F32)
        kst = _alloc(nc, es, "c_kst", [64, 256], BF16)
        vst = _alloc(nc, es, "c_vst", [128, 2, 64], BF16)
        pc = _palloc(nc, es, "c_pc", [128, 512], F32)
        ph = _palloc(nc, es, "c_ph", [128, 512], F32)
        pk = _palloc(nc, es, "c_pk", [128, 512], F32)
        b_w1, b_w2, b_pe, b_kin, b_hid, b_bcol, b_kst, b_vst = (Buf(n) for n in "w1 w2 pe kin hid bcol kst vst".split())
        b_pc, b_ph, b_pk = Buf("pc", True), Buf("ph", True), Buf("pk", True)
        for kv in range(2):
            for half in range(2):
                sc.dma("pool", w1s[half * 64:(half + 1) * 64, kv, :, :],
                       dr.cmp_w1[l, kv].rearrange("(lp d) h -> d lp h", d=64), b_w1, writes=[b_w1])
                sc.dma("pool", peT[half * 64:(half + 1) * 64, kv, :], dr.cmp_peT[l, kv], b_pe, writes=[b_pe])
            sc.dma("pool", w2s[:, kv, :], dr.cmp_w2[l, kv], b_w2, writes=[b_w2])
        sc.dma("sp", kin[:], dr.kcmpT.rearrange("i p n -> p i n"), b_kin, writes=[b_kin])
        sc.op("dve", lambda e: e.memset(hid[:], 0.0), writes=[b_hid])
        for kv in range(2):
            for g in range(2):
                ps_ = slice(g * 64, (g + 1) * 64)
                for lp in range(32):
                    sc.op("pe", lambda e, lp=lp: e.matmul(pc[:, 0:1], w1s[ps_, kv, lp, :], peT[ps_, kv, lp:lp + 1],
                                                          start=(lp == 0), stop=(lp == 31)),
                          reads=[b_w1, b_pe], writes=[b_pc])
                sc.op("dve", lambda e: e.tensor_copy(out=bcol[:], in_=pc[:, 0:1]), reads=[b_pc], writes=[b_bcol])
                kv3 = kin[:, kv, :].rearrange("p (n s) -> p n s", s=16)
                for lp in range(32):
                    rhs = kv3[ps_, 0:255, lp] if lp < 16 else kv3[ps_, 1:256, lp - 16]
                    sc.op("pe", lambda e, lp=lp, rhs=rhs: e.matmul(ph[:, 0:255], w1s[ps_, kv, lp, :], rhs,
                                                                   start=(lp == 0), stop=(lp == 31)),
                          reads=[b_w1, b_kin], writes=[b_ph])
                sc.op("act", lambda e: e.activation(out=hid[:, 0:255], in_=ph[:, 0:255], func=AF.Gelu_apprx_tanh,
                                                    bias=bcol[:], scale=1.0),
                      reads=[b_ph, b_bcol], writes=[b_hid])
                if kv == 0:
                    sc.op("pe", lambda e: e.matmul(pk[0:64, 0:256], w2s[:, 0, :], hid[:, 0:256], start=True, stop=True),
                          reads=[b_w2, b_hid], writes=[b_pk])
                    sc.op("dve", lambda e: e.tensor_copy(out=kst[:], in_=pk[0:64, 0:256]), reads=[b_pk], writes=[b_kst])
                    sc.dma("sp", dr.kcT[g], kst[:], b_kst, reads=[b_kst])
                else:
                    for nt in range(2):
                        sc.op("pe", lambda e, nt=nt: e.matmul(pk[:, nt * 64:(nt + 1) * 64], hid[:, nt * 128:(nt + 1) * 128],
                                                              w2s[:, 1, :], start=True, stop=True),
                              reads=[b_w2, b_hid], writes=[b_pk])
                    sc.op("dve", lambda e: e.tensor_copy(out=vst[:], in_=pk[:, 0:128].rearrange("p (a d) -> p a d", a=2)),
                          reads=[b_pk], writes=[b_vst])
                    sc.dma("sp", dr.vc[g].rearrange("(a p) d -> p a d", p=128), vst[:], b_vst, reads=[b_vst])
        sc.barrier()


def phase_att(cx, l, dr, after_setup=None):
    nc, sc = cx.nc, cx.sc
    NQ = int(os.environ.get("ATT_NQ", S // 128))
    SCALE = 0.125
    NEG = -30000.0
    with ExitStack() as es:
        KE = _alloc(nc, es, "a_KE", [128, S], BF16)
        KW = _alloc(nc, es, "a_KW", [64, S], BF16)
        VS = _alloc(nc, es, "a_VS", [128, 32, 65], BF16)
        VW = _alloc(nc, es, "a_VW", [128, 32, 65], BF16)
        KC = _alloc(nc, es, "a_KC", [64, 256], BF16)
        VC = _alloc(nc, es, "a_VC", [128, 2, 65], BF16)
        SM = _alloc(nc, es, "a_SM", [128, 2, 64], BF16)
        QM = Ring([_alloc(nc, es, "a_QM%d" % k, [128, 4, 128], BF16) for k in range(2)], "QM")
        QN = Ring([_alloc(nc, es, "a_QN%d" % k, [64, 4, 128], BF16) for k in range(2)], "QN")
        NG = Ring([_alloc(nc, es, "a_NG%d" % k, [128, 12], F32) for k in range(2)], "NG")
        CM = Ring([_alloc(nc, es, "a_CM%d" % k, [128, 2, 128], F32) for k in range(2)], "CM")
        CK = Ring([_alloc(nc, es, "a_CK%d" % k, [128, 2, 64], F32) for k in range(2)], "CK")
        PT = Ring([_alloc(nc, es, "a_PT%d" % k, [128, 4, 128], BF16) for k in range(4)], "PT")
        negm = _alloc(nc, es, "a_negm", [128, 128], BF16)
        ybb = _alloc(nc, es, "a_ybb", [128, 256], BF16)
        ybf = _alloc(nc, es, "a_ybf", [128, 256], F32)
        YT = Ring([_alloc(nc, es, "a_YT%d" % k, [128, 2, 128], BF16) for k in range(2)], "YT")
        sm_ = _alloc(nc, es, "a_small", [128, 256], F32)
        identb = _alloc(nc, es, "a_identb", [128, 128], BF16)
        rsum = sm_[:, 0:12].rearrange("p (h r) -> p h r", r=3)
        cf_ = _alloc(nc, es, "a_coef", [128, 12], F32)
        coef = cf_[:, 0:12].rearrange("p (h r) -> p h r", r=3)
        imp = sm_[:, 32:96]
        score = sm_[:, 96:160]
        sc2 = sm_[:, 160:224]
        m1 = sm_[:, 224:232]
        m2 = sm_[:, 232:240]
        pS = Ring([_palloc(nc, es, "a_pS%d" % k, [128, 512], F32) for k in range(2)], "pS", True)
        pOC = _palloc(nc, es, "a_pOC", [128, 4, 128], F32)
        pU = _palloc(nc, es, "a_pU", [128, 4, 128], F32)
        pOS = _palloc(nc, es, "a_pOS", [128, 4, 128], F32)
        pOW = _palloc(nc, es, "a_pOW", [128, 4, 128], F32)
        pT = _palloc(nc, es, "a_pT", [128, 1024], BF16)
        b_pOC, b_pU, b_pOS, b_pOW, b_pT = (Buf(n, True) for n in "pOC pU pOS pOW pT".split())
        b_KE, b_KW, b_VS, b_VW, b_KC, b_VC, b_SM, b_E = (Buf(n) for n in "KE KW VS VW KC VC SM E".split())
        b_negm, b_ybb, b_ybf, b_sm, b_id, b_cf = (Buf(n) for n in "negm ybb ybf sm idb cf".split())

        for q4 in range(4):
            sc.dma("pool", KE[64:128, q4 * 1024:(q4 + 1) * 1024], dr.Esel[:, q4 * 1024:(q4 + 1) * 1024], b_E, writes=[b_E])
        sc.dma("pool", SM[:], dr.slcmap.rearrange("(a p) j -> p a j", p=128), b_SM, writes=[b_SM])
        sc.op("dve", lambda e: e.tensor_copy(out=identb[:], in_=cx.ident[:]), reads=[cx.b_const], writes=[b_id])
        sc.op("dve", lambda e: e.memset(negm[:], 0.0), writes=[b_negm])
        if after_setup is not None:
            after_setup()

        for g in range(2):
            gs = slice(g * 64, (g + 1) * 64)
            sc.dma("sp", KE[0:64, :], dr.kT[0, gs, :], b_KE, writes=[b_KE])
            sc.dma("sp", KW[:], dr.kT[1, gs, :], b_KW, writes=[b_KW])
            sc.dma("sp", VS[:, :, 0:64], dr.v[:, g * 64:(g + 1) * 64].rearrange("(a p) d -> p a d", p=128), b_VS, writes=[b_VS])
            sc.dma("sp", VW[:, :, 0:64], dr.v[:, 128 + g * 64:128 + (g + 1) * 64].rearrange("(a p) d -> p a d", p=128),
                   b_VW, writes=[b_VW])
            sc.dma("sp", KC[:], dr.kcT[g], b_KC, writes=[b_KC])
            sc.dma("sp", VC[:, :, 0:64], dr.vc[g].rearrange("(a p) d -> p a d", p=128), b_VC, writes=[b_VC])
            sc.op("dve", lambda e: e.memset(VS[:, :, 64:65], 1.0), writes=[b_VS])
            sc.op("dve", lambda e: e.memset(VW[:, :, 64:65], 1.0), writes=[b_VW])
            sc.op("dve", lambda e: e.memset(VC[:, :, 64:65], 1.0), writes=[b_VC])

            def load_q(qt):
                t0 = qt * 128
                Q, b_Q = QM.next()
                Qn, b_Qn = QN.next()
                G_, b_G = NG.next()
                C_, b_C = CM.next()
                K_, b_K = CK.next()
                sc.dma("sp", Q[0:64, :, :], dr.qrT[g * 256:(g + 1) * 256, t0:t0 + 128].rearrange("(h d) n -> d h n", d=64),
                       b_Q, writes=[b_Q])
                sc.dma("sp", Qn[:], dr.qnT[g * 256:(g + 1) * 256, t0:t0 + 128].rearrange("(h d) n -> d h n", d=64),
                       b_Qn, writes=[b_Qn])
                sc.dma("sp", G_[:], dr.ng[t0:t0 + 128, g * 12:(g + 1) * 12], b_G, writes=[b_G])
                sc.dma("sp", C_[:], dr.cmpmask[:, t0:t0 + 128].rearrange("(a p) q -> p a q", p=128), b_C, writes=[b_C])
                sc.dma("sp", K_[:, 0, :], dr.cmask[t0:t0 + 128, :], b_K, writes=[b_K])
                sc.dma("sp", K_[:, 1, :], dr.cbias[t0:t0 + 128, :], b_K, writes=[b_K])
                return (Q, b_Q, Qn, b_Qn, G_, b_G, C_, b_C, K_, b_K)

            def pv(Pt, b_Pt, dst, b_dst, V, b_V, kt, first, ncol=65):
                for h in range(4):
                    sc.op("pe", lambda e, h=h: e.matmul(dst[:, h, 0:ncol], Pt[:, h, :], V[:, kt, 0:ncol],
                                                        start=(first and h == 0), stop=False, skip_group_check=True),
                          reads=[b_Pt, b_V], writes=[b_dst])

            def branch(tiles, smat, mask_of, dst, b_dst, V, b_V, vidx_of, extra=None):
                n = len(tiles)
                pend = [None] * n

                def issue_s(i):
                    P_, b_P = pS.next()
                    smat(tiles[i], P_, b_P)
                    pend[i] = (P_, b_P)

                if n:
                    issue_s(0)
                for i in range(n):
                    if i + 1 < n:
                        issue_s(i + 1)
                    P_, b_P = pend[i]
                    Pt, b_Pt = PT.next()
                    sc.op("act", lambda e: e.activation(out=Pt[:].rearrange("p h n -> p (h n)"), in_=P_[:, :], func=AF.Exp,
                                                        scale=SCALE), reads=[b_P], writes=[b_Pt])
                    m = mask_of(tiles[i])
                    if m is not None:
                        msk, b_m = m
                        sc.op("dve", lambda e: e.tensor_tensor(out=Pt[:], in0=Pt[:], in1=msk.unsqueeze(1).to_broadcast([128, 4, 128]),
                                                               op=ALU.mult), reads=[b_Pt, b_m], writes=[b_Pt])
                    pv(Pt, b_Pt, dst, b_dst, V, b_V, vidx_of(tiles[i]), i == 0)
                    if extra is not None:
                        extra(Pt, b_Pt, tiles[i], i == 0)

            fin = [None]
            nxt = load_q(0)
            for qt in range(NQ):
                (Q, b_Q, Qn, b_Qn, G_, b_G, C_, b_C, K_, b_K) = nxt
                if qt + 1 < NQ:
                    nxt = load_q(qt + 1)
                t0 = qt * 128
                G3 = G_[:].rearrange("p (h r) -> p h r", r=3)
                Qf = Q[:].rearrange("p h n -> p (h n)")
                Qr = Q[0:64, :, :].rearrange("p h n -> p (h n)")
                Qnf = Qn[:].rearrange("p h n -> p (h n)")
                nnt = 1 if (t0 + 127) < (16 * 128 + 31) else 2
                branch(list(range(nnt)),
                       lambda nt, P_, b_P: sc.op("pe", lambda e: e.matmul(P_[:, :], KC[:, nt * 128:(nt + 1) * 128], Qnf,
                                                                          start=True, stop=True),
                                                 reads=[b_KC, b_Qn], writes=[b_P]),
                       lambda nt: (C_[:, nt, :], b_C), pOC, b_pOC, VC, b_VC, lambda nt: nt,
                       extra=lambda Pt, b_Pt, nt, first: pv(Pt, b_Pt, pU, b_pU, SM, b_SM, nt, first, ncol=64))
                if fin[0] is not None:
                    fin[0]()
                    fin[0] = None
                sc.op("dve", lambda e: e.tensor_scalar(out=rsum[:, :, 0], in0=pOC[:, :, 64], scalar1=1e-30, scalar2=None,
                                                       op0=ALU.max), reads=[b_pOC], writes=[b_sm])
                sc.op("dve", lambda e: e.reciprocal(out=rsum[:, :, 0], in_=rsum[:, :, 0]), reads=[b_sm], writes=[b_sm])
                sc.op("dve", lambda e: e.tensor_scalar(out=imp, in0=pU[:, 0, 0:64], scalar1=rsum[:, 0, 0:1], scalar2=None,
                                                       op0=ALU.mult), reads=[b_pU, b_sm], writes=[b_sm])
                for h in range(1, 4):
                    sc.op("dve", lambda e, h=h: e.scalar_tensor_tensor(out=imp, in0=pU[:, h, 0:64], scalar=rsum[:, h, 0:1],
                                                                       in1=imp, op0=ALU.mult, op1=ALU.add),
                          reads=[b_pU, b_sm], writes=[b_sm])
                sc.op("dve", lambda e: e.tensor_tensor(out=score, in0=imp, in1=K_[:, 0, :], op=ALU.mult),
                      reads=[b_sm, b_K], writes=[b_sm])
                sc.op("dve", lambda e: e.tensor_tensor(out=score, in0=score, in1=K_[:, 1, :], op=ALU.add),
                      reads=[b_sm, b_K], writes=[b_sm])
                sc.op("dve", lambda e: e.max(out=m1, in_=score), reads=[b_sm], writes=[b_sm])
                sc.op("dve", lambda e: e.match_replace(out=sc2, in_to_replace=m1, in_values=score, imm_value=-1e9),
                      reads=[b_sm], writes=[b_sm])
                sc.op("dve", lambda e: e.max(out=m2, in_=sc2), reads=[b_sm], writes=[b_sm])
                sc.op("dve", lambda e: e.tensor_scalar(out=negm[:, 64:128], in0=score, scalar1=m2[:, 7:8], scalar2=NEG,
                                                       op0=ALU.is_lt, op1=ALU.mult), reads=[b_sm], writes=[b_negm])
                sc.op("dve", lambda e: e.tensor_tensor(out=coef[:, :, 0], in0=rsum[:, :, 0], in1=G3[:, :, 0], op=ALU.mult),
                      reads=[b_sm, b_G], writes=[b_cf])
                for h in range(4):
                    hs = slice(h * 64, (h + 1) * 64)
                    sc.op("dve", lambda e, h=h, hs=hs: e.tensor_scalar(out=ybf[:, hs], in0=pOC[:, h, 0:64], scalar1=coef[:, h, 0:1],
                                                                       scalar2=None, op0=ALU.mult),
                          reads=[b_pOC, b_cf], writes=[b_ybf])
                k0 = max(0, qt - 4)
                branch(list(range(k0, qt + 1)),
                       lambda kt, P_, b_P: sc.op("pe", lambda e: e.matmul(P_[:, :], KW[:, kt * 128:(kt + 1) * 128], Qr,
                                                                          start=True, stop=True),
                                                 reads=[b_KW, b_Q], writes=[b_P]),
                       lambda kt: ((cx.triL[:], cx.b_const) if kt == qt else ((cx.triU[:], cx.b_const) if kt == qt - 4 else None)),
                       pOW, b_pOW, VW, b_VW, lambda kt: kt)
                sc.op("dve", lambda e: e.reciprocal(out=rsum[:, :, 2], in_=pOW[:, :, 64]), reads=[b_pOW], writes=[b_sm])
                sc.op("dve", lambda e: e.tensor_tensor(out=coef[:, :, 2], in0=rsum[:, :, 2], in1=G3[:, :, 2], op=ALU.mult),
                      reads=[b_sm, b_G], writes=[b_cf])
                for h in range(4):
                    hs = slice(h * 64, (h + 1) * 64)
                    sc.op("dve", lambda e, h=h, hs=hs: e.scalar_tensor_tensor(out=ybf[:, hs], in0=pOW[:, h, 0:64], scalar=coef[:, h, 2:3],
                                                                              in1=ybf[:, hs], op0=ALU.mult, op1=ALU.add),
                          reads=[b_pOW, b_cf, b_ybf], writes=[b_ybf])
                sc.op("pe", lambda e: e.transpose(pT[:, 0:128], negm[:], identb[:]), reads=[b_negm, b_id], writes=[b_pT])
                for h in range(4):
                    if h % 2 == 0:
                        sc.op("act", lambda e, h=h: e.copy(out=Q[64:128, h, :], in_=pT[64:128, 0:128]),
                              reads=[b_pT], writes=[b_Q])
                    else:
                        sc.op("dve", lambda e, h=h: e.tensor_copy(out=Q[64:128, h, :], in_=pT[64:128, 0:128]),
                              reads=[b_pT], writes=[b_Q])
                branch(list(range(qt + 1)),
                       lambda kt, P_, b_P: sc.op("pe", lambda e: e.matmul(P_[:, :], KE[:, kt * 128:(kt + 1) * 128], Qf,
                                                                          start=True, stop=True),
                                                 reads=[b_KE, b_E, b_Q], writes=[b_P]),
                       lambda kt: ((cx.triL[:], cx.b_const) if kt == qt else None),
                       pOS, b_pOS, VS, b_VS, lambda kt: kt)
                sc.op("dve", lambda e: e.reciprocal(out=rsum[:, :, 1], in_=pOS[:, :, 64]), reads=[b_pOS], writes=[b_sm])
                sc.op("dve", lambda e: e.tensor_tensor(out=coef[:, :, 1], in0=rsum[:, :, 1], in1=G3[:, :, 1], op=ALU.mult),
                      reads=[b_sm, b_G], writes=[b_cf])
                for h in range(4):
                    hs = slice(h * 64, (h + 1) * 64)
                    sc.op("dve", lambda e, h=h, hs=hs: e.scalar_tensor_tensor(out=ybb[:, hs], in0=pOS[:, h, 0:64], scalar=coef[:, h, 1:2],
                                                                              in1=ybf[:, hs], op0=ALU.mult, op1=ALU.add),
                          reads=[b_pOS, b_cf, b_ybf], writes=[b_ybb])

                def finalize(t0=t0):
                    Y_, b_Y = YT.next()
                    for c2 in range(2):
                        sc.op("pe", lambda e, c2=c2: e.transpose(pT[:, 256 + c2 * 128:256 + (c2 + 1) * 128],
                                                                 ybb[:, c2 * 128:(c2 + 1) * 128], identb[:]),
                              reads=[b_ybb, b_id], writes=[b_pT])
                    sc.op("act", lambda e: e.copy(out=Y_[:], in_=pT[:, 256:512].rearrange("p (c n) -> p c n", c=2)),
                          reads=[b_pT], writes=[b_Y])
                    sc.dma("sp", dr.ybT[g * 256:(g + 1) * 256, t0:t0 + 128].rearrange("(c p) n -> p c n", p=128), Y_[:], b_Y,
                           reads=[b_Y])
                fin[0] = finalize
            if fin[0] is not None:
                fin[0]()
                fin[0] = None
        sc.barrier()


def merge_weights_alloc(cx, es):
    nc = cx.nc
    w = Ctx()
    w.wg = _alloc(nc, es, "m_wg", [128, NCH, 3 * D], BF16)
    w.wb = _alloc(nc, es, "m_wb", [128, 3, 4, D], BF16)
    w.wo = _alloc(nc, es, "m_wo", [128, NCH, D], BF16)
    w.b_wg = [Buf("wg") for _ in range(NCH)]
    w.b_wb = [Buf("wb") for _ in range(3)]
    w.b_wo = [Buf("wo") for _ in range(NCH)]
    return w


def merge_weights_load(cx, l, dr, w):
    sc = cx.sc
    for kc in range(NCH):
        for cc in range(2):
            sc.dma("pool", w.wg[:, kc, cc * 1536:(cc + 1) * 1536],
                   dr.w_gate[l, kc * 128:(kc + 1) * 128, cc * 1536:(cc + 1) * 1536], w.b_wg[kc], writes=[w.b_wg[kc]])
    for n in range(3):
        for k4 in range(4):
            sc.dma("pool", w.wb[:, n, k4, :], dr.w_branch[l, n, k4 * 128:(k4 + 1) * 128, :], w.b_wb[n], writes=[w.b_wb[n]])
    for kc in range(NCH):
        sc.dma("pool", w.wo[:, kc, :], dr.w_out[l, kc * 128:(kc + 1) * 128, :], w.b_wo[kc], writes=[w.b_wo[kc]])


def phase_merge(cx, l, xT_in, xT_out, dr, w=None):
    nc, sc = cx.nc, cx.sc
    TT = 256
    NT = int(os.environ.get("MERGE_NT", S // TT))
    C = cx.modC[l][1]
    with ExitStack() as es:
        if w is None:
            w = merge_weights_alloc(cx, es)
            merge_weights_load(cx, l, dr, w)
        wg, wb, wo, b_wg, b_wb, b_wo = w.wg, w.wb, w.wo, w.b_wg, w.b_wb, w.b_wo
        xt = Ring([_alloc(nc, es, "m_x%d" % k, [128, NCH, TT], F32) for k in range(2)], "x")
        hT = Ring([_alloc(nc, es, "m_h%d" % k, [128, NCH, TT], BF16) for k in range(2)], "h")
        ys = Ring([_alloc(nc, es, "m_ys%d" % k, [128, 3, 4, TT], BF16) for k in range(2)], "ys")
        mg = _alloc(nc, es, "m_mg", [128, NCH, TT], BF16)
        yT = _alloc(nc, es, "m_y", [128, NCH, TT], F32)
        sq = _alloc(nc, es, "m_sq", [128, NCH, TT], BF16)
        rstd = _alloc(nc, es, "m_rstd", [128, TT], F32)
        lnt = _alloc(nc, es, "m_lnt", [128, TT], F32)
        sg = Ring([_alloc(nc, es, "m_sg%d" % k, [128, TT], F32) for k in range(3)], "sg")
        ac = Ring([_alloc(nc, es, "m_ac%d" % k, [128, TT], F32) for k in range(2)], "ac")
        t2 = Ring([_alloc(nc, es, "m_t2%d" % k, [128, TT], F32) for k in range(2)], "t2")
        pG = Ring([_palloc(nc, es, "m_pG%d" % k, [128, 512], F32) for k in range(2)], "pG", True)
        pB = Ring([_palloc(nc, es, "m_pB%d" % k, [128, 512], F32) for k in range(2)], "pB", True)
        pY = Ring([_palloc(nc, es, "m_pY%d" % k, [128, 512], F32) for k in range(2)], "pY", True)
        pS = _palloc(nc, es, "m_pS", [128, 512], F32)
        b_pS = Buf("pS", True)
        b_mg = [Buf("mg") for _ in range(NCH)]
        b_yc = [Buf("yc") for _ in range(NCH)]
        b_sq, b_rstd = Buf("sq"), Buf("rstd")
        ysrc = (dr.yaT, dr.ybT, dr.ycT)

        def load(t):
            t0 = t * TT
            X, b_X = xt.next()
            H, b_H = hT.next()
            Y, b_Y = ys.next()
            sc.dma("sp", X[:], xT_in[:, t0:t0 + TT].rearrange("(c p) n -> p c n", p=128), b_X, writes=[b_X])
            sc.dma("sp", H[:], dr.hT[:, t0:t0 + TT].rearrange("(c p) n -> p c n", p=128), b_H, writes=[b_H])
            for n in range(3):
                sc.dma("sp", Y[:, n, :, :], ysrc[n][:, t0:t0 + TT].rearrange("(c p) n -> p c n", p=128), b_Y, writes=[b_Y])
            return X, b_X, H, b_H, Y, b_Y

        nxt = load(0)
        for t in range(NT):
            X, b_X, H, b_H, Y, b_Y = nxt
            if t + 1 < NT:
                nxt = load(t + 1)
            t0 = t * TT
            for oc in range(NCH):
                AC, b_AC = ac.next()
                for n in range(3):
                    G_, b_G = pG.next()
                    for kc in range(NCH):
                        sc.op("pe", lambda e, kc=kc: e.matmul(G_[:, 0:TT], wg[:, kc, n * D + oc * 128:n * D + (oc + 1) * 128], H[:, kc, :],
                                                              start=(kc == 0), stop=(kc == NCH - 1)),
                              reads=[b_wg[kc], b_H], writes=[b_G])
                    B_, b_B = pB.next()
                    for k4 in range(4):
                        sc.op("pe", lambda e, k4=k4: e.matmul(B_[:, 0:TT], wb[:, n, k4, oc * 128:(oc + 1) * 128], Y[:, n, k4, :],
                                                              start=(k4 == 0), stop=(k4 == 3)),
                              reads=[b_wb[n], b_Y], writes=[b_B])
                    SG, b_SG = sg.next()
                    sc.op("act", lambda e: e.activation(out=SG[:], in_=G_[:, 0:TT], func=AF.Sigmoid), reads=[b_G], writes=[b_SG])
                    if n == 0:
                        sc.op("dve", lambda e: e.tensor_tensor(out=AC[:], in0=B_[:, 0:TT], in1=SG[:], op=ALU.mult),
                              reads=[b_B, b_SG], writes=[b_AC])
                    else:
                        T2, b_T2 = t2.next()
                        sc.op("dve", lambda e: e.tensor_tensor(out=T2[:], in0=B_[:, 0:TT], in1=SG[:], op=ALU.mult),
                              reads=[b_B, b_SG], writes=[b_T2])
                        if n == 1:
                            sc.op("pool", lambda e: e.tensor_tensor(out=AC[:], in0=AC[:], in1=T2[:], op=ALU.add),
                                  reads=[b_AC, b_T2], writes=[b_AC])
                        else:
                            sc.op("pool", lambda e: e.tensor_tensor(out=mg[:, oc, :], in0=AC[:], in1=T2[:], op=ALU.add),
                                  reads=[b_AC, b_T2], writes=[b_mg[oc]])
            for oc2 in range(NCH):
                Y_, b_Yp = pY.next()
                for oc in range(NCH):
                    sc.op("pe", lambda e, oc=oc: e.matmul(Y_[:, 0:TT], wo[:, oc, oc2 * 128:(oc2 + 1) * 128], mg[:, oc, :],
                                                          start=(oc == 0), stop=(oc == NCH - 1)),
                          reads=[b_wo[oc], b_mg[oc]], writes=[b_Yp])
                sc.op("dve", lambda e: e.tensor_copy(out=yT[:, oc2, :], in_=Y_[:, 0:TT]), reads=[b_Yp], writes=[b_yc[oc2]])
                sc.op("act", lambda e: e.activation(out=sq[:, oc2, :], in_=yT[:, oc2, :], func=AF.Square),
                      reads=[b_yc[oc2]], writes=[b_sq])
            emit_rstd(cx, sq, b_sq, pS, b_pS, rstd, b_rstd, lnt, TT)
            for ch in range(NCH):
                T2, b_T2 = t2.next()
                sc.op("dve", lambda e: e.scalar_tensor_tensor(out=T2[:], in0=yT[:, ch, :], scalar=C[:, ch:ch + 1], in1=rstd[:],
                                                              op0=ALU.mult, op1=ALU.mult),
                      reads=[b_yc[ch], b_rstd, cx.b_mod], writes=[b_T2])
                sc.op("pool", lambda e: e.tensor_tensor(out=X[:, ch, :], in0=X[:, ch, :], in1=T2[:], op=ALU.add),
                      reads=[b_X, b_T2], writes=[b_X])
            sc.dma("sp", xT_out[:, t0:t0 + TT].rearrange("(c p) n -> p c n", p=128), X[:], b_X, reads=[b_X])
        sc.barrier()


def host_constants():
    ct = {}
    ct["ident_in"] = np.eye(128, dtype=np.float32)
    k = np.arange(128)
    ct["triL_in"] = (k[:, None] <= k[None, :]).astype(np.float32)
    ct["triU_in"] = (k[:, None] > k[None, :]).astype(np.float32)
    pos = np.arange(S, dtype=np.float32)
    inv = (1.0 / (np.float32(500000.0) ** (np.arange(0, 16, 2, dtype=np.float32) / np.float32(16)))).astype(np.float32)
    ang = pos[:, None] * inv[None, :]
    cos, sin = np.cos(ang).astype(np.float32), np.sin(ang).astype(np.float32)
    C = np.ones((64, S), np.float32)
    Sn = np.zeros((64, S), np.float32)
    C[0:8] = cos.T
    C[8:16] = cos.T
    Sn[0:8] = -sin.T
    Sn[8:16] = sin.T
    ct["ropeC"] = np.ascontiguousarray(np.concatenate([C, C], 0))
    ct["ropeS"] = np.ascontiguousarray(np.concatenate([Sn, Sn], 0))
    ct["Esel"] = (np.arange(64)[:, None] == (np.arange(S)[None, :] // 64)).astype(np.float32)
    n = np.arange(256)
    cm = ((n[:, None] * 16 + 31) <= np.arange(S)[None, :]) & (n[:, None] < 255)
    ct["cmpmask"] = cm.astype(np.float32)
    ncb = S // 16 - 1
    cs = np.arange(ncb)[:, None] * 16
    ss = np.arange(64)[None, :] * 64
    ov = np.clip(np.minimum(cs + 32, ss + 64) - np.maximum(cs, ss), 0, None)
    sm = np.zeros((256, 64), np.float32)
    sm[:ncb] = ov / 32.0
    ct["slcmap"] = sm
    t = np.arange(S)
    cur = t // 64
    blk = np.arange(64)
    forced = (blk[None, :] == 0) | (blk[None, :] == cur[:, None]) | (blk[None, :] == cur[:, None] - 1)
    causal = blk[None, :] <= cur[:, None]
    ct["cmask"] = (causal & ~forced).astype(np.float32)
    ct["cbias"] = np.where(forced, 1e4, np.where(causal, 0.0, -1.0)).astype(np.float32)
    return ct


def relayout_w_in(w_in):
    perm = np.arange(64)
    perm[0:8] = np.arange(8, 16)
    perm[8:16] = np.arange(0, 8)
    cols = []
    cols += list(range(0, 512))
    cols += list(range(OFF_Q, OFF_Q + 512))
    cols += [OFF_Q + h * 64 + perm[d] for h in range(8) for d in range(64)]
    kv = lambda i: OFF_KV + i * 128
    cols += list(range(kv(0), kv(0) + 128))
    cols += list(range(kv(1), kv(1) + 128))
    cols += list(range(kv(2), kv(2) + 128))
    cols += [kv(2) + g * 64 + perm[d] for g in range(2) for d in range(64)]
    cols += list(range(kv(4), kv(4) + 128))
    cols += [kv(4) + g * 64 + perm[d] for g in range(2) for d in range(64)]
    cols += list(range(OFF_C + 512, OFF_C + 1024))
    cols += list(range(OFF_C + 1024, OFF_C + 1536))
    cols += list(range(OFF_C, OFF_C + 512))
    cols += list(range(512, 1024))
    cols += list(range(kv(3), kv(3) + 128))
    cols += list(range(kv(5), kv(5) + 128))
    cols += list(range(OFF_NG, OFF_NG + 24))
    cols = np.asarray(cols)
    assert cols.size == WP_COLS
    return np.ascontiguousarray(w_in[:, :, cols])


class DR:
    pass


def build_program(upto="all", debug=False):
    nc = bass.Bass("TRN2", target_bir_lowering=False)
    cx = Ctx()
    cx.nc = nc
    es = ExitStack()
    cx.es = es
    sc = Sched(nc, es)
    cx.sc = sc
    dr = DR()

    def din(name, shape):
        return nc.dram_tensor(name, list(shape), F32, kind="ExternalInput").ap()

    kind_i = "ExternalOutput" if debug else "Internal"

    def dscr(name, shape, dt=BF16):
        return nc.dram_tensor(name, list(shape), dt, kind=kind_i).ap()

    x = din("x", [S, D])
    c = din("c", [1, D])
    mod_w = din("mod_w", [L, D, 9 * D])
    mod_b = din("mod_b", [L, 9 * D])
    norm_g = din("norm_g", [L, 6, D])
    ffn_w13 = din("ffn_w13", [L, 2, D, 2 * DFF])
    ffn_w2 = din("ffn_w2", [L, 2, DFF, D])
    dr.wp = din("wp", [L, D, WP_COLS])
    dr.gm_ln_g = din("gm_ln_g", [L, 512])
    dr.gm_ln_b = din("gm_ln_b", [L, 512])
    dr.gm_ws = din("gm_ws", [L, 4, 128, 128])
    dr.gm_bs = din("gm_bs", [L, 4, 128])
    dr.cmp_peT = din("cmp_peT", [L, 2, 64, 32])
    dr.cmp_w1 = din("cmp_w1", [L, 2, 2048, 128])
    dr.cmp_w2 = din("cmp_w2", [L, 2, 128, 64])
    dr.conv_w = din("conv_w", [L, 3, 512])
    dr.w_branch = din("w_branch", [L, 3, 512, D])
    dr.w_gate = din("w_gate", [L, D, 3 * D])
    dr.w_out = din("w_out", [L, D, D])
    ident_d = din("ident_in", [128, 128])
    triL_d = din("triL_in", [128, 128])
    triU_d = din("triU_in", [128, 128])
    dr.ropeC = din("ropeC", [128, S])
    dr.ropeS = din("ropeS", [128, S])
    dr.Esel = din("Esel", [64, S])
    dr.cmpmask = din("cmpmask", [256, S])
    dr.slcmap = din("slcmap", [256, 64])
    dr.cmask = din("cmask", [S, 64])
    dr.cbias = din("cbias", [S, 64])
    out = nc.dram_tensor("out", [S, D], F32, kind="ExternalOutput").ap()
    xT = [nc.dram_tensor("xT%d" % i, [D, S], F32, kind=kind_i).ap() for i in range(2)]
    dr.hT = dscr("hT_d", [D, S])
    dr.yaT = dscr("yaT_d", [512, S])
    dr.ybT = dscr("ybT_d", [512, S])
    dr.ycT = dscr("ycT_d", [512, S])
    dr.qrT = dscr("qrT_d", [512, S])
    dr.qnT = dscr("qnT_d", [512, S])
    dr.kT = dscr("kT_d", [2, 128, S])
    dr.kcmpT = dscr("kcmpT_d", [2, 128, S])
    dr.v = dscr("v_d", [S, 256])
    dr.ng = dscr("ng_d", [S, 24], F32)
    dr.kcT = dscr("kcT_d", [2, 64, 256])
    dr.vc = dscr("vc_d", [2, 256, 64])
    cx.b_x = Buf("x")
    cx.b_out = Buf("out")
    cx.b_xT = [Buf("xT") for _ in range(S // 512)]
    bx = [Buf("xTd") for _ in range(S // 512)]

    cx.ident = _alloc(nc, es, "ident", [128, 128], F32)
    cx.triL = _alloc(nc, es, "triL", [128, 128], F32)
    cx.triU = _alloc(nc, es, "triU", [128, 128], F32)
    cx.ones_bf = _alloc(nc, es, "ones_bf", [128, 128], BF16)
    cx.eps_col = _alloc(nc, es, "eps_col", [128, 1], F32)
    cx.b_const = Buf("const")
    cx.b_mod = Buf("mod")
    cx.modT = [_alloc(nc, es, "modT%d" % l, [128, 72], F32) for l in range(L)]
    cx.modbT = [_alloc(nc, es, "modbT%d" % l, [128, 72], F32) for l in range(L)]
    cx.normgT = [_alloc(nc, es, "normgT%d" % l, [128, 48], F32) for l in range(L)]
    cx.modA = [[_alloc(nc, es, "modA%d_%d" % (l, s_), [128, 8], F32) for s_ in range(3)] for l in range(L)]
    cx.modB = [[_alloc(nc, es, "modB%d_%d" % (l, s_), [128, 8], F32) for s_ in range(3)] for l in range(L)]
    cx.modC = [[_alloc(nc, es, "modC%d_%d" % (l, s_), [128, 8], F32) for s_ in range(3)] for l in range(L)]
    sc.dma("sp", cx.ident[:], ident_d, cx.b_const, writes=[cx.b_const])
    sc.dma("sp", cx.triL[:], triL_d, cx.b_const, writes=[cx.b_const])
    sc.dma("sp", cx.triU[:], triU_d, cx.b_const, writes=[cx.b_const])
    sc.op("dve", lambda e: e.memset(cx.ones_bf[:], 1.0), writes=[cx.b_const])
    sc.op("dve", lambda e: e.memset(cx.eps_col[:], EPS), writes=[cx.b_const])

    stages = upto.split(",")

    def want(nm):
        return upto == "all" or nm in stages

    phase_transpose_in(cx, x, xT[0], side=phase_mod(cx, c, mod_w, mod_b, norm_g))
    if debug:
        dbgm = nc.dram_tensor("dbg_mod", [128, 72 + 24], F32, kind="ExternalOutput").ap()
        bd = Buf("dbg")
        sc.dma("sp", dbgm[:, 0:72], cx.modT[0][:], bd, reads=[cx.b_mod])
        for s_ in range(3):
            sc.dma("sp", dbgm[:, 72 + s_ * 8:80 + s_ * 8], cx.modA[0][s_][:], bd, reads=[cx.b_mod])
    cur = 0
    for l in range(L):
        if want("ffn%d0" % l):
            phase_ffn(cx, l, 0, ffn_w13[l, 0], ffn_w2[l, 0], xT[cur], bx, xT[1 - cur], bx)
            cur = 1 - cur
        if want("proj%d" % l):
            phase_proj(cx, l, xT[cur], dr)
        if want("cmp%d" % l):
            phase_cmp(cx, l, dr)
        if want("att%d" % l) and want("merge%d" % l):
            with ExitStack() as es2:
                mw = merge_weights_alloc(cx, es2)
                phase_att(cx, l, dr, after_setup=lambda: merge_weights_load(cx, l, dr, mw))
                phase_merge(cx, l, xT[cur], xT[1 - cur], dr, w=mw)
            cur = 1 - cur
        else:
            if want("att%d" % l):
                phase_att(cx, l, dr)
            if want("merge%d" % l):
                phase_merge(cx, l, xT[cur], xT[1 - cur], dr)
                cur = 1 - cur
        if want("ffn%d1" % l):
            last = (l == L - 1)
            phase_ffn(cx, l, 1, ffn_w13[l, 1], ffn_w2[l, 1], xT[cur], bx, xT[1 - cur], bx, out_tok=(out if last else None))
            cur = 1 - cur
    sc.finish()
    es.close()
    return nc


def make_in_maps(inputs, cores):
    ct = host_constants()
    wp = relayout_w_in(np.asarray(inputs["w_in"], np.float32))
    peT = np.ascontiguousarray(np.transpose(np.asarray(inputs["cmp_pe"], np.float32), (0, 1, 3, 2)))
    shared = {k: np.ascontiguousarray(np.asarray(inputs[k], np.float32)) for k in
              ("mod_w", "mod_b", "norm_g", "ffn_w13", "ffn_w2", "gm_ln_g", "gm_ln_b", "gm_ws", "gm_bs",
               "cmp_w1", "cmp_w2", "conv_w", "w_branch", "w_gate", "w_out")}
    shared["wp"] = wp
    shared["cmp_peT"] = peT
    shared.update(ct)
    maps = []
    for b in cores:
        m = dict(shared)
        m["x"] = np.ascontiguousarray(np.asarray(inputs["x"][b], np.float32))
        m["c"] = np.ascontiguousarray(np.asarray(inputs["c"][b:b + 1], np.float32))
        maps.append(m)
    return maps


def kernel(**inputs):
    nc = build_program("all", debug=False)
    maps = make_in_maps(inputs, list(range(8)))
    res = run_bass_kernel_spmd(nc, maps, core_ids=list(range(8)))
    return np.stack([np.asarray(r["out"], np.float32) for r in res.results], 0)
```
